# Optimizing a Trainium2 kernel written in Bass

```python
import jax, jax.numpy as jnp
from jax import lax
import numpy as np

D_MODEL = 1024
BATCH = 8
SEQ = 4096
DEPTH = 4

GRID_W = 64
CTX_LEN = 256
D_FF = 2816
N_SUB = 3
N_MOD = 3 * N_SUB
FFN_HALF = 0.5
NORM_EPS = 1e-6
ROPE_THETA = 10000.0
DEEP_ALPHA = (2 * DEPTH) ** 0.25
DEEP_BETA = (8 * DEPTH) ** -0.25

A_HEADS = 8
A_KV = 2
A_HD = 64
WINDOW = 128
BLOCK = 128
B_HEADS = 4
B_DK = 128
B_DV = 128
CONV_K = 5
CHUNK = 64
C_HEADS = 8
C_KV = 2
C_HD = 128
Q_BLOCK = 128

N_EVEN = (DEPTH + 1) // 2
N_ODD = DEPTH // 2
A_Q = A_HEADS * A_HD
A_KVW = A_KV * A_HD
B_QK = B_HEADS * B_DK
B_VW = B_HEADS * B_DV
B_QKV = 2 * B_QK + B_VW
AB_SIZES = (A_Q, A_KVW, A_KVW, B_QKV, B_VW, B_HEADS, B_HEADS, B_HEADS, B_HEADS)
AB_IN = A_Q + 2 * A_KVW + B_QKV + B_VW + 4 * B_HEADS
AB_OUT = A_Q + B_VW
C_IN = C_HEADS * C_HD + 2 * C_KV * C_HD
C_OUT = C_HEADS * C_HD

kernel_name = 'hybrid_dit_window_delta_axial_gqa'


def _split(p, sizes):
    out, start = [], 0
    for s in sizes:
        out.append(p[..., start:start + s])
        start += s
    return out


def layer_norm(x, g, b):
    x32 = x.astype(jnp.float32)
    mu = jnp.mean(x32, axis=-1, keepdims=True)
    var = jnp.mean(jnp.square(x32 - mu), axis=-1, keepdims=True)
    return ((x32 - mu) * lax.rsqrt(var + NORM_EPS) * g + b).astype(x.dtype)


def rms_norm(x, g):
    x32 = x.astype(jnp.float32)
    return (x32 * lax.rsqrt(jnp.mean(jnp.square(x32), axis=-1, keepdims=True) + NORM_EPS) * g).astype(x.dtype)


def l2_normalize(x):
    x32 = x.astype(jnp.float32)
    return (x32 * lax.rsqrt(jnp.sum(jnp.square(x32), axis=-1, keepdims=True) + NORM_EPS)).astype(x.dtype)


def axial_rope(rows, head_dim):
    n_freq = head_dim // 4
    inv = ROPE_THETA ** (-jnp.arange(n_freq, dtype=jnp.float32) / n_freq)
    r, col = jnp.meshgrid(jnp.arange(rows, dtype=jnp.float32), jnp.arange(GRID_W, dtype=jnp.float32), indexing='ij')
    r, col = r.reshape(-1), col.reshape(-1)
    ang = jnp.concatenate([r[:, None] * inv, col[:, None] * inv], axis=-1)
    return jnp.cos(ang), jnp.sin(ang)


def apply_rope(x, cos, sin):
    half = x.shape[-1] // 2
    x1, x2 = x[..., :half], x[..., half:]
    cs = cos[None, :, None, :].astype(x.dtype)
    sn = sin[None, :, None, :].astype(x.dtype)
    return jnp.concatenate([x1 * cs - x2 * sn, x1 * sn + x2 * cs], axis=-1)


def swiglu(h, w_gu, w_down):
    gate, up = jnp.split(h @ w_gu, 2, axis=-1)
    return (jax.nn.silu(gate) * up) @ w_down


def short_conv(x, w):
    pad = CONV_K // 2
    return lax.conv_general_dilated(x, w[:, None, :].astype(x.dtype), window_strides=(1,), padding=[(pad, pad)],
                                    dimension_numbers=('NWC', 'WIO', 'NWC'), feature_group_count=x.shape[-1])


def ctx_attention(q, k, v, n_kv, sink=None):
    bsz, lq, h, d = q.shape
    g = h // n_kv
    qg = q.reshape(bsz, lq, n_kv, g, d)
    s = jnp.einsum('bqhgd,bkhd->bhgqk', qg, k).astype(jnp.float32) * d ** -0.5
    if sink is not None:
        sk = jnp.broadcast_to(sink.reshape(n_kv, g, 1, 1).astype(jnp.float32), (bsz, n_kv, g, lq, 1))
        s = jnp.concatenate([sk, s], axis=-1)
    p = jax.nn.softmax(s, axis=-1)
    if sink is not None:
        p = p[..., 1:]
    o = jnp.einsum('bhgqk,bkhd->bqhgd', p.astype(v.dtype), v)
    return o.reshape(bsz, lq, h * d)


def window_sink_attention(q, k, v, kc, vc, sink):
    bsz, t = q.shape[:2]
    nb = t // BLOCK
    g = A_HEADS // A_KV
    scale = A_HD ** -0.5
    qb = q.reshape(bsz, nb, BLOCK, A_KV, g, A_HD)

    def band(a):
        ap = jnp.pad(a, ((0, 0), (BLOCK, BLOCK), (0, 0), (0, 0)))
        ab = ap.reshape(bsz, nb + 2, BLOCK, A_KV, A_HD)
        return jnp.concatenate([ab[:, :-2], ab[:, 1:-1], ab[:, 2:]], axis=2)

    kb, vb = band(k), band(v)
    s_loc = jnp.einsum('bnqhgd,bnkhd->bnhgqk', qb, kb).astype(jnp.float32) * scale
    s_ctx = jnp.einsum('bnqhgd,bkhd->bnhgqk', qb, kc).astype(jnp.float32) * scale
    qpos = jnp.arange(nb)[:, None] * BLOCK + jnp.arange(BLOCK)[None, :]
    kpos = (jnp.arange(nb)[:, None] - 1) * BLOCK + jnp.arange(3 * BLOCK)[None, :]
    kp = kpos[:, None, :]
    valid = (jnp.abs(qpos[:, :, None] - kp) <= WINDOW) & (kp >= 0) & (kp < t)
    s_loc = jnp.where(valid[None, :, None, None], s_loc, -jnp.inf)
    sk = jnp.broadcast_to(sink.reshape(A_KV, g, 1, 1).astype(jnp.float32), (bsz, nb, A_KV, g, BLOCK, 1))
    p = jax.nn.softmax(jnp.concatenate([sk, s_loc, s_ctx], axis=-1), axis=-1).astype(v.dtype)
    p_loc, p_ctx = p[..., 1:1 + 3 * BLOCK], p[..., 1 + 3 * BLOCK:]
    o = jnp.einsum('bnhgqk,bnkhd->bnqhgd', p_loc, vb) + jnp.einsum('bnhgqk,bkhd->bnqhgd', p_ctx, vc)
    return o.reshape(bsz, t, A_HEADS * A_HD)


def gated_delta_chunked(q, k, v, beta, g, s0):
    out_dtype = v.dtype
    bsz, t, h, _ = q.shape
    dv = v.shape[-1]
    n = t // CHUNK

    def chunks(a):
        a = a.astype(jnp.float32).reshape((bsz, n, CHUNK) + a.shape[2:])
        return jnp.swapaxes(a, 2, 3)

    qc, kc, vc, bc = chunks(q), chunks(k), chunks(v), chunks(beta)
    gc = jnp.cumsum(chunks(g), axis=-1)
    idx = jnp.arange(CHUNK)
    incl = idx[:, None] >= idx[None, :]
    strict = idx[:, None] > idx[None, :]
    decay = jnp.exp(jnp.where(incl, gc[..., :, None] - gc[..., None, :], -jnp.inf))
    kb = kc * bc[..., None]
    lmat = jnp.where(strict, jnp.einsum('bnhid,bnhjd->bnhij', kb, kc) * decay, 0.0)
    rhs = jnp.concatenate([vc * bc[..., None], kb * jnp.exp(gc)[..., None]], axis=-1)
    uw = lax.linalg.triangular_solve(lmat, rhs, left_side=True, lower=True, unit_diagonal=True)
    u, w = uw[..., :dv], uw[..., dv:]
    attn = jnp.einsum('bnhid,bnhjd->bnhij', qc, kc) * decay
    qd = qc * jnp.exp(gc)[..., None]
    g_last = gc[..., -1]
    kt = kc * jnp.exp(g_last[..., None] - gc)[..., None]

    def step(s, xs):
        u_i, w_i, qd_i, at_i, kt_i, gl_i = xs
        v_new = u_i - jnp.einsum('bhcd,bhde->bhce', w_i, s)
        o_i = jnp.einsum('bhcd,bhde->bhce', qd_i, s) + jnp.einsum('bhij,bhje->bhie', at_i, v_new)
        s = s * jnp.exp(gl_i)[..., None, None] + jnp.einsum('bhcd,bhce->bhde', kt_i, v_new)
        return s, o_i

    xs = tuple(jnp.moveaxis(a, 1, 0) for a in (u, w, qd, attn, kt, g_last))
    s_fin, o = lax.scan(step, s0, xs)
    o = jnp.transpose(o, (1, 0, 3, 2, 4)).reshape(bsz, t, h, dv)
    return o.astype(out_dtype), s_fin


def reverse_delta(q, k, v, beta, g, s0):
    f = lambda a: a[:, ::-1]
    o, s = gated_delta_chunked(f(q), f(k), f(v), f(beta), f(g), s0)
    return f(o), s


def mixer_ab(hl, hc, w_in, conv_w, a_log, dt_bias, gnorm, sink, w_out, cos, sin, ctx_out):
    def prep(h, rope):
        bsz, t, _ = h.shape
        aq, ak, av, bqkv, bz, bbf, bbb, baf, bab = _split(h @ w_in, AB_SIZES)
        aq = aq.reshape(bsz, t, A_HEADS, A_HD)
        ak = ak.reshape(bsz, t, A_KV, A_HD)
        av = av.reshape(bsz, t, A_KV, A_HD)
        if rope:
            aq, ak = apply_rope(aq, cos, sin), apply_rope(ak, cos, sin)
        bq, bk, bv = _split(jax.nn.silu(short_conv(bqkv, conv_w)), (B_QK, B_QK, B_VW))
        bq = l2_normalize(bq.reshape(bsz, t, B_HEADS, B_DK)) * B_DK ** -0.5
        bk = l2_normalize(bk.reshape(bsz, t, B_HEADS, B_DK))
        bv = bv.reshape(bsz, t, B_HEADS, B_DV)
        beta = (jax.nn.sigmoid(bbf), jax.nn.sigmoid(bbb))
        gdec = (-jnp.exp(a_log[0]) * jax.nn.softplus(baf.astype(jnp.float32) + dt_bias[0]),
                -jnp.exp(a_log[1]) * jax.nn.softplus(bab.astype(jnp.float32) + dt_bias[1]))
        return (aq, ak, av), (bq, bk, bv, beta, gdec, bz.reshape(bsz, t, B_HEADS, B_DV))

    (aql, akl, avl), (bql, bkl, bvl, betal, gl, zl) = prep(hl, True)
    (aqc, akc, avc), (bqc, bkc, bvc, betac, gcx, zc) = prep(hc, False)
    bsz, t = hl.shape[:2]
    ol_a = window_sink_attention(aql, akl, avl, akc, avc, sink)
    s0 = jnp.zeros((bsz, B_HEADS, B_DK, B_DV), jnp.float32)
    oc_f, sf = gated_delta_chunked(bqc, bkc, bvc, betac[0], gcx[0], s0)
    oc_b, sb = reverse_delta(bqc, bkc, bvc, betac[1], gcx[1], s0)
    ol_f, _ = gated_delta_chunked(bql, bkl, bvl, betal[0], gl[0], sf)
    ol_b, _ = reverse_delta(bql, bkl, bvl, betal[1], gl[1], sb)
    ol_b2 = (rms_norm(ol_f + ol_b, gnorm) * jax.nn.silu(zl)).reshape(bsz, t, B_VW)
    yl = jnp.concatenate([ol_a, ol_b2], axis=-1) @ w_out
    if not ctx_out:
        return yl, None
    lc = hc.shape[1]
    oc_a = ctx_attention(aqc, akc, avc, A_KV, sink)
    oc_b2 = (rms_norm(oc_f + oc_b, gnorm) * jax.nn.silu(zc)).reshape(bsz, lc, B_VW)
    yc = jnp.concatenate([oc_a, oc_b2], axis=-1) @ w_out
    return yl, yc


def full_attention_blocks(q, k, v, kc, vc):
    bsz, t = q.shape[:2]
    nb = t // Q_BLOCK
    g = C_HEADS // C_KV
    k_all = jnp.concatenate([kc, k], axis=1)
    v_all = jnp.concatenate([vc, v], axis=1)
    qb = jnp.moveaxis(q.reshape(bsz, nb, Q_BLOCK, C_KV, g, C_HD), 1, 0)

    def one_block(qblk):
        s = jnp.einsum('bqhgd,bkhd->bhgqk', qblk, k_all).astype(jnp.float32) * C_HD ** -0.5
        p = jax.nn.softmax(s, axis=-1).astype(v_all.dtype)
        return jnp.einsum('bhgqk,bkhd->bqhgd', p, v_all)

    o = lax.map(one_block, qb)
    return jnp.moveaxis(o, 0, 1).reshape(bsz, t, C_HEADS * C_HD)


def mixer_c(hl, hc, w_in, q_norm, k_norm, w_out, cos, sin, ctx_out):
    def prep(h, rope):
        bsz, t, _ = h.shape
        q, k, v = _split(h @ w_in, (C_HEADS * C_HD, C_KV * C_HD, C_KV * C_HD))
        q = rms_norm(q.reshape(bsz, t, C_HEADS, C_HD), q_norm)
        k = rms_norm(k.reshape(bsz, t, C_KV, C_HD), k_norm)
        v = v.reshape(bsz, t, C_KV, C_HD)
        if rope:
            q, k = apply_rope(q, cos, sin), apply_rope(k, cos, sin)
        return q, k, v

    ql, kl, vl = prep(hl, True)
    qc, kc, vc = prep(hc, False)
    yl = full_attention_blocks(ql, kl, vl, kc, vc) @ w_out
    if not ctx_out:
        return yl, None
    return yl, ctx_attention(qc, kc, vc, C_KV) @ w_out


def _mod(m, s):
    return m[:, 3 * s], m[:, 3 * s + 1], m[:, 3 * s + 2]


def ffn_sublayer(x, m, s, w_gu, w_down, g, b):
    shift, scale, gate = _mod(m, s)
    y = swiglu(x * (1 + scale) + shift, w_gu, w_down)
    return layer_norm(DEEP_ALPHA * x + FFN_HALF * gate * y, g, b)


def setup_inputs(seed: int = 0) -> dict:
    key = jax.random.key(seed)
    ks = jax.random.split(key, 24)
    f32 = jnp.float32
    nrm = lambda k, shape, sc: jax.random.normal(k, shape, f32) * sc
    dt = jnp.exp(jax.random.uniform(ks[13], (N_EVEN, 2, B_HEADS), f32, np.log(1e-3), np.log(1e-1)))
    return {
        'x': nrm(ks[0], (BATCH, SEQ, D_MODEL), 1.0),
        'c': nrm(ks[1], (BATCH, D_MODEL), 1.0),
        'ctx': nrm(ks[2], (BATCH, CTX_LEN, D_MODEL), 1.0),
        'c_ctx': nrm(ks[3], (D_MODEL,), 1.0),
        'ada_w': nrm(ks[4], (DEPTH, D_MODEL, N_MOD * D_MODEL), 0.5 * D_MODEL ** -0.5),
        'ada_b': nrm(ks[5], (DEPTH, N_MOD * D_MODEL), 0.02),
        'ln_g': 1.0 + nrm(ks[6], (DEPTH, N_SUB, D_MODEL), 0.02),
        'ln_b': nrm(ks[7], (DEPTH, N_SUB, D_MODEL), 0.02),
        'ffn_w_gu': nrm(ks[8], (DEPTH, 2, D_MODEL, 2 * D_FF), D_MODEL ** -0.5),
        'ffn_w_down': nrm(ks[9], (DEPTH, 2, D_FF, D_MODEL), DEEP_BETA * D_FF ** -0.5),
        'ab_w_in': nrm(ks[10], (N_EVEN, D_MODEL, AB_IN), D_MODEL ** -0.5),
        'ab_conv_w': nrm(ks[11], (N_EVEN, CONV_K, B_QKV), CONV_K ** -0.5),
        'ab_a_log': jnp.log(jax.random.uniform(ks[12], (N_EVEN, 2, B_HEADS), f32, 1.0, 16.0)),
        'ab_dt_bias': dt + jnp.log(-jnp.expm1(-dt)),
        'ab_gnorm': 1.0 + nrm(ks[14], (N_EVEN, B_DV), 0.02),
        'ab_sink': nrm(ks[15], (N_EVEN, A_HEADS), 0.5),
        'ab_w_out': nrm(ks[16], (N_EVEN, AB_OUT, D_MODEL), DEEP_BETA * AB_OUT ** -0.5),
        'c_w_in': nrm(ks[17], (N_ODD, D_MODEL, C_IN), D_MODEL ** -0.5),
        'c_q_norm': 1.0 + nrm(ks[18], (N_ODD, C_HD), 0.02),
        'c_k_norm': 1.0 + nrm(ks[19], (N_ODD, C_HD), 0.02),
        'c_w_out': nrm(ks[20], (N_ODD, C_OUT, D_MODEL), DEEP_BETA * C_OUT ** -0.5),
    }


def reference(x, c, ctx, c_ctx, ada_w, ada_b, ln_g, ln_b, ffn_w_gu, ffn_w_down, ab_w_in, ab_conv_w, ab_a_log,
              ab_dt_bias, ab_gnorm, ab_sink, ab_w_out, c_w_in, c_q_norm, c_k_norm, c_w_out):
    bsz, t, _ = x.shape
    rows = t // GRID_W
    cos_a, sin_a = axial_rope(rows, A_HD)
    cos_c, sin_c = axial_rope(rows, C_HD)
    xl, xc = x, ctx
    for l in range(DEPTH):
        ctx_out = l < DEPTH - 1
        m_l = (jax.nn.silu(c) @ ada_w[l] + ada_b[l]).reshape(bsz, N_MOD, 1, D_MODEL)
        m_c = (jax.nn.silu(c_ctx) @ ada_w[l] + ada_b[l]).reshape(1, N_MOD, 1, D_MODEL)
        xl = ffn_sublayer(xl, m_l, 0, ffn_w_gu[l, 0], ffn_w_down[l, 0], ln_g[l, 0], ln_b[l, 0])
        xc = ffn_sublayer(xc, m_c, 0, ffn_w_gu[l, 0], ffn_w_down[l, 0], ln_g[l, 0], ln_b[l, 0])
        sh_l, sc_l, gt_l = _mod(m_l, 1)
        sh_c, sc_c, gt_c = _mod(m_c, 1)
        hl = xl * (1 + sc_l) + sh_l
        hc = xc * (1 + sc_c) + sh_c
        i = l // 2
        if l % 2 == 0:
            yl, yc = mixer_ab(hl, hc, ab_w_in[i], ab_conv_w[i], ab_a_log[i], ab_dt_bias[i], ab_gnorm[i], ab_sink[i],
                              ab_w_out[i], cos_a, sin_a, ctx_out)
        else:
            yl, yc = mixer_c(hl, hc, c_w_in[i], c_q_norm[i], c_k_norm[i], c_w_out[i], cos_c, sin_c, ctx_out)
        xl = layer_norm(DEEP_ALPHA * xl + gt_l * yl, ln_g[l, 1], ln_b[l, 1])
        xl = ffn_sublayer(xl, m_l, 2, ffn_w_gu[l, 1], ffn_w_down[l, 1], ln_g[l, 2], ln_b[l, 2])
        if ctx_out:
            xc = layer_norm(DEEP_ALPHA * xc + gt_c * yc, ln_g[l, 1], ln_b[l, 1])
            xc = ffn_sublayer(xc, m_c, 2, ffn_w_gu[l, 1], ffn_w_down[l, 1], ln_g[l, 2], ln_b[l, 2])
    return xl
```

```python
from concourse.bass_utils import run_bass_kernel_spmd
from contextlib import ExitStack
import numpy as np
import concourse.bass as bass
import concourse.mybir as mybir

F32 = mybir.dt.float32
BF16 = mybir.dt.bfloat16
AF = mybir.ActivationFunctionType
ALU = mybir.AluOpType
AX = mybir.AxisListType

ENGS = ("pe", "act", "dve", "pool", "sp")


class Sched:
    NDMA = 6

    def __init__(self, nc):
        self.nc = nc
        self.es = ExitStack()
        self.eng = {"pe": nc.tensor, "act": nc.scalar, "dve": nc.vector,
                    "pool": nc.gpsimd, "sp": nc.sync}
        self.sem = {}
        self.cnt = {}
        for e in ENGS:
            self.sem[e] = self.es.enter_context(nc.semaphore("c_" + e))
            self.cnt[e] = 0
        self.dq = {}
        for q in ("sp", "pool", "act"):
            sems = [self.es.enter_context(nc.semaphore(f"d_{q}{j}")) for j in range(self.NDMA)]
            self.dq[q] = {"sems": sems, "vals": [0] * self.NDMA, "next": 0}
        self.semh = dict(self.sem)
        for q, d in self.dq.items():
            for j, s in enumerate(d["sems"]):
                self.semh[("d", q, j)] = s
        self.seen = {e: {} for e in ENGS}
        self.res = {}
        self.nwait = 0
        self.nins = 0

    def _deps(self, e, reads, writes):
        need = {}

        def add(tok):
            if tok is None:
                return
            s, v = tok
            if e == "pe" and s == "pe":
                return
            if need.get(s, 0) < v:
                need[s] = v

        for r in reads:
            st = self.res.get(r)
            if st is not None:
                add(st["w"])
        for w in writes:
            st = self.res.get(w)
            if st is not None:
                add(st["w"])
                for s, v in st["r"].items():
                    add((s, v))
        seen = self.seen[e]
        for s, v in need.items():
            if seen.get(s, 0) < v:
                self.eng[e].wait_ge(self.semh[s], v)
                seen[s] = v
                self.nwait += 1

    def _mark(self, tok, reads, writes):
        s, v = tok
        for r in reads:
            st = self.res.setdefault(r, {"w": None, "r": {}})
            if st["r"].get(s, 0) < v:
                st["r"][s] = v
        for w in writes:
            self.res[w] = {"w": tok, "r": {}}

    def op(self, e, fn, reads=(), writes=()):
        self._deps(e, reads, writes)
        ins = fn(self.eng[e])
        self.cnt[e] += 1
        ins.then_inc(self.sem[e], 1)
        self.nins += 1
        self._mark((e, self.cnt[e]), reads, writes)
        return ins

    def dma(self, q, out, in_, reads=(), writes=(), **kw):
        d = self.dq[q]
        j = d["next"]
        d["next"] = (j + 1) % self.NDMA
        key = ("d", q, j)
        seen = self.seen[q]
        if seen.get(key, 0) < d["vals"][j]:
            self.eng[q].wait_ge(d["sems"][j], d["vals"][j])
            seen[key] = d["vals"][j]
            self.nwait += 1
        self._deps(q, reads, writes)
        ins = self.eng[q].dma_start(out=out, in_=in_, **kw)
        d["vals"][j] += 16
        ins.then_inc(d["sems"][j], 16)
        self.nins += 1
        self._mark((key, d["vals"][j]), reads, writes)
        return ins

    def barrier(self):
        for e in ENGS:
            seen = self.seen[e]
            for s in ENGS:
                if self.cnt[s] == 0:
                    continue
                if seen.get(s, 0) < self.cnt[s]:
                    self.eng[e].wait_ge(self.sem[s], self.cnt[s])
                    seen[s] = self.cnt[s]
            for q, d in self.dq.items():
                for j in range(self.NDMA):
                    key = ("d", q, j)
                    if seen.get(key, 0) < d["vals"][j]:
                        self.eng[e].wait_ge(d["sems"][j], d["vals"][j])
                        seen[key] = d["vals"][j]
        self.res = {}

    def close(self):
        self.es.close()


import os as _os
AB_STOP = int(_os.environ.get('AB_STOP', '0'))
AB_CUT = int(_os.environ.get('AB_CUT', '9'))
AB_SUB = int(_os.environ.get('AB_SUB', '9'))


def make_mixer_ab(nc, S, io, mk, idf, idb, epsb, helpers):
    from contextlib import ExitStack
    op, dma = S.op, S.dma
    Work, load_x, modulate, transpose8, load_mods, out_proj = helpers
    NCH = NTOK // 64

    def phase(l, src, dst, q_tiles):
        i = l // 2
        win = io["ab_w_in"][i].rearrange("(kc p) n -> p kc n", p=128)
        eso = ExitStack()
        sbo, pso = mk(eso)
        AKT = sbo("AKT", [64, 2, NTOK], BF16)
        AV = sbo("AV", [128, NT, 2, 128], BF16)
        op("pool", lambda e: e.memset(AV[:, :, :, 64:128], 1.0), [], ["AVones"])
        esm = ExitStack()
        sbm, psm = mk(esm)
        hTall = sbm("hTall", [128, 8, NTOK], BF16)
        es = ExitStack()
        sb, ps = mk(es)
        w1 = sb("w1", [128, 8, 1296], BF16)
        dma("pool", w1[:, :, 0:768], win[:, :, 0:768], writes=["w1a"])
        dma("pool", w1[:, :, 768:1296], win[:, :, 2304:2832], writes=["w1b"])
        mods = load_mods(sb, l, 1, ("sh", "sc"))
        W = Work(sb, ps, nxt=2, npy=0, ntmp=1, nhb=2, nz=0)
        dtb = sb("dtb", [128, 8], F32)
        nea = sb("nea", [128, 8], F32)
        one1 = sb("one1", [128, 1], F32)
        op("dve", lambda e: e.memset(one1[:], 1.0), [], ["one1"])
        dma("sp", dtb[:], io["ab_dt_bias"][i:i + 1].rearrange("o a b -> o (a b)").broadcast_to([128, 8]), writes=["dtb"])
        dma("sp", nea[:], io["ab_a_log"][i:i + 1].rearrange("o a b -> o (a b)").broadcast_to([128, 8]), writes=["nea"])
        op("act", lambda e: e.activation(out=nea[:], in_=nea[:], func=AF.Exp), ["nea"], ["nea"])
        op("dve", lambda e: e.tensor_scalar_mul(out=nea[:], in0=nea[:], scalar1=-1.0), ["nea"], ["nea"])
        qa = RB([sb(f"qa{k}", [128, 640], F32) for k in range(2)], "qa")
        qra = RB([sb(f"qra{k}", [128, 10, 64], BF16) for k in range(2)], "qra")
        csa = RB([sb(f"csa{k}", [128, 2, 32], F32) for k in range(2)], "csa")
        ra = sb("raA", [128, 10, 32], F32)
        rb_ = sb("rbA", [128, 10, 32], F32)
        aqt = RB([sb(f"aqt{k}", [64, 8, 128], BF16) for k in range(2)], "aqt")
        zs = RB([sb(f"zs{k}", [128, 512], F32) for k in range(2)], "zs")
        gbt = RB([sb(f"gbt{k}", [128, 16], F32) for k in range(2)], "gbt")
        gtmp = sb("gtmp", [128, 8], F32)
        pa0 = ps("pa0", [128, 512], F32)
        pa1f = ps("pa1", [128, 512], F32)
        pa1 = pa1f[:, 0:256]
        pz = ps("pz", [128, 512], F32)
        pgtf = ps("pgt", [128, 512], F32)
        pgt = pgtf[:, 0:16]
        ptrA = ps("ptrA", [64, 16, 128], BF16)
        for t in range(NT):
            r = 1 if t < 2 else 0
            xt, xk = load_x(W, src, t)
            hb, hbk = modulate(W, xt, xk, mods, r)
            transpose8(W, hb, hbk, hTall[:, :, t * 128:(t + 1) * 128], ("hT", t))
            h = hTall[:, :, t * 128:(t + 1) * 128]
            for (pp, pk, c0, c1, wk) in ((pa0, "pa0", 0, 512, "w1a"), (pa1, "pa1", 512, 768, "w1a"), (pz, "pz", 768, 1280, "w1b"), (pgt, "pgt", 1280, 1296, "w1b")):
                for kc in range(8):
                    op("pe", lambda e: e.matmul(pp[:], lhsT=h[:, kc, :], rhs=w1[:, kc, c0:c1], start=(kc == 0), stop=(kc == 7)), [("hT", t), wk], [pk])
            if AB_CUT < 2:
                continue
            q, qk = qa.next()
            op("act", lambda e: e.copy(out=q[:, 0:512], in_=pa0[:]), ["pa0"], [qk])
            op("act", lambda e: e.copy(out=q[:, 512:640], in_=pa1[:, 0:128]), ["pa1"], [qk])
            op("act", lambda e: e.copy(out=AV[:, t, :, 0:64], in_=pa1[:, 128:256].rearrange("p (g d) -> p g d", g=2)), ["pa1"], [("AV", t)])
            if AB_SUB < 2:
                continue
            qo, qok = qra.next()
            q3 = q[:].rearrange("p (h d) -> p h d", d=64)
            if r == 1:
                op("dve", lambda e: e.tensor_copy(out=qo[:], in_=q3), [qk], [qok])
            else:
                c, ck = csa.next()
                p0 = (t - 2) * 128
                dma("sp", c[:, 0, :], io["cosA"][p0:p0 + 128, :], writes=[ck])
                dma("sp", c[:, 1, :], io["sinA"][p0:p0 + 128, :], writes=[ck])
                q4 = q[:].rearrange("p (h two d) -> p h two d", two=2, d=32)
                o4 = qo[:].rearrange("p h (two d) -> p h two d", two=2)
                cosb = c[:, 0, :].unsqueeze(1).to_broadcast([128, 10, 32])
                sinb = c[:, 1, :].unsqueeze(1).to_broadcast([128, 10, 32])
                x1, x2 = q4[:, :, 0, :], q4[:, :, 1, :]
                op("dve", lambda e: e.tensor_tensor(out=ra[:], in0=x1, in1=cosb, op=ALU.mult), [qk, ck], ["raA"])
                op("dve", lambda e: e.tensor_tensor(out=rb_[:], in0=x2, in1=sinb, op=ALU.mult), [qk, ck], ["rbA"])
                op("dve", lambda e: e.tensor_tensor(out=o4[:, :, 0, :], in0=ra[:], in1=rb_[:], op=ALU.subtract), ["raA", "rbA"], [qok])
                op("dve", lambda e: e.tensor_tensor(out=ra[:], in0=x1, in1=sinb, op=ALU.mult), [qk, ck], ["raA"])
                op("dve", lambda e: e.tensor_tensor(out=rb_[:], in0=x2, in1=cosb, op=ALU.mult), [qk, ck], ["rbA"])
                op("dve", lambda e: e.tensor_tensor(out=o4[:, :, 1, :], in0=ra[:], in1=rb_[:], op=ALU.add), ["raA", "rbA"], [qok])
            if AB_SUB < 3:
                continue
            for hh in range(10):
                op("pe", lambda e: e.transpose(out=ptrA[:, hh, :], in_=qo[:, hh, :], identity=idb[:]), [qok, "idb"], ["ptrA"])
            a, ak = aqt.next()
            op("act", lambda e: e.copy(out=a[:], in_=ptrA[:, 0:8, :]), ["ptrA"], [ak])
            dma("sp", io["AQT"][t], a[:], reads=[ak], writes=[("AQT", t)])
            op("dve", lambda e: e.tensor_copy(out=AKT[:, :, t * 128:(t + 1) * 128], in_=ptrA[:, 8:10, :]), ["ptrA"], [("AKT", t)])
            if AB_CUT < 3:
                continue
            z, zk = zs.next()
            op("act", lambda e: e.activation(out=z[:], in_=pz[:], func=AF.Silu), ["pz"], [zk])
            dma("sp", io["BZ"][t * 128:(t + 1) * 128, :], z[:], reads=[zk], writes=[("BZ", t)])
            if AB_CUT < 4:
                continue
            gb, gbk = gbt.next()
            op("act", lambda e: e.activation(out=gb[:, 8:16], in_=pgt[:, 0:8], func=AF.Sigmoid), ["pgt"], [gbk])
            op("dve", lambda e: e.tensor_tensor(out=gtmp[:], in0=pgt[:, 8:16], in1=dtb[:], op=ALU.add), ["pgt", "dtb"], ["gtmp"])
            op("act", lambda e: e.activation(out=gtmp[:], in_=gtmp[:], func=AF.Exp), ["gtmp"], ["gtmp"])
            op("act", lambda e: e.activation(out=gtmp[:], in_=gtmp[:], func=AF.Ln, bias=one1[:, 0:1], scale=1.0), ["gtmp", "one1"], ["gtmp"])
            op("dve", lambda e: e.tensor_tensor(out=gb[:, 0:8], in0=gtmp[:], in1=nea[:], op=ALU.mult), ["gtmp", "nea"], [gbk])
            dma("sp", io["GB"][t * 128:(t + 1) * 128, :], gb[:], reads=[gbk], writes=[("GB", t)])
        S.barrier()
        es.close()
        if AB_STOP == 1:
            esm.close(); eso.close(); return
        es = ExitStack()
        sb, ps = mk(es)
        w2 = sb("w2", [128, 8, 1536], BF16)
        dma("pool", w2[:], win[:, :, 768:2304], writes=["w2"])
        cw = sb("cw", [128, 12, 5], F32)
        cwv = io["ab_conv_w"][i].rearrange("k (cc p) -> p cc k", p=128)
        for cc in range(12):
            dma("sp", cw[:, cc, :], cwv[:, cc, :], writes=["cw"], allow_slow_non_contiguous=True)
        onesb = sb("onesb", [128, 128], BF16)
        op("dve", lambda e: e.memset(onesb[:], 1.0), [], ["onesb"])
        RAWW = 4360
        raw = sb("raw", [128, RAWW], F32)
        acc = sb("acc", [128, RAWW], F32)
        sqb = sb("sqb", [128, RAWW], BF16)
        op("pool", lambda e: e.memset(raw[:], 0.0), [], ["raw"])
        fm = RB([sb(f"fm{k}", [128, 512], F32) for k in range(2)], "fm")
        rn = RB([sb(f"rn{k}", [128, 512], F32) for k in range(2)], "rn")
        tk = RB([sb(f"tk{k}", [128, 4, 128], F32) for k in range(2)], "tk")
        pc = RB([ps(f"pc{k}", [128, 512], F32) for k in range(2)], "pc")
        pn = RB([ps(f"pn{k}", [128, 512], F32) for k in range(2)], "pn")
        ptk = RB([ps(f"ptk{k}", [128, 4, 128], F32) for k in range(2)], "ptk")
        blocks = [(0, 256, 2)] + [(256 + b * 512, 512, 256 + b * 512 + 6) for b in range(8)]
        for cc in range(12):
            kind, hd = ("q", "k", "v")[cc // 4], cc % 4
            for (tok0, n, rc) in blocks:
                p, pk = pc.next()
                for kc in range(8):
                    op("pe", lambda e: e.matmul(p[:, :n], lhsT=w2[:, kc, cc * 128:(cc + 1) * 128], rhs=hTall[:, kc, tok0:tok0 + n], start=(kc == 0), stop=(kc == 7)),
                       ["w2"], [pk])
                op("act", lambda e: e.copy(out=raw[:, rc:rc + n], in_=p[:, :n]), [pk], ["raw"])
            lo, hi = 2, RAWW - 2
            op("dve", lambda e: e.tensor_scalar_mul(out=acc[:, lo:hi], in0=raw[:, lo - 2:hi - 2], scalar1=cw[:, cc, 0:1]), ["raw", "cw"], ["acc"])
            for k in range(1, 5):
                op("dve", lambda e: e.scalar_tensor_tensor(out=acc[:, lo:hi], in0=raw[:, lo + k - 2:hi + k - 2], scalar=cw[:, cc, k:k + 1], in1=acc[:, lo:hi],
                                                            op0=ALU.mult, op1=ALU.add), ["raw", "cw", "acc"], ["acc"])
            op("act", lambda e: e.activation(out=acc[:, lo:hi], in_=acc[:, lo:hi], func=AF.Silu), ["acc"], ["acc"])
            if kind != "v":
                op("act", lambda e: e.activation(out=sqb[:, lo:hi], in_=acc[:, lo:hi], func=AF.Square), ["acc"], ["sqb"])
            for (tok0, n, rc) in blocks:
                if kind != "v":
                    p, pk = pn.next()
                    op("pe", lambda e: e.matmul(p[:, :n], lhsT=onesb[:], rhs=sqb[:, rc:rc + n], start=True, stop=True), ["onesb", "sqb"], [pk])
                    r_, rk = rn.next()
                    op("act", lambda e: e.activation(out=r_[:, :n], in_=p[:, :n], func=AF.Sqrt, bias=epsb[:, 0:1], scale=1.0), [pk, "epsb"], [rk])
                    op("dve", lambda e: e.reciprocal(out=r_[:, :n], in_=r_[:, :n]), [rk], [rk])
                    f, fk = fm.next()
                    sc_ = float(128 ** -0.5) if kind == "q" else 1.0
                    op("dve", lambda e: e.scalar_tensor_tensor(out=f[:, :n], in0=acc[:, rc:rc + n], scalar=sc_, in1=r_[:, :n], op0=ALU.mult, op1=ALU.mult),
                       ["acc", rk], [fk])
                    dst_fm = io["BQT"] if kind == "q" else io["BKT"]
                    dma("sp", dst_fm[hd, :, tok0:tok0 + n], f[:, :n], reads=[fk], writes=[("BFM", cc)])
                    srcT, srck = f, fk
                    off = 0
                else:
                    srcT, srck = acc, "acc"
                    off = rc
                if kind != "q":
                    nt_ = n // 128
                    pt_, ptk_ = ptk.next()
                    for j in range(nt_):
                        op("pe", lambda e: e.transpose(out=pt_[:, j, :], in_=srcT[:, off + j * 128:off + (j + 1) * 128], identity=idf[:]), [srck, "idf"], [ptk_])
                    tt, ttk = tk.next()
                    op("dve" if kind == "k" else "act", (lambda e: e.tensor_copy(out=tt[:, :nt_, :], in_=pt_[:, :nt_, :])) if kind == "k" else
                       (lambda e: e.copy(out=tt[:, :nt_, :], in_=pt_[:, :nt_, :])), [ptk_], [ttk])
                    kvi = 0 if kind == "k" else 1
                    dma("sp", io["BKV"][tok0:tok0 + n, kvi, hd * 128:(hd + 1) * 128].rearrange("(j p) d -> p j d", p=128), tt[:, :nt_, :],
                        reads=[ttk], writes=[("BKVw", cc)])
        S.barrier()
        es.close()
        esm.close()
        if AB_STOP == 2:
            eso.close(); return
        es = ExitStack()
        sb, ps = mk(es)
        mk32 = sb("mk32", [128, 2, 128], F32)
        mkb = sb("mkb", [128, 2, 128], BF16)
        dma("sp", mk32[:], io["cA"][:, :, :], writes=["mk32"])
        op("dve", lambda e: e.tensor_copy(out=mkb[:], in_=mk32[:]), ["mk32"], ["mkb"])
        mk4 = sb("mk4", [128, 2, 4, 128], BF16)
        for mi_ in range(2):
            for hh_ in range(4):
                op("dve", lambda e: e.tensor_copy(out=mk4[:, mi_, hh_, :], in_=mk32[:, mi_, :]), ["mk32"], ["mk4"])
        esink = sb("esink", [128, 8], F32)
        dma("sp", esink[:], io["ab_sink"][i:i + 1, :].broadcast_to([128, 8]), writes=["esink"])
        op("act", lambda e: e.activation(out=esink[:], in_=esink[:], func=AF.Exp), ["esink"], ["esink"])
        aq = RB([sb(f"aq{k}", [64, 8, 128], BF16) for k in range(2)], "aq")
        PT = RB([sb(f"PTa{k}", [128, 4, 128], BF16) for k in range(3)], "PTa")
        oa = RB([sb(f"oa{k}", [128, 8, 64], BF16) for k in range(2)], "oa")
        den = sb("den", [128, 4], F32)
        pS = RB([ps(f"pSa{k}", [128, 512], F32) for k in range(2)], "pSa")
        pO = [ps(f"pOa{k}", [128, 4, 128], F32) for k in range(2)]
        for t in q_tiles:
            a, ak = aq.next()
            dma("sp", a[:], io["AQT"][t], reads=[("AQT", t)], writes=[ak])
            if t < 2:
                kl = [(0, None), (1, None)]
            else:
                kl = []
                if t - 1 >= 2:
                    kl.append((t - 1, 0))
                kl.append((t, None))
                if t + 1 < NT:
                    kl.append((t + 1, 1))
                kl += [(0, None), (1, None)]
            o, ok = oa.next()
            for g in range(2):
                for ki, (kt, mi) in enumerate(kl):
                    p, pk = pS.next()
                    op("pe", lambda e: e.matmul(p[:], lhsT=AKT[:, g, kt * 128:(kt + 1) * 128], rhs=a[:, 4 * g:4 * g + 4, :], start=True, stop=True),
                       [("AKT", kt), ak], [pk])
                    pt, ptk_ = PT.next()
                    op("act", lambda e: e.activation(out=pt[:].rearrange("p h q -> p (h q)"), in_=p[:], func=AF.Exp, scale=0.125), [pk], [ptk_])
                    if mi is not None:
                        op("dve", lambda e: e.tensor_tensor(out=pt[:], in0=pt[:], in1=mk4[:, mi, :, :], op=ALU.mult), [ptk_, "mk4"], [ptk_])
                    for hh in range(4):
                        op("pe", lambda e: e.matmul(pO[g][:, hh, 0:65], lhsT=pt[:, hh, :], rhs=AV[:, kt, g, 0:65], start=(ki == 0 and hh == 0), stop=(ki == len(kl) - 1), skip_group_check=True),
                           [ptk_, ("AV", kt), "AVones"], [f"pOa{g}"])
                op("dve", lambda e: e.tensor_tensor(out=den[:], in0=pO[g][:, :, 64], in1=esink[:, 4 * g:4 * g + 4], op=ALU.add), [f"pOa{g}", "esink"], ["den"])
                op("dve", lambda e: e.reciprocal(out=den[:], in_=den[:]), ["den"], ["den"])
                op("dve", lambda e: e.tensor_tensor(out=o[:, 4 * g:4 * g + 4, :], in0=pO[g][:, :, 0:64], in1=den[:, :].unsqueeze(2).to_broadcast([128, 4, 64]), op=ALU.mult),
                   [f"pOa{g}", "den"], [ok])
            dma("sp", io["OA"][t * 128:(t + 1) * 128, :], o[:].rearrange("p h d -> p (h d)"), reads=[ok], writes=[("OA", t)])
        S.barrier()
        es.close()
        eso.close()
        if AB_STOP == 3:
            return
        es = ExitStack()
        sb, ps = mk(es)
        cB = sb("cB", [64, 4, 64], F32)
        dma("sp", cB[:], io["cB"][:, :, :], writes=["cB"])
        ones64 = sb("ones64", [64, 128], F32)
        op("dve", lambda e: e.memset(ones64[:], 1.0), [], ["ones64"])
        Sst = [sb(f"Sst{d}", [128, 4, 128], F32) for d in range(2)]
        for d in range(2):
            op("pool", lambda e: e.memset(Sst[d][:], 0.0), [], [f"S{d}"])
        NB = 6
        def rb(name, shape, n=NB):
            return RB([sb(f"{name}{k}", shape, F32) for k in range(n)], name)
        kT4 = rb("kT4", [128, 4, 64]); qT4 = rb("qT4", [128, 4, 64]); kv = rb("kvB", [64, 2, 512]); gbb = rb("gbB", [64, 16])
        Rm = rb("Rm", [64, 4, 64]); gcs = rb("gcs", [64, 4]); gls = rb("gls", [128, 4]); egc = rb("egc", [64, 4]); ekt = rb("ekt", [64, 4]); egl = rb("egl", [128, 4])
        Dm = rb("Dm", [64, 4, 64]); A0 = rb("A0", [64, 4, 64]); AT = rb("AT", [64, 4, 64]); Q = rb("Qm", [64, 4, 64]); Qfin = rb("Qfin", [64, 4, 64]); attT = rb("attT", [64, 4, 64])
        tmpv = rb("tmpv", [64, 4, 128]); rr = rb("rr", [64, 4, 128]); vnew = rb("vnew", [64, 4, 128]); ktok = rb("ktok", [64, 4, 128]); ob = rb("obB", [64, 4, 128])
        pb = RB([ps(f"pb{k}", [128, 512], F32) for k in range(8)], "pb")

        def P64(p):
            return p[0:64, 0:256].rearrange("p (h i) -> p h i", h=4)

        def P64w(p):
            return p[0:64, :].rearrange("p (h i) -> p h i", h=4)

        def prep(c, d, R_):
            tri = cB[:, d, :]
            nstr = cB[:, 2 + d, :]
            k4, k4k = kT4.next(); q4, q4k = qT4.next(); kvt, kvk = kv.next(); g_, gk = gbb.next()
            c0 = c * 64
            dma("sp", k4[:], io["BKT"][:, :, c0:c0 + 64].rearrange("h d t -> d h t"), writes=[k4k])
            dma("sp", q4[:], io["BQT"][:, :, c0:c0 + 64].rearrange("h d t -> d h t"), writes=[q4k])
            dma("sp", kvt[:], io["BKV"][c0:c0 + 64, :, :], writes=[kvk])
            dma("sp", g_[:], io["GB"][c0:c0 + 64, :], writes=[gk])
            g4 = g_[:, 4 * d:4 * d + 4]
            b4 = g_[:, 8 + 4 * d:8 + 4 * d + 4]
            R, Rk = Rm.next()
            op("dve", lambda e: e.tensor_tensor(out=R[:], in0=tri.unsqueeze(1).to_broadcast([64, 4, 64]), in1=g4.unsqueeze(2).to_broadcast([64, 4, 64]), op=ALU.mult),
               ["cB", gk], [Rk])
            pgr, pgrk = pb.next()
            op("pe", lambda e: e.matmul(pgr[0:64, 0:256], lhsT=ones64[:, 0:64], rhs=R[:].rearrange("p h i -> p (h i)"), start=True, stop=True), ["ones64", Rk], [pgrk])
            psm, psmk = pb.next()
            op("pe", lambda e: e.matmul(psm[0:64, 0:4], lhsT=tri, rhs=g4, start=True, stop=True), ["cB", gk], [psmk])
            op("pe", lambda e: e.matmul(psm[:, 4:8], lhsT=ones64[:, :], rhs=g4, start=True, stop=True), ["ones64", gk], [psmk])
            gc, gck = gcs.next(); gl, glk = gls.next(); eg, egk = egc.next(); ek, ekk = ekt.next(); el, elk = egl.next()
            op("dve", lambda e: e.tensor_copy(out=gc[:], in_=psm[0:64, 0:4]), [psmk], [gck])
            op("dve", lambda e: e.tensor_copy(out=gl[:], in_=psm[:, 4:8]), [psmk], [glk])
            op("act", lambda e: e.activation(out=eg[:], in_=gc[:], func=AF.Exp), [gck], [egk])
            op("dve", lambda e: e.tensor_tensor(out=ek[:], in0=gl[0:64, :], in1=gc[:], op=ALU.subtract), [glk, gck], [ekk])
            op("act", lambda e: e.activation(out=ek[:], in_=ek[:], func=AF.Exp), [ekk], [ekk])
            op("act", lambda e: e.activation(out=el[:], in_=gl[:], func=AF.Exp), [glk], [elk])
            D_, Dk = Dm.next()
            op("dve", lambda e: e.tensor_tensor(out=D_[:], in0=P64(pgr), in1=gc[:, :].unsqueeze(2).to_broadcast([64, 4, 64]), op=ALU.subtract), [pgrk, gck], [Dk])
            op("dve", lambda e: e.tensor_scalar_min(out=D_[:], in0=D_[:], scalar1=0.0), [Dk], [Dk])
            op("act", lambda e: e.activation(out=D_[:], in_=D_[:], func=AF.Exp), [Dk], [Dk])
            op("dve", lambda e: e.tensor_tensor(out=D_[:], in0=D_[:], in1=tri.unsqueeze(1).to_broadcast([64, 4, 64]), op=ALU.mult), [Dk, "cB"], [Dk])
            yield
            pkk, pkkk = pb.next()
            pqk, pqkk = pb.next()
            for h in range(4):
                op("pe", lambda e: e.matmul(pkk[0:64, h * 64:(h + 1) * 64], lhsT=k4[:, h, :], rhs=k4[:, h, :], start=True, stop=True), [k4k], [pkkk])
            for h in range(4):
                op("pe", lambda e: e.matmul(pqk[0:64, h * 64:(h + 1) * 64], lhsT=k4[:, h, :], rhs=q4[:, h, :], start=True, stop=True), [k4k, q4k], [pqkk])
            a0, a0k = A0.next()
            op("dve", lambda e: e.tensor_tensor(out=a0[:], in0=P64(pkk), in1=D_[:], op=ALU.mult), [pkkk, Dk], [a0k])
            op("dve", lambda e: e.tensor_tensor(out=a0[:], in0=a0[:], in1=nstr.unsqueeze(1).to_broadcast([64, 4, 64]), op=ALU.mult), [a0k, "cB"], [a0k])
            op("dve", lambda e: e.tensor_tensor(out=a0[:], in0=a0[:], in1=b4.unsqueeze(2).to_broadcast([64, 4, 64]), op=ALU.mult), [a0k, gk], [a0k])
            at_, atk = attT.next()
            op("dve", lambda e: e.tensor_tensor(out=at_[:], in0=P64(pqk), in1=D_[:], op=ALU.mult), [pqkk, Dk], [atk])
            ptt, pttk = pb.next()
            for h in range(4):
                op("pe", lambda e: e.transpose(out=ptt[0:64, h * 64:(h + 1) * 64], in_=a0[:, h, :], identity=idf[0:64, 0:64]), [a0k, "idf"], [pttk])
            aT_, aTk = AT.next()
            op("act", lambda e: e.copy(out=aT_[:], in_=P64(ptt)), [pttk], [aTk])
            q_, qk_ = Q.next()
            op("dve", lambda e: e.tensor_tensor(out=q_[:], in0=a0[:], in1=idf[0:64, 0:64].unsqueeze(1).to_broadcast([64, 4, 64]), op=ALU.add), [a0k, "idf"], [qk_])
            yield
            am, amk, amT, amTk = a0, a0k, aT_, aTk
            for m in range(1, 6):
                pAT, pATk = pb.next()
                for h in range(4):
                    op("pe", lambda e: e.matmul(pAT[0:64, h * 64:(h + 1) * 64], lhsT=am[:, h, :], rhs=amT[:, h, :], start=True, stop=True), [amk, amTk], [pATk])
                if m < 5:
                    pA, pAk = pb.next()
                    for h in range(4):
                        op("pe", lambda e: e.matmul(pA[0:64, h * 64:(h + 1) * 64], lhsT=amT[:, h, :], rhs=am[:, h, :], start=True, stop=True), [amk, amTk], [pAk])
                nT, nTk = AT.next()
                op("act", lambda e: e.copy(out=nT[:], in_=P64(pAT)), [pATk], [nTk])
                if m < 5:
                    nA, nAk = A0.next()
                    op("dve", lambda e: e.tensor_copy(out=nA[:], in_=P64(pA)), [pAk], [nAk])
                else:
                    nA, nAk = None, None
                pQ, pQk = pb.next()
                for h in range(4):
                    op("pe", lambda e: e.matmul(pQ[0:64, h * 64:(h + 1) * 64], lhsT=nT[:, h, :], rhs=q_[:, h, :], start=True, stop=True), [nTk, qk_], [pQk])
                nq, nqk = (Q.next() if m < 5 else Qfin.next())
                op("dve", lambda e: e.tensor_tensor(out=nq[:], in0=P64(pQ), in1=q_[:], op=ALU.add), [pQk, qk_], [nqk])
                q_, qk_ = nq, nqk
                am, amk, amT, amTk = nA, nAk, nT, nTk
                yield
            R_.update(dict(k4=k4, k4k=k4k, q4=q4, q4k=q4k, kvt=kvt, kvk=kvk, gk=gk, b4=b4, eg=eg, egk=egk, ek=ek, ekk=ekk, el=el, elk=elk, at_=at_, atk=atk, q_=q_, qk_=qk_))

        def scan(R_, c, d):
            k4, k4k, q4, q4k, kvt, kvk, gk, b4 = R_["k4"], R_["k4k"], R_["q4"], R_["q4k"], R_["kvt"], R_["kvk"], R_["gk"], R_["b4"]
            eg, egk, ek, ekk, el, elk, at_, atk, q_, qk_ = R_["eg"], R_["egk"], R_["ek"], R_["ekk"], R_["el"], R_["elk"], R_["at_"], R_["atk"], R_["q_"], R_["qk_"]
            c0 = c * 64
            Sd, Sk = Sst[d], f"S{d}"
            pks, pksk = pb.next()
            for h in range(4):
                op("pe", lambda e: e.matmul(pks[0:64, h * 128:(h + 1) * 128], lhsT=k4[:, h, :], rhs=Sd[:, h, :], start=True, stop=True), [k4k, Sk], [pksk])
            pqs, pqsk = pb.next()
            for h in range(4):
                op("pe", lambda e: e.matmul(pqs[0:64, h * 128:(h + 1) * 128], lhsT=q4[:, h, :], rhs=Sd[:, h, :], start=True, stop=True), [q4k, Sk], [pqsk])
            tv, tvk = tmpv.next()
            op("dve", lambda e: e.tensor_tensor(out=tv[:], in0=P64w(pks), in1=eg[:, :].unsqueeze(2).to_broadcast([64, 4, 128]), op=ALU.mult), [pksk, egk], [tvk])
            r_, rk = rr.next()
            v4 = kvt[:, 1, :].rearrange("p (h d) -> p h d", h=4)
            kk4 = kvt[:, 0, :].rearrange("p (h d) -> p h d", h=4)
            op("dve", lambda e: e.tensor_tensor(out=r_[:], in0=v4, in1=tv[:], op=ALU.subtract), [kvk, tvk], [rk])
            o_, ok_ = ob.next()
            op("dve", lambda e: e.tensor_tensor(out=o_[:], in0=P64w(pqs), in1=eg[:, :].unsqueeze(2).to_broadcast([64, 4, 128]), op=ALU.mult), [pqsk, egk], [ok_])
            kt_, ktk = ktok.next()
            op("dve", lambda e: e.tensor_tensor(out=kt_[:], in0=kk4, in1=ek[:, :].unsqueeze(2).to_broadcast([64, 4, 128]), op=ALU.mult), [kvk, ekk], [ktk])
            yield
            pv, pvk = pb.next()
            for h in range(4):
                op("pe", lambda e: e.matmul(pv[0:64, h * 128:(h + 1) * 128], lhsT=q_[:, h, :], rhs=r_[:, h, :], start=True, stop=True), [qk_, rk], [pvk])
            vn, vnk = vnew.next()
            op("dve", lambda e: e.tensor_tensor(out=vn[:], in0=P64w(pv), in1=b4.unsqueeze(2).to_broadcast([64, 4, 128]), op=ALU.mult), [pvk, gk], [vnk])
            yield
            pav, pavk = pb.next()
            for h in range(4):
                op("pe", lambda e: e.matmul(pav[0:64, h * 128:(h + 1) * 128], lhsT=at_[:, h, :], rhs=vn[:, h, :], start=True, stop=True), [atk, vnk], [pavk])
            psn, psnk = pb.next()
            for h in range(4):
                op("pe", lambda e: e.matmul(psn[:, h * 128:(h + 1) * 128], lhsT=kt_[:, h, :], rhs=vn[:, h, :], start=True, stop=True), [ktk, vnk], [psnk])
            op("dve", lambda e: e.tensor_tensor(out=o_[:], in0=P64w(pav), in1=o_[:], op=ALU.add), [pavk, ok_], [ok_])
            dma("pool", io["OB"][d, c0:c0 + 64, :], o_[:].rearrange("p h d -> p (h d)"), reads=[ok_], writes=[("OB", d, c)])
            op("dve", lambda e: e.tensor_tensor(out=Sd[:], in0=Sd[:], in1=el[:, :].unsqueeze(2).to_broadcast([128, 4, 128]), op=ALU.mult), [Sk, elk], [Sk])
            op("dve", lambda e: e.tensor_tensor(out=Sd[:], in0=psn[:, :].rearrange("p (h d) -> p h d", h=4), in1=Sd[:], op=ALU.add), [psnk, Sk], [Sk])
            yield

        def lockstep(gens):
            live = list(gens)
            while live:
                nxt = []
                for g_ in live:
                    try:
                        next(g_)
                        nxt.append(g_)
                    except StopIteration:
                        pass
                live = nxt

        order_f = list(range(0, 4)) + list(range(4, NCH))
        order_b = list(range(3, -1, -1)) + list(range(NCH - 1, 3, -1))
        PR = {}
        PR[(0, 0)] = {}; PR[(0, 1)] = {}
        lockstep([prep(order_f[0], 0, PR[(0, 0)]), prep(order_b[0], 1, PR[(0, 1)])])
        for s_ in range(NCH):
            gens = [scan(PR[(s_, 0)], order_f[s_], 0), scan(PR[(s_, 1)], order_b[s_], 1)]
            if s_ + 1 < NCH:
                PR[(s_ + 1, 0)] = {}; PR[(s_ + 1, 1)] = {}
                gens += [prep(order_f[s_ + 1], 0, PR[(s_ + 1, 0)]), prep(order_b[s_ + 1], 1, PR[(s_ + 1, 1)])]
            lockstep(gens)
            PR.pop((s_, 0)); PR.pop((s_, 1))
        S.barrier()
        es.close()
        if AB_STOP == 4:
            return
        es = ExitStack()
        sb, ps = mk(es)
        wout = sb("woutab", [128, 8, 1024], BF16)
        dma("pool", wout[:], io["ab_w_out"][i].rearrange("(kc p) n -> p kc n", p=128), writes=["woutab"])
        mods = load_mods(sb, l, 1, ("gt", "lng", "lnb"))
        W = Work(sb, ps, nxt=3, npy=2, with_pT=True, ntmp=0, nhb=0, nz=2)
        gn = sb("gn", [128, 128], F32)
        dma("sp", gn[:], io["ab_gnorm"][i:i + 1, :].broadcast_to([128, 128]), writes=["gn"])
        of = RB([sb(f"of{k}", [128, 512], F32) for k in range(2)], "of")
        obk = RB([sb(f"obk{k}", [128, 512], F32) for k in range(2)], "obk")
        zz = RB([sb(f"zz{k}", [128, 512], F32) for k in range(2)], "zz")
        sq5 = sb("sq5", [128, 512], F32)
        ss5 = sb("ss5", [128, 4], F32)
        oc = RB([sb(f"oc{k}", [128, 1024], BF16) for k in range(2)], "oc")
        oT = RB([sb(f"oT5{k}", [128, 8, 128], BF16) for k in range(2)], "oT5")
        for t in q_tiles:
            r = 1 if t < 2 else 0
            rows = slice(t * 128, (t + 1) * 128)
            f_, fk = of.next(); b_, bk = obk.next(); z_, zk = zz.next(); o_, ok_ = oc.next()
            dma("sp", f_[:], io["OB"][0, rows, :], writes=[fk])
            dma("sp", b_[:], io["OB"][1, rows, :], writes=[bk])
            dma("sp", z_[:], io["BZ"][rows, :], writes=[zk])
            dma("sp", o_[:, 0:512], io["OA"][rows, :], writes=[ok_ + "a"])
            op("dve", lambda e: e.tensor_tensor(out=f_[:], in0=f_[:], in1=b_[:], op=ALU.add), [fk, bk], [fk])
            op("act", lambda e: e.activation(out=sq5[:], in_=f_[:], func=AF.Square), [fk], ["sq5"])
            op("dve", lambda e: e.tensor_reduce(out=ss5[:], in_=sq5[:].rearrange("p (h d) -> p h d", d=128), axis=AX.X, op=ALU.add), ["sq5"], ["ss5"])
            op("act", lambda e: e.activation(out=ss5[:], in_=ss5[:], func=AF.Sqrt, bias=epsb[:, 0:1], scale=1.0 / 128), ["ss5", "epsb"], ["ss5"])
            op("dve", lambda e: e.reciprocal(out=ss5[:], in_=ss5[:]), ["ss5"], ["ss5"])
            f3 = f_[:].rearrange("p (h d) -> p h d", d=128)
            op("dve", lambda e: e.tensor_tensor(out=f3, in0=f3, in1=ss5[:, :].unsqueeze(2).to_broadcast([128, 4, 128]), op=ALU.mult), [fk, "ss5"], [fk])
            op("dve", lambda e: e.tensor_tensor(out=f3, in0=f3, in1=gn[:, :].unsqueeze(1).to_broadcast([128, 4, 128]), op=ALU.mult), [fk, "gn"], [fk])
            op("dve", lambda e: e.tensor_tensor(out=o_[:, 512:1024], in0=f_[:], in1=z_[:], op=ALU.mult), [fk, zk], [ok_ + "b"])
            ot, otk = oT.next()
            pT, ptk2 = W.pT.next()
            for kc in range(8):
                op("pe", lambda e: e.transpose(out=pT[:, kc, :], in_=o_[:, kc * 128:(kc + 1) * 128], identity=idb[:]), [ok_ + "a", ok_ + "b", "idb"], [ptk2])
            op("act", lambda e: e.copy(out=ot[:], in_=pT[:]), [ptk2], [otk])
            xt, xk = load_x(W, src, t)
            out_proj(W, ot, otk, wout, "woutab", xt, xk, mods["gt"][r], f"mod_gt{r}", mods, dst, t)
        S.barrier()
        es.close()

    return phase


import numpy as np

D = 1024
SEQ = 4096
CTX = 256
NTOK = SEQ + CTX
NT = NTOK // 128
DFF = 2816
NJ = DFF // 128
DEPTH = 4
ALPHA = (2 * DEPTH) ** 0.25
EPS = 1e-6
AB_IN = 2832


class RB:
    def __init__(self, items, name):
        self.items = items
        self.name = name
        self.i = 0

    def next(self):
        k = self.i
        self.i = (k + 1) % len(self.items)
        return self.items[k], f"{self.name}{k}"


def build(nc, S, io, layers=(0, 1, 2, 3), out_mode="final", skip_ctx_last=True):
    from contextlib import ExitStack
    es0 = ExitStack()

    uid = [0]

    def mk(es):
        def sb(name, shape, dt):
            uid[0] += 1
            return es.enter_context(nc.sbuf_tensor(f"{name}_u{uid[0]}", shape, dt))

        def ps(name, shape, dt):
            uid[0] += 1
            return es.enter_context(nc.psum_tensor(f"{name}_u{uid[0]}", shape, dt))
        return sb, ps

    sb0, ps0 = mk(es0)
    op, dma = S.op, S.dma
    XS = io["XS"]
    MOD = io["MOD"]

    idf = sb0("idf", [128, 128], F32)
    idb = sb0("idb", [128, 128], BF16)
    epsb = sb0("epsb", [128, 1], F32)
    dma("sp", idf[:], io["ident"][:, :], writes=["idf"])
    op("dve", lambda e: e.tensor_copy(out=idb[:], in_=idf[:]), ["idf"], ["idb"])
    op("dve", lambda e: e.memset(epsb[:], EPS), [], ["epsb"])

    def xrows(ap, t):
        return ap[t * 128:(t + 1) * 128, :]

    def phase_mod():
        es = ExitStack()
        sb, ps = mk(es)
        scT = sb("scT", [128, 8, 2], F32)
        aw = [sb(f"aw{i}", [128, 8, 512], F32) for i in range(3)]
        awr = RB(aw, "aw")
        mrow = sb("mrow", [2, 9216], F32)
        brow = sb("brow", [2, 9216], F32)
        pm = [ps(f"pm{i}", [128, 512], F32)[0:2, :] for i in range(2)]
        pmr = RB(pm, "pm")
        dma("sp", scT[:], io["cvecT"][:, :, :], writes=["scT"])
        op("act", lambda e: e.activation(out=scT[:], in_=scT[:], func=AF.Silu), ["scT"], ["scT"])
        for l in layers:
            dma("sp", brow[:], io["ada_b"][l:l + 1, :].broadcast_to([2, 9216]), writes=["brow"])
            wv = io["ada_w"][l].rearrange("(kc p) n -> p kc n", p=128)
            for n in range(18):
                a, ak = awr.next()
                dma("sp" if n % 2 == 0 else "act", a[:], wv[:, :, n * 512:(n + 1) * 512], writes=[ak])
                p, pk = pmr.next()
                for kc in range(8):
                    op("pe", lambda e: e.matmul(p[:], lhsT=scT[:, kc, :], rhs=a[:, kc, :], start=(kc == 0), stop=(kc == 7)),
                       ["scT", ak], [pk])
                op("dve", lambda e: e.tensor_tensor(out=mrow[:, n * 512:(n + 1) * 512], in0=p[:], in1=brow[:, n * 512:(n + 1) * 512], op=ALU.add),
                   [pk, "brow"], ["mrow"])
            for s in range(3):
                c0 = (3 * s + 1) * 1024
                op("dve", lambda e: e.tensor_scalar_add(out=mrow[:, c0:c0 + 1024], in0=mrow[:, c0:c0 + 1024], scalar1=1.0), ["mrow"], ["mrow"])
                if s != 1:
                    c1 = (3 * s + 2) * 1024
                    op("dve", lambda e: e.tensor_scalar_mul(out=mrow[:, c1:c1 + 1024], in0=mrow[:, c1:c1 + 1024], scalar1=0.5), ["mrow"], ["mrow"])
            dma("sp", MOD[l, :, :], mrow[:], reads=["mrow"], writes=["MOD"])
        S.barrier()
        es.close()

    def load_mods(sb, l, s, which):
        out = {}
        for nm in which:
            if nm in ("sh", "sc", "gt"):
                k = {"sh": 0, "sc": 1, "gt": 2}[nm]
                tl = []
                for r in range(2):
                    t = sb(f"mod_{nm}{r}", [128, 1024], F32)
                    dma("sp", t[:], MOD[l, r:r + 1, (3 * s + k) * 1024:(3 * s + k + 1) * 1024].broadcast_to([128, 1024]),
                        reads=["MOD"], writes=[f"mod_{nm}{r}"])
                    tl.append(t)
                out[nm] = tl
            else:
                src = io["ln_g"] if nm == "lng" else io["ln_b"]
                t = sb(f"mod_{nm}", [128, 1024], F32)
                dma("sp", t[:], src[l, s:s + 1, :].broadcast_to([128, 1024]), writes=[f"mod_{nm}"])
                out[nm] = t
        return out

    class Work:
        def __init__(self, sb, ps, nxt=6, npy=2, with_pT=True, ntmp=2, nhb=2, nz=2):
            self.xt = RB([sb(f"xt{i}", [128, 1024], F32) for i in range(nxt)], "xt")
            self.tmp = RB([sb(f"mtmp{i}", [128, 1024], F32) for i in range(ntmp)], "mtmp")
            self.hb = RB([sb(f"hb{i}", [128, 1024], BF16) for i in range(nhb)], "hb")
            self.z = RB([sb(f"z{i}", [128, 1024], F32) for i in range(nz)], "z")
            self.st6 = sb("st6", [128, 2, 6], F32)
            self.mv = sb("mv", [128, 2], F32)
            self.rstd = sb("rstd", [128, 1], F32)
            self.nb = sb("nb", [128, 1], F32)
            if with_pT:
                self.pT = RB([ps(f"pT{i}", [128, 8, 128], BF16) for i in range(2)], "pT")
            self.py = RB([ps(f"py{i}", [128, 512], F32) for i in range(npy)], "py")

    def load_x(W, src, t):
        xt, xk = W.xt.next()
        dma("sp", xt[:], xrows(src, t), reads=[("X", t)], writes=[xk])
        return xt, xk

    def modulate(W, xt, xk, mods, r):
        tmp, tk = W.tmp.next()
        hb, hk = W.hb.next()
        op("pool", lambda e: e.tensor_tensor(out=tmp[:], in0=xt[:], in1=mods["sc"][r][:], op=ALU.mult), [xk, f"mod_sc{r}"], [tk])
        op("pool", lambda e: e.tensor_tensor(out=hb[:], in0=tmp[:], in1=mods["sh"][r][:], op=ALU.add), [tk, f"mod_sh{r}"], [hk])
        return hb, hk

    def transpose8(W, src, sk, dst_ap, dk, eng="act"):
        pT, pk = W.pT.next()
        for kc in range(8):
            op("pe", lambda e: e.transpose(out=pT[:, kc, :], in_=src[:, kc * 128:(kc + 1) * 128], identity=idb[:]), [sk, "idb"], [pk])
        if eng == "act":
            op("act", lambda e: e.copy(out=dst_ap, in_=pT[:]), [pk], [dk])
        else:
            op("dve", lambda e: e.tensor_copy(out=dst_ap, in_=pT[:]), [pk], [dk])

    def cast_wgu(l, j, slot):
        wv = io["ffn_w_gu"][l, j].rearrange("(kc p) n -> p kc n", p=128)
        dst = io["WGUB"][slot]
        for g in range(11):
            for u in range(2):
                dma("pool", dst[g, :, :, u * 256:(u + 1) * 256], wv[:, :, u * DFF + g * 256:u * DFF + (g + 1) * 256],
                    writes=[("WGUB", slot, g)])

    def phase_ffn(l, j, src, dst, tiles, slot, final=False):
        s = 0 if j == 0 else 2
        es = ExitStack()
        sb, ps = mk(es)
        wd = sb("wd", [128, NJ, 1024], BF16)
        wdv = io["ffn_w_down"][l, j].rearrange("(jc p) n -> p jc n", p=128)
        for q in range(2):
            dma("pool", wd[:, q * 11:(q + 1) * 11, :], wdv[:, q * 11:(q + 1) * 11, :], writes=[f"wd{q}"])
        mods = load_mods(sb, l, s, ("sh", "sc", "gt", "lng", "lnb"))
        W = Work(sb, ps, nxt=8, npy=2)
        wg = RB([sb(f"wg{i}", [128, 8, 512], BF16) for i in range(3)], "wg")
        hT = RB([sb(f"hT{i}", [128, 8, 512], BF16) for i in range(2)], "hT")
        aT = sb("aT", [128, NJ, 512], BF16)
        sg = RB([sb(f"sg{i}", [128, 512], F32) for i in range(2)], "sg")
        pg = RB([ps(f"pg{i}", [128, 512], F32) for i in range(2)], "pg")
        pu = RB([ps(f"pu{i}", [128, 512], F32) for i in range(2)], "pu")
        sts = [tiles[i:i + 4] for i in range(0, len(tiles), 4)]

        def prologue(st):
            h, hk = hT.next()
            xs = []
            for ti, t in enumerate(st):
                xt, xk = load_x(W, src, t)
                xs.append((xt, xk))
                r = 1 if t < 2 else 0
                hb, hbk = modulate(W, xt, xk, mods, r)
                transpose8(W, hb, hbk, h[:, :, ti * 128:(ti + 1) * 128], hk)
            return h, hk, xs

        nxt = prologue(sts[0])
        for si, st in enumerate(sts):
            n = 128 * len(st)
            h, hk, xs = nxt
            for g in range(11):
                w, wk = wg.next()
                dma("sp", w[:], io["WGUB"][slot][g], reads=[("WGUB", slot, g)], writes=[wk])
                for c in range(2):
                    jj = 2 * g + c
                    p1, p1k = pg.next()
                    p2, p2k = pu.next()
                    for kc in range(8):
                        op("pe", lambda e: e.matmul(p1[:, :n], lhsT=w[:, kc, c * 128:(c + 1) * 128], rhs=h[:, kc, :n], start=(kc == 0), stop=(kc == 7)),
                           [wk, hk], [p1k])
                    for kc in range(8):
                        op("pe", lambda e: e.matmul(p2[:, :n], lhsT=w[:, kc, 256 + c * 128:256 + (c + 1) * 128], rhs=h[:, kc, :n], start=(kc == 0), stop=(kc == 7)),
                           [wk, hk], [p2k])
                    s1, s1k = sg.next()
                    op("act", lambda e: e.activation(out=s1[:, :n], in_=p1[:, :n], func=AF.Silu), [p1k], [s1k])
                    op("dve", lambda e: e.tensor_tensor(out=aT[:, jj, :n], in0=p2[:, :n], in1=s1[:, :n], op=ALU.mult), [p2k, s1k], [("aT", jj)])
            if si + 1 < len(sts):
                nxt = prologue(sts[si + 1])
            for ti, t in enumerate(st):
                xt, xk = xs[ti]
                r = 1 if t < 2 else 0
                z, zk = W.z.next()
                for nh in range(2):
                    py, pk = W.py.next()
                    for k in range(NJ):
                        op("pe", lambda e: e.matmul(py[:], lhsT=aT[:, k, ti * 128:(ti + 1) * 128], rhs=wd[:, k, nh * 512:(nh + 1) * 512], start=(k == 0), stop=(k == NJ - 1)),
                           [("aT", k), f"wd{k // 11}"], [pk])
                    op("dve", lambda e: e.tensor_tensor(out=z[:, nh * 512:(nh + 1) * 512], in0=py[:], in1=mods["gt"][r][:, nh * 512:(nh + 1) * 512], op=ALU.mult),
                       [pk, f"mod_gt{r}"], [zk])
                ln_store(W, z, zk, xt, xk, mods, dst, t, final)
        S.barrier()
        es.close()

    def ln_store(W, z, zk, xt, xk, mods, dst, t, final):
        op("dve", lambda e: e.scalar_tensor_tensor(out=z[:], in0=xt[:], scalar=float(ALPHA), in1=z[:], op0=ALU.mult, op1=ALU.add), [xk, zk], [zk])
        for c in range(2):
            op("dve", lambda e: e.bn_stats(out=W.st6[:, c, :], in_=z[:, c * 512:(c + 1) * 512]), [zk], ["st6"])
        op("dve", lambda e: e.bn_aggr(out=W.mv[:], in_=W.st6[:]), ["st6"], ["mv"])
        op("act", lambda e: e.activation(out=W.rstd[:], in_=W.mv[:, 1:2], func=AF.Sqrt, bias=epsb[:, 0:1], scale=1.0), ["mv", "epsb"], ["rstd"])
        op("dve", lambda e: e.reciprocal(out=W.rstd[:], in_=W.rstd[:]), ["rstd"], ["rstd"])
        op("dve", lambda e: e.scalar_tensor_tensor(out=W.nb[:], in0=W.mv[:, 0:1], scalar=-1.0, in1=W.rstd[:], op0=ALU.mult, op1=ALU.mult), ["mv", "rstd"], ["nb"])
        op("act", lambda e: e.activation(out=z[:], in_=z[:], func=AF.Identity, bias=W.nb[:, 0:1], scale=W.rstd[:, 0:1]), [zk, "nb", "rstd"], [zk])
        op("pool", lambda e: e.tensor_tensor(out=z[:], in0=z[:], in1=mods["lng"][:], op=ALU.mult), [zk, "mod_lng"], [zk])
        op("pool", lambda e: e.tensor_tensor(out=z[:], in0=z[:], in1=mods["lnb"][:], op=ALU.add), [zk, "mod_lnb"], [zk])
        if final:
            dma("sp", io["out"][(t - 2) * 128:(t - 1) * 128, :], z[:], reads=[zk], writes=[("OUT", t)])
        else:
            dma("sp", xrows(dst, t), z[:], reads=[zk], writes=[("X", t)])

    def out_proj(W, oT, ok, wt, wk, xt, xk, gate, gk, mods, dst, t):
        z, zk = W.z.next()
        for nh in range(2):
            py, pk = W.py.next()
            for k in range(8):
                op("pe", lambda e: e.matmul(py[:], lhsT=oT[:, k, :], rhs=wt[:, k, nh * 512:(nh + 1) * 512], start=(k == 0), stop=(k == 7)),
                   [ok, wk], [pk])
            op("dve", lambda e: e.tensor_tensor(out=z[:, nh * 512:(nh + 1) * 512], in0=py[:], in1=gate[:, nh * 512:(nh + 1) * 512], op=ALU.mult),
               [pk, gk], [zk])
        ln_store(W, z, zk, xt, xk, mods, dst, t, False)

    def phase_mixer_c(l, src, dst, q_tiles):
        i = l // 2
        SC = 128 ** -0.5
        eso = ExitStack()
        sbo, pso = mk(eso)
        QT = sbo("QT", [128, NT, 8, 128], BF16)
        KT = sbo("KT", [128, 2, NTOK], BF16)
        V = sbo("V", [128, NT, 2, 130], BF16)
        op("pool", lambda e: e.memset(V[:, :, :, 128:130], 1.0), [], ["Vones"])
        es = ExitStack()
        sb, ps = mk(es)
        win = sb("win", [128, 8, 1536], BF16)
        dma("pool", win[:], io["c_w_in"][i].rearrange("(kc p) n -> p kc n", p=128), writes=["win"])
        gq = sb("gq", [128, 10, 128], F32)
        for h in range(10):
            srcg = io["c_q_norm"] if h < 8 else io["c_k_norm"]
            dma("sp", gq[:, h, :], srcg[i:i + 1, :].broadcast_to([128, 128]), writes=["gq"])
        mods = load_mods(sb, l, 1, ("sh", "sc"))
        W = Work(sb, ps, nxt=2, npy=1, ntmp=1, nhb=2, nz=0)
        hT = RB([sb(f"hTc{k}", [128, 8, 128], BF16) for k in range(2)], "hTc")
        qkv = RB([sb(f"qkv{k}", [128, 1536], F32) for k in range(2)], "qkv")
        sq = sb("sq", [128, 1280], F32)
        ss = sb("ss", [128, 10], F32)
        cs = RB([sb(f"cs{k}", [128, 2, 64], F32) for k in range(2)], "cs")
        ra = sb("ra", [128, 10, 64], F32)
        rb_ = sb("rb", [128, 10, 64], F32)
        qr = RB([sb(f"qr{k}", [128, 10, 128], BF16) for k in range(2)], "qr")
        pq = [ps(f"pq{k}", [128, 512], F32) for k in range(3)]
        ptr = ps("ptr", [128, 16, 128], BF16)
        for t in range(NT):
            r = 1 if t < 2 else 0
            xt, xk = load_x(W, src, t)
            hb, hbk = modulate(W, xt, xk, mods, r)
            h, hk = hT.next()
            transpose8(W, hb, hbk, h[:], hk)
            qv, qk = qkv.next()
            for n in range(3):
                for kc in range(8):
                    op("pe", lambda e: e.matmul(pq[n][:], lhsT=h[:, kc, :], rhs=win[:, kc, n * 512:(n + 1) * 512], start=(kc == 0), stop=(kc == 7)),
                       [hk, "win"], [f"pq{n}"])
                op("act", lambda e: e.copy(out=qv[:, n * 512:(n + 1) * 512], in_=pq[n][:]), [f"pq{n}"], [qk])
            op("act", lambda e: e.activation(out=sq[:], in_=qv[:, 0:1280], func=AF.Square), [qk], ["sq"])
            op("dve", lambda e: e.tensor_reduce(out=ss[:], in_=sq[:].rearrange("p (h d) -> p h d", d=128), axis=AX.X, op=ALU.add), ["sq"], ["ss"])
            op("act", lambda e: e.activation(out=ss[:], in_=ss[:], func=AF.Sqrt, bias=epsb[:, 0:1], scale=1.0 / 128), ["ss", "epsb"], ["ss"])
            op("dve", lambda e: e.reciprocal(out=ss[:], in_=ss[:]), ["ss"], ["ss"])
            q3 = qv[:, 0:1280].rearrange("p (h d) -> p h d", d=128)
            op("dve", lambda e: e.tensor_tensor(out=q3, in0=q3, in1=ss[:, :].unsqueeze(2).to_broadcast([128, 10, 128]), op=ALU.mult), [qk, "ss"], [qk])
            qo, qok = qr.next()
            if r == 1:
                op("dve", lambda e: e.tensor_tensor(out=qo[:], in0=q3, in1=gq[:], op=ALU.mult), [qk, "gq"], [qok])
            else:
                op("dve", lambda e: e.tensor_tensor(out=q3, in0=q3, in1=gq[:], op=ALU.mult), [qk, "gq"], [qk])
                c, ck = cs.next()
                p0 = (t - 2) * 128
                dma("sp", c[:, 0, :], io["cosC"][p0:p0 + 128, :], writes=[ck])
                dma("sp", c[:, 1, :], io["sinC"][p0:p0 + 128, :], writes=[ck])
                q4 = qv[:, 0:1280].rearrange("p (h two d) -> p h two d", two=2, d=64)
                o4 = qo[:].rearrange("p h (two d) -> p h two d", two=2)
                cosb = c[:, 0, :].unsqueeze(1).to_broadcast([128, 10, 64])
                sinb = c[:, 1, :].unsqueeze(1).to_broadcast([128, 10, 64])
                x1, x2 = q4[:, :, 0, :], q4[:, :, 1, :]
                op("dve", lambda e: e.tensor_tensor(out=ra[:], in0=x1, in1=cosb, op=ALU.mult), [qk, ck], ["ra"])
                op("dve", lambda e: e.tensor_tensor(out=rb_[:], in0=x2, in1=sinb, op=ALU.mult), [qk, ck], ["rb"])
                op("dve", lambda e: e.tensor_tensor(out=o4[:, :, 0, :], in0=ra[:], in1=rb_[:], op=ALU.subtract), ["ra", "rb"], [qok])
                op("dve", lambda e: e.tensor_tensor(out=ra[:], in0=x1, in1=sinb, op=ALU.mult), [qk, ck], ["ra"])
                op("dve", lambda e: e.tensor_tensor(out=rb_[:], in0=x2, in1=cosb, op=ALU.mult), [qk, ck], ["rb"])
                op("dve", lambda e: e.tensor_tensor(out=o4[:, :, 1, :], in0=ra[:], in1=rb_[:], op=ALU.add), ["ra", "rb"], [qok])
            for hh in range(10):
                op("pe", lambda e: e.transpose(out=ptr[:, hh, :], in_=qo[:, hh, :], identity=idb[:]), [qok, "idb"], ["ptr"])
            op("act", lambda e: e.copy(out=QT[:, t, :, :], in_=ptr[:, 0:8, :]), ["ptr"], [("QT", t)])
            op("dve", lambda e: e.tensor_copy(out=KT[:, :, t * 128:(t + 1) * 128], in_=ptr[:, 8:10, :]), ["ptr"], [("KT", t)])
            op("pool", lambda e: e.tensor_copy(out=V[:, t, :, 0:128], in_=qv[:, 1280:1536].rearrange("p (g d) -> p g d", g=2)), [qk], [("V", t)])
        S.barrier()
        es.close()
        es = ExitStack()
        sb, ps = mk(es)
        wout = sb("wout", [128, 8, 1024], BF16)
        dma("pool", wout[:], io["c_w_out"][i].rearrange("(kc p) n -> p kc n", p=128), writes=["wout"])
        mods = load_mods(sb, l, 1, ("gt", "lng", "lnb"))
        W = Work(sb, ps, nxt=3, npy=1, with_pT=True, ntmp=0, nhb=0, nz=2)
        PT = RB([sb(f"PT{k}", [128, 512], BF16) for k in range(3)], "PT")
        ob = RB([sb(f"ob{k}", [128, 8, 128], BF16) for k in range(2)], "ob")
        oT = RB([sb(f"oT{k}", [128, 8, 128], BF16) for k in range(2)], "oT")
        rden = sb("rden", [128, 4], F32)
        pS = RB([ps(f"pS{k}", [128, 512], F32) for k in range(2)], "pS")
        pO = [ps(f"pO{k}", [128, 4, 128], F32) for k in range(2)]
        pO2f = ps("pO2", [128, 512], F32)
        pO2 = pO2f[:, 0:8]
        for t in q_tiles:
            r = 1 if t < 2 else 0
            kts = [0, 1] if r == 1 else list(range(NT))
            o, ok = ob.next()
            for g in range(2):
                for ki, kt in enumerate(kts):
                    p, pk = pS.next()
                    op("pe", lambda e: e.matmul(p[:], lhsT=KT[:, g, kt * 128:(kt + 1) * 128], rhs=QT[:, t, 4 * g:4 * g + 4, :], start=True, stop=True),
                       [("KT", kt), ("QT", t)], [pk])
                    pt, ptk = PT.next()
                    op("act", lambda e: e.activation(out=pt[:], in_=p[:], func=AF.Exp, scale=float(SC)), [pk], [ptk])
                    for hh in range(4):
                        op("pe", lambda e: e.matmul(pO[g][:, hh, :], lhsT=pt[:, hh * 128:(hh + 1) * 128], rhs=V[:, kt, g, 0:128],
                                                    start=(ki == 0 and hh == 0), stop=(ki == len(kts) - 1), skip_group_check=True), [ptk, ("V", kt)], [f"pO{g}"])
                        op("pe", lambda e: e.matmul(pO2[:, 4 * g + hh:4 * g + hh + 1], lhsT=pt[:, hh * 128:(hh + 1) * 128], rhs=V[:, kt, g, 128:129],
                                                    start=(ki == 0 and hh == 0), stop=(ki == len(kts) - 1), skip_group_check=True), [ptk, "Vones"], [f"pO2{g}"])
                op("dve", lambda e: e.reciprocal(out=rden[:], in_=pO2[:, 4 * g:4 * g + 4]), [f"pO2{g}"], ["rden"])
                op("dve", lambda e: e.tensor_tensor(out=o[:, 4 * g:4 * g + 4, :], in0=pO[g][:], in1=rden[:, :].unsqueeze(2).to_broadcast([128, 4, 128]), op=ALU.mult),
                   [f"pO{g}", "rden"], [ok])
            ot, otk = oT.next()
            pT, ptk2 = W.pT.next()
            for kc in range(8):
                op("pe", lambda e: e.transpose(out=pT[:, kc, :], in_=o[:, kc, :], identity=idb[:]), [ok, "idb"], [ptk2])
            op("act", lambda e: e.copy(out=ot[:], in_=pT[:]), [ptk2], [otk])
            xt, xk = load_x(W, src, t)
            out_proj(W, ot, otk, wout, "wout", xt, xk, mods["gt"][r], f"mod_gt{r}", mods, dst, t)
        S.barrier()
        es.close()
        eso.close()

    io["mixer_ab"] = make_mixer_ab(nc, S, io, mk, idf, idb, epsb, (Work, load_x, modulate, transpose8, load_mods, out_proj))
    phase_mod()
    all_tiles = list(range(NT))
    lat_tiles = list(range(2, NT))
    ffn_list = [(l, j) for l in layers for j in range(2)]
    cast_wgu(ffn_list[0][0], ffn_list[0][1], 0)
    fi = 0
    cur = io["xin"]
    for li, l in enumerate(layers):
        last = (li == len(layers) - 1)
        if fi + 1 < len(ffn_list):
            cast_wgu(ffn_list[fi + 1][0], ffn_list[fi + 1][1], (fi + 1) % 2)
        phase_ffn(l, 0, cur, XS, all_tiles, fi % 2)
        fi += 1
        cur = XS
        qt = lat_tiles if (last and skip_ctx_last) else all_tiles
        if l % 2 == 1:
            phase_mixer_c(l, XS, XS, qt)
        else:
            io["mixer_ab"](l, XS, XS, qt)
        if fi + 1 < len(ffn_list):
            cast_wgu(ffn_list[fi + 1][0], ffn_list[fi + 1][1], (fi + 1) % 2)
        phase_ffn(l, 1, XS, XS, qt, fi % 2, final=last)
        fi += 1
    S.barrier()
    es0.close()


import numpy as np


def _consts():
    c = {}
    c["ident"] = np.eye(128, dtype=np.float32)

    def rope(hd):
        nf = hd // 4
        inv = (10000.0 ** (-np.arange(nf, dtype=np.float32) / nf)).astype(np.float32)
        r, col = np.meshgrid(np.arange(64, dtype=np.float32), np.arange(64, dtype=np.float32), indexing="ij")
        r, col = r.reshape(-1), col.reshape(-1)
        ang = np.concatenate([r[:, None] * inv, col[:, None] * inv], axis=-1).astype(np.float32)
        return np.cos(ang).astype(np.float32), np.sin(ang).astype(np.float32)
    c["cosC"], c["sinC"] = rope(128)
    c["cosA"], c["sinA"] = rope(64)
    j = np.arange(128)[:, None]
    i = np.arange(128)[None, :]
    c["cA"] = np.stack([(j >= i), (j <= i)], axis=1).astype(np.float32)
    j = np.arange(64)[:, None]
    i = np.arange(64)[None, :]
    c["cB"] = np.stack([(j <= i), (j >= i), -1.0 * (i > j), -1.0 * (i < j)], axis=1).astype(np.float32)
    return c


W_NAMES = ["ada_w", "ada_b", "ln_g", "ln_b", "ffn_w_gu", "ffn_w_down", "ab_w_in", "ab_conv_w", "ab_a_log",
           "ab_dt_bias", "ab_gnorm", "ab_sink", "ab_w_out", "c_w_in", "c_q_norm", "c_k_norm", "c_w_out"]


def make_program(shapes, layers=(0, 1, 2, 3), dbg=False):
    nc = bass.Bass("TRN2", target_bir_lowering=False)
    try:
        nc.allow_low_precision("bf16 matmul operands, fp32 accumulation")
    except Exception:
        pass
    io = {}
    io["xin"] = nc.dram_tensor("xin", [NTOK, D], F32, kind="ExternalInput").ap()
    io["cvecT"] = nc.dram_tensor("cvecT", [128, 8, 2], F32, kind="ExternalInput").ap()
    for k in W_NAMES:
        io[k] = nc.dram_tensor(k, list(shapes[k]), F32, kind="ExternalInput").ap()
    for k, v in _consts().items():
        io[k] = nc.dram_tensor(k, list(v.shape), F32, kind="ExternalInput").ap()
    io["out"] = nc.dram_tensor("out", [SEQ, D], F32, kind="ExternalOutput").ap()
    io["XS"] = nc.dram_tensor("XS", [NTOK, D], F32, kind="ExternalOutput" if dbg else "Internal").ap()
    io["MOD"] = nc.dram_tensor("MOD", [4, 2, 9216], F32, kind="ExternalOutput" if dbg else "Internal").ap()
    io["AQT"] = nc.dram_tensor("AQT", [NT, 64, 8, 128], BF16, kind="Internal").ap()
    io["BZ"] = nc.dram_tensor("BZ", [NTOK, 512], F32, kind="Internal").ap()
    io["GB"] = nc.dram_tensor("GB", [NTOK, 16], F32, kind="Internal").ap()
    io["BQT"] = nc.dram_tensor("BQT", [4, 128, NTOK], F32, kind="Internal").ap()
    io["BKT"] = nc.dram_tensor("BKT", [4, 128, NTOK], F32, kind="Internal").ap()
    io["BKV"] = nc.dram_tensor("BKV", [NTOK, 2, 512], F32, kind="Internal").ap()
    io["OA"] = nc.dram_tensor("OA", [NTOK, 512], BF16, kind="ExternalOutput" if dbg else "Internal").ap()
    io["OB"] = nc.dram_tensor("OB", [2, NTOK, 512], F32, kind="ExternalOutput" if dbg else "Internal").ap()
    io["WGUB"] = [nc.dram_tensor(f"WGUB{i}", [11, 128, 8, 512], BF16, kind="Internal").ap() for i in range(2)]
    S = Sched(nc)
    build(nc, S, io, layers=layers)
    return nc, S


def make_in_maps(inputs):
    x = np.asarray(inputs["x"], dtype=np.float32)
    c = np.asarray(inputs["c"], dtype=np.float32)
    ctx = np.asarray(inputs["ctx"], dtype=np.float32)
    c_ctx = np.asarray(inputs["c_ctx"], dtype=np.float32)
    consts = _consts()
    shared = {k: np.ascontiguousarray(np.asarray(inputs[k], dtype=np.float32)) for k in W_NAMES}
    shared.update(consts)
    maps = []
    for b in range(8):
        m = dict(shared)
        m["xin"] = np.ascontiguousarray(np.concatenate([ctx[b], x[b]], axis=0))
        cv = np.stack([c[b], c_ctx], axis=0)
        m["cvecT"] = np.ascontiguousarray(cv.reshape(2, 8, 128).transpose(2, 1, 0))
        maps.append(m)
    return maps


def kernel(**inputs):
    shapes = {k: np.asarray(inputs[k]).shape for k in W_NAMES}
    nc, S = make_program(shapes)
    maps = make_in_maps(inputs)
    res = run_bass_kernel_spmd(nc, maps, core_ids=list(range(8)))
    return np.stack([np.asarray(r["out"], dtype=np.float32) for r in res.results], axis=0)
```

```python
from concourse.bass_utils import run_bass_kernel_spmd
from contextlib import ExitStack
import numpy as np
import concourse.bass as bass
import concourse.mybir as mybir

F32 = mybir.dt.float32
BF16 = mybir.dt.bfloat16
AF = mybir.ActivationFunctionType
ALU = mybir.AluOpType
AX = mybir.AxisListType

ENGS = ("pe", "act", "dve", "pool", "sp")


class Sched:
    NDMA = 6

    def __init__(self, nc):
        self.nc = nc
        self.es = ExitStack()
        self.eng = {"pe": nc.tensor, "act": nc.scalar, "dve": nc.vector,
                    "pool": nc.gpsimd, "sp": nc.sync}
        self.sem = {}
        self.cnt = {}
        for e in ENGS:
            self.sem[e] = self.es.enter_context(nc.semaphore("c_" + e))
            self.cnt[e] = 0
        self.dq = {}
        for q in ("sp", "pool", "act"):
            sems = [self.es.enter_context(nc.semaphore(f"d_{q}{j}")) for j in range(self.NDMA)]
            self.dq[q] = {"sems": sems, "vals": [0] * self.NDMA, "next": 0}
        self.semh = dict(self.sem)
        for q, d in self.dq.items():
            for j, s in enumerate(d["sems"]):
                self.semh[("d", q, j)] = s
        self.seen = {e: {} for e in ENGS}
        self.res = {}
        self.nwait = 0
        self.nins = 0

    def _deps(self, e, reads, writes):
        need = {}

        def add(tok):
            if tok is None:
                return
            s, v = tok
            if e == "pe" and s == "pe":
                return
            if need.get(s, 0) < v:
                need[s] = v

        for r in reads:
            st = self.res.get(r)
            if st is not None:
                add(st["w"])
        for w in writes:
            st = self.res.get(w)
            if st is not None:
                add(st["w"])
                for s, v in st["r"].items():
                    add((s, v))
        seen = self.seen[e]
        for s, v in need.items():
            if seen.get(s, 0) < v:
                self.eng[e].wait_ge(self.semh[s], v)
                seen[s] = v
                self.nwait += 1

    def _mark(self, tok, reads, writes):
        s, v = tok
        for r in reads:
            st = self.res.setdefault(r, {"w": None, "r": {}})
            if st["r"].get(s, 0) < v:
                st["r"][s] = v
        for w in writes:
            self.res[w] = {"w": tok, "r": {}}

    def op(self, e, fn, reads=(), writes=()):
        self._deps(e, reads, writes)
        ins = fn(self.eng[e])
        self.cnt[e] += 1
        ins.then_inc(self.sem[e], 1)
        self.nins += 1
        self._mark((e, self.cnt[e]), reads, writes)
        return ins

    def dma(self, q, out, in_, reads=(), writes=(), **kw):
        d = self.dq[q]
        j = d["next"]
        d["next"] = (j + 1) % self.NDMA
        key = ("d", q, j)
        seen = self.seen[q]
        if seen.get(key, 0) < d["vals"][j]:
            self.eng[q].wait_ge(d["sems"][j], d["vals"][j])
            seen[key] = d["vals"][j]
            self.nwait += 1
        self._deps(q, reads, writes)
        ins = self.eng[q].dma_start(out=out, in_=in_, **kw)
        d["vals"][j] += 16
        ins.then_inc(d["sems"][j], 16)
        self.nins += 1
        self._mark((key, d["vals"][j]), reads, writes)
        return ins

    def barrier(self):
        for e in ENGS:
            seen = self.seen[e]
            for s in ENGS:
                if self.cnt[s] == 0:
                    continue
                if seen.get(s, 0) < self.cnt[s]:
                    self.eng[e].wait_ge(self.sem[s], self.cnt[s])
                    seen[s] = self.cnt[s]
            for q, d in self.dq.items():
                for j in range(self.NDMA):
                    key = ("d", q, j)
                    if seen.get(key, 0) < d["vals"][j]:
                        self.eng[e].wait_ge(d["sems"][j], d["vals"][j])
                        seen[key] = d["vals"][j]
        self.res = {}

    def close(self):
        self.es.close()


import os as _os
AB_STOP = int(_os.environ.get('AB_STOP', '0'))
AB_CUT = int(_os.environ.get('AB_CUT', '9'))
AB_SUB = int(_os.environ.get('AB_SUB', '9'))


def make_mixer_ab(nc, S, io, mk, idf, idb, epsb, helpers):
    from contextlib import ExitStack
    op, dma = S.op, S.dma
    Work, load_x, modulate, transpose8, load_mods, out_proj = helpers
    NCH = NTOK // 64

    def phase(l, src, dst, q_tiles):
        i = l // 2
        win = io["ab_w_in"][i].rearrange("(kc p) n -> p kc n", p=128)
        eso = ExitStack()
        sbo, pso = mk(eso)
        AKT = sbo("AKT", [64, 2, NTOK], BF16)
        AV = sbo("AV", [128, NT, 2, 128], BF16)
        op("pool", lambda e: e.memset(AV[:, :, :, 64:128], 1.0), [], ["AVones"])
        esm = ExitStack()
        sbm, psm = mk(esm)
        hTall = sbm("hTall", [128, 8, NTOK], BF16)
        es = ExitStack()
        sb, ps = mk(es)
        w1 = sb("w1", [128, 8, 1296], BF16)
        dma("pool", w1[:, :, 0:768], win[:, :, 0:768], writes=["w1a"])
        dma("pool", w1[:, :, 768:1296], win[:, :, 2304:2832], writes=["w1b"])
        mods = load_mods(sb, l, 1, ("sh", "sc"))
        W = Work(sb, ps, nxt=2, npy=0, ntmp=1, nhb=2, nz=0)
        dtb = sb("dtb", [128, 8], F32)
        nea = sb("nea", [128, 8], F32)
        one1 = sb("one1", [128, 1], F32)
        op("dve", lambda e: e.memset(one1[:], 1.0), [], ["one1"])
        dma("sp", dtb[:], io["ab_dt_bias"][i:i + 1].rearrange("o a b -> o (a b)").broadcast_to([128, 8]), writes=["dtb"])
        dma("sp", nea[:], io["ab_a_log"][i:i + 1].rearrange("o a b -> o (a b)").broadcast_to([128, 8]), writes=["nea"])
        op("act", lambda e: e.activation(out=nea[:], in_=nea[:], func=AF.Exp), ["nea"], ["nea"])
        op("dve", lambda e: e.tensor_scalar_mul(out=nea[:], in0=nea[:], scalar1=-1.0), ["nea"], ["nea"])
        qa = RB([sb(f"qa{k}", [128, 640], F32) for k in range(2)], "qa")
        qra = RB([sb(f"qra{k}", [128, 10, 64], BF16) for k in range(2)], "qra")
        csa = RB([sb(f"csa{k}", [128, 2, 32], F32) for k in range(2)], "csa")
        ra = sb("raA", [128, 10, 32], F32)
        rb_ = sb("rbA", [128, 10, 32], F32)
        aqt = RB([sb(f"aqt{k}", [64, 8, 128], BF16) for k in range(2)], "aqt")
        zs = RB([sb(f"zs{k}", [128, 512], F32) for k in range(2)], "zs")
        gbt = RB([sb(f"gbt{k}", [128, 16], F32) for k in range(2)], "gbt")
        gtmp = sb("gtmp", [128, 8], F32)
        pa0 = ps("pa0", [128, 512], F32)
        pa1f = ps("pa1", [128, 512], F32)
        pa1 = pa1f[:, 0:256]
        pz = ps("pz", [128, 512], F32)
        pgtf = ps("pgt", [128, 512], F32)
        pgt = pgtf[:, 0:16]
        ptrA = ps("ptrA", [64, 16, 128], BF16)
        for t in range(NT):
            r = 1 if t < 2 else 0
            xt, xk = load_x(W, src, t)
            hb, hbk = modulate(W, xt, xk, mods, r)
            transpose8(W, hb, hbk, hTall[:, :, t * 128:(t + 1) * 128], ("hT", t))
            h = hTall[:, :, t * 128:(t + 1) * 128]
            for (pp, pk, c0, c1, wk) in ((pa0, "pa0", 0, 512, "w1a"), (pa1, "pa1", 512, 768, "w1a"), (pz, "pz", 768, 1280, "w1b"), (pgt, "pgt", 1280, 1296, "w1b")):
                for kc in range(8):
                    op("pe", lambda e: e.matmul(pp[:], lhsT=h[:, kc, :], rhs=w1[:, kc, c0:c1], start=(kc == 0), stop=(kc == 7)), [("hT", t), wk], [pk])
            if AB_CUT < 2:
                continue
            q, qk = qa.next()
            op("act", lambda e: e.copy(out=q[:, 0:512], in_=pa0[:]), ["pa0"], [qk])
            op("act", lambda e: e.copy(out=q[:, 512:640], in_=pa1[:, 0:128]), ["pa1"], [qk])
            op("act", lambda e: e.copy(out=AV[:, t, :, 0:64], in_=pa1[:, 128:256].rearrange("p (g d) -> p g d", g=2)), ["pa1"], [("AV", t)])
            if AB_SUB < 2:
                continue
            qo, qok = qra.next()
            q3 = q[:].rearrange("p (h d) -> p h d", d=64)
            if r == 1:
                op("dve", lambda e: e.tensor_copy(out=qo[:], in_=q3), [qk], [qok])
            else:
                c, ck = csa.next()
                p0 = (t - 2) * 128
                dma("sp", c[:, 0, :], io["cosA"][p0:p0 + 128, :], writes=[ck])
                dma("sp", c[:, 1, :], io["sinA"][p0:p0 + 128, :], writes=[ck])
                q4 = q[:].rearrange("p (h two d) -> p h two d", two=2, d=32)
                o4 = qo[:].rearrange("p h (two d) -> p h two d", two=2)
                cosb = c[:, 0, :].unsqueeze(1).to_broadcast([128, 10, 32])
                sinb = c[:, 1, :].unsqueeze(1).to_broadcast([128, 10, 32])
                x1, x2 = q4[:, :, 0, :], q4[:, :, 1, :]
                op("dve", lambda e: e.tensor_tensor(out=ra[:], in0=x1, in1=cosb, op=ALU.mult), [qk, ck], ["raA"])
                op("dve", lambda e: e.tensor_tensor(out=rb_[:], in0=x2, in1=sinb, op=ALU.mult), [qk, ck], ["rbA"])
                op("dve", lambda e: e.tensor_tensor(out=o4[:, :, 0, :], in0=ra[:], in1=rb_[:], op=ALU.subtract), ["raA", "rbA"], [qok])
                op("dve", lambda e: e.tensor_tensor(out=ra[:], in0=x1, in1=sinb, op=ALU.mult), [qk, ck], ["raA"])
                op("dve", lambda e: e.tensor_tensor(out=rb_[:], in0=x2, in1=cosb, op=ALU.mult), [qk, ck], ["rbA"])
                op("dve", lambda e: e.tensor_tensor(out=o4[:, :, 1, :], in0=ra[:], in1=rb_[:], op=ALU.add), ["raA", "rbA"], [qok])
            if AB_SUB < 3:
                continue
            for hh in range(10):
                op("pe", lambda e: e.transpose(out=ptrA[:, hh, :], in_=qo[:, hh, :], identity=idb[:]), [qok, "idb"], ["ptrA"])
            a, ak = aqt.next()
            op("act", lambda e: e.copy(out=a[:], in_=ptrA[:, 0:8, :]), ["ptrA"], [ak])
            dma("sp", io["AQT"][t], a[:], reads=[ak], writes=[("AQT", t)])
            op("dve", lambda e: e.tensor_copy(out=AKT[:, :, t * 128:(t + 1) * 128], in_=ptrA[:, 8:10, :]), ["ptrA"], [("AKT", t)])
            if AB_CUT < 3:
                continue
            z, zk = zs.next()
            op("act", lambda e: e.activation(out=z[:], in_=pz[:], func=AF.Silu), ["pz"], [zk])
            dma("sp", io["BZ"][t * 128:(t + 1) * 128, :], z[:], reads=[zk], writes=[("BZ", t)])
            if AB_CUT < 4:
                continue
            gb, gbk = gbt.next()
            op("act", lambda e: e.activation(out=gb[:, 8:16], in_=pgt[:, 0:8], func=AF.Sigmoid), ["pgt"], [gbk])
            op("dve", lambda e: e.tensor_tensor(out=gtmp[:], in0=pgt[:, 8:16], in1=dtb[:], op=ALU.add), ["pgt", "dtb"], ["gtmp"])
            op("act", lambda e: e.activation(out=gtmp[:], in_=gtmp[:], func=AF.Exp), ["gtmp"], ["gtmp"])
            op("act", lambda e: e.activation(out=gtmp[:], in_=gtmp[:], func=AF.Ln, bias=one1[:, 0:1], scale=1.0), ["gtmp", "one1"], ["gtmp"])
            op("dve", lambda e: e.tensor_tensor(out=gb[:, 0:8], in0=gtmp[:], in1=nea[:], op=ALU.mult), ["gtmp", "nea"], [gbk])
            dma("sp", io["GB"][t * 128:(t + 1) * 128, :], gb[:], reads=[gbk], writes=[("GB", t)])
        S.barrier()
        es.close()
        if AB_STOP == 1:
            esm.close(); eso.close(); return
        es = ExitStack()
        sb, ps = mk(es)
        w2 = sb("w2", [128, 8, 1536], BF16)
        dma("pool", w2[:], win[:, :, 768:2304], writes=["w2"])
        cw = sb("cw", [128, 12, 5], F32)
        cwv = io["ab_conv_w"][i].rearrange("k (cc p) -> p cc k", p=128)
        for cc in range(12):
            dma("sp", cw[:, cc, :], cwv[:, cc, :], writes=["cw"], allow_slow_non_contiguous=True)
        onesb = sb("onesb", [128, 128], BF16)
        op("dve", lambda e: e.memset(onesb[:], 1.0), [], ["onesb"])
        RAWW = 4360
        raw = sb("raw", [128, RAWW], F32)
        acc = sb("acc", [128, RAWW], F32)
        sqb = sb("sqb", [128, RAWW], BF16)
        op("pool", lambda e: e.memset(raw[:], 0.0), [], ["raw"])
        fm = RB([sb(f"fm{k}", [128, 512], F32) for k in range(2)], "fm")
        rn = RB([sb(f"rn{k}", [128, 512], F32) for k in range(2)], "rn")
        tk = RB([sb(f"tk{k}", [128, 4, 128], F32) for k in range(2)], "tk")
        pc = RB([ps(f"pc{k}", [128, 512], F32) for k in range(2)], "pc")
        pn = RB([ps(f"pn{k}", [128, 512], F32) for k in range(2)], "pn")
        ptk = RB([ps(f"ptk{k}", [128, 4, 128], F32) for k in range(2)], "ptk")
        blocks = [(0, 256, 2)] + [(256 + b * 512, 512, 256 + b * 512 + 6) for b in range(8)]
        for cc in range(12):
            kind, hd = ("q", "k", "v")[cc // 4], cc % 4
            for (tok0, n, rc) in blocks:
                p, pk = pc.next()
                for kc in range(8):
                    op("pe", lambda e: e.matmul(p[:, :n], lhsT=w2[:, kc, cc * 128:(cc + 1) * 128], rhs=hTall[:, kc, tok0:tok0 + n], start=(kc == 0), stop=(kc == 7)),
                       ["w2"], [pk])
                op("act", lambda e: e.copy(out=raw[:, rc:rc + n], in_=p[:, :n]), [pk], ["raw"])
            lo, hi = 2, RAWW - 2
            op("dve", lambda e: e.tensor_scalar_mul(out=acc[:, lo:hi], in0=raw[:, lo - 2:hi - 2], scalar1=cw[:, cc, 0:1]), ["raw", "cw"], ["acc"])
            for k in range(1, 5):
                op("dve", lambda e: e.scalar_tensor_tensor(out=acc[:, lo:hi], in0=raw[:, lo + k - 2:hi + k - 2], scalar=cw[:, cc, k:k + 1], in1=acc[:, lo:hi],
                                                            op0=ALU.mult, op1=ALU.add), ["raw", "cw", "acc"], ["acc"])
            op("act", lambda e: e.activation(out=acc[:, lo:hi], in_=acc[:, lo:hi], func=AF.Silu), ["acc"], ["acc"])
            if kind != "v":
                op("act", lambda e: e.activation(out=sqb[:, lo:hi], in_=acc[:, lo:hi], func=AF.Square), ["acc"], ["sqb"])
            for (tok0, n, rc) in blocks:
                if kind != "v":
                    p, pk = pn.next()
                    op("pe", lambda e: e.matmul(p[:, :n], lhsT=onesb[:], rhs=sqb[:, rc:rc + n], start=True, stop=True), ["onesb", "sqb"], [pk])
                    r_, rk = rn.next()
                    op("act", lambda e: e.activation(out=r_[:, :n], in_=p[:, :n], func=AF.Sqrt, bias=epsb[:, 0:1], scale=1.0), [pk, "epsb"], [rk])
                    op("dve", lambda e: e.reciprocal(out=r_[:, :n], in_=r_[:, :n]), [rk], [rk])
                    f, fk = fm.next()
                    sc_ = float(128 ** -0.5) if kind == "q" else 1.0
                    op("dve", lambda e: e.scalar_tensor_tensor(out=f[:, :n], in0=acc[:, rc:rc + n], scalar=sc_, in1=r_[:, :n], op0=ALU.mult, op1=ALU.mult),
                       ["acc", rk], [fk])
                    dst_fm = io["BQT"] if kind == "q" else io["BKT"]
                    dma("sp", dst_fm[hd, :, tok0:tok0 + n], f[:, :n], reads=[fk], writes=[("BFM", cc)])
                    srcT, srck = f, fk
                    off = 0
                else:
                    srcT, srck = acc, "acc"
                    off = rc
                if kind != "q":
                    nt_ = n // 128
                    pt_, ptk_ = ptk.next()
                    for j in range(nt_):
                        op("pe", lambda e: e.transpose(out=pt_[:, j, :], in_=srcT[:, off + j * 128:off + (j + 1) * 128], identity=idf[:]), [srck, "idf"], [ptk_])
                    tt, ttk = tk.next()
                    op("dve" if kind == "k" else "act", (lambda e: e.tensor_copy(out=tt[:, :nt_, :], in_=pt_[:, :nt_, :])) if kind == "k" else
                       (lambda e: e.copy(out=tt[:, :nt_, :], in_=pt_[:, :nt_, :])), [ptk_], [ttk])
                    kvi = 0 if kind == "k" else 1
                    dma("sp", io["BKV"][tok0:tok0 + n, kvi, hd * 128:(hd + 1) * 128].rearrange("(j p) d -> p j d", p=128), tt[:, :nt_, :],
                        reads=[ttk], writes=[("BKVw", cc)])
        S.barrier()
        es.close()
        esm.close()
        if AB_STOP == 2:
            eso.close(); return
        es = ExitStack()
        sb, ps = mk(es)
        mk32 = sb("mk32", [128, 2, 128], F32)
        mkb = sb("mkb", [128, 2, 128], BF16)
        dma("sp", mk32[:], io["cA"][:, :, :], writes=["mk32"])
        op("dve", lambda e: e.tensor_copy(out=mkb[:], in_=mk32[:]), ["mk32"], ["mkb"])
        mk4 = sb("mk4", [128, 2, 4, 128], BF16)
        for mi_ in range(2):
            for hh_ in range(4):
                op("dve", lambda e: e.tensor_copy(out=mk4[:, mi_, hh_, :], in_=mk32[:, mi_, :]), ["mk32"], ["mk4"])
        esink = sb("esink", [128, 8], F32)
        dma("sp", esink[:], io["ab_sink"][i:i + 1, :].broadcast_to([128, 8]), writes=["esink"])
        op("act", lambda e: e.activation(out=esink[:], in_=esink[:], func=AF.Exp), ["esink"], ["esink"])
        aq = RB([sb(f"aq{k}", [64, 8, 128], BF16) for k in range(2)], "aq")
        PT = RB([sb(f"PTa{k}", [128, 4, 128], BF16) for k in range(3)], "PTa")
        oa = RB([sb(f"oa{k}", [128, 8, 64], BF16) for k in range(2)], "oa")
        den = sb("den", [128, 4], F32)
        pS = RB([ps(f"pSa{k}", [128, 512], F32) for k in range(2)], "pSa")
        pO = [ps(f"pOa{k}", [128, 4, 128], F32) for k in range(2)]
        for t in q_tiles:
            a, ak = aq.next()
            dma("sp", a[:], io["AQT"][t], reads=[("AQT", t)], writes=[ak])
            if t < 2:
                kl = [(0, None), (1, None)]
            else:
                kl = []
                if t - 1 >= 2:
                    kl.append((t - 1, 0))
                kl.append((t, None))
                if t + 1 < NT:
                    kl.append((t + 1, 1))
                kl += [(0, None), (1, None)]
            o, ok = oa.next()
            for g in range(2):
                for ki, (kt, mi) in enumerate(kl):
                    p, pk = pS.next()
                    op("pe", lambda e: e.matmul(p[:], lhsT=AKT[:, g, kt * 128:(kt + 1) * 128], rhs=a[:, 4 * g:4 * g + 4, :], start=True, stop=True),
                       [("AKT", kt), ak], [pk])
                    pt, ptk_ = PT.next()
                    op("act", lambda e: e.activation(out=pt[:].rearrange("p h q -> p (h q)"), in_=p[:], func=AF.Exp, scale=0.125), [pk], [ptk_])
                    if mi is not None:
                        op("dve", lambda e: e.tensor_tensor(out=pt[:], in0=pt[:], in1=mk4[:, mi, :, :], op=ALU.mult), [ptk_, "mk4"], [ptk_])
                    for hh in range(4):
                        op("pe", lambda e: e.matmul(pO[g][:, hh, 0:65], lhsT=pt[:, hh, :], rhs=AV[:, kt, g, 0:65], start=(ki == 0 and hh == 0), stop=(ki == len(kl) - 1), skip_group_check=True),
                           [ptk_, ("AV", kt), "AVones"], [f"pOa{g}"])
                op("dve", lambda e: e.tensor_tensor(out=den[:], in0=pO[g][:, :, 64], in1=esink[:, 4 * g:4 * g + 4], op=ALU.add), [f"pOa{g}", "esink"], ["den"])
                op("dve", lambda e: e.reciprocal(out=den[:], in_=den[:]), ["den"], ["den"])
                op("dve", lambda e: e.tensor_tensor(out=o[:, 4 * g:4 * g + 4, :], in0=pO[g][:, :, 0:64], in1=den[:, :].unsqueeze(2).to_broadcast([128, 4, 64]), op=ALU.mult),
                   [f"pOa{g}", "den"], [ok])
            dma("sp", io["OA"][t * 128:(t + 1) * 128, :], o[:].rearrange("p h d -> p (h d)"), reads=[ok], writes=[("OA", t)])
        S.barrier()
        es.close()
        eso.close()
        if AB_STOP == 3:
            return
        es = ExitStack()
        sb, ps = mk(es)
        cB = sb("cB", [64, 4, 64], F32)
        dma("sp", cB[:], io["cB"][:, :, :], writes=["cB"])
        ones64 = sb("ones64", [64, 128], F32)
        op("dve", lambda e: e.memset(ones64[:], 1.0), [], ["ones64"])
        Sst = [sb(f"Sst{d}", [128, 4, 128], F32) for d in range(2)]
        for d in range(2):
            op("pool", lambda e: e.memset(Sst[d][:], 0.0), [], [f"S{d}"])
        NB = 6
        def rb(name, shape, n=NB):
            return RB([sb(f"{name}{k}", shape, F32) for k in range(n)], name)
        kT4 = rb("kT4", [128, 4, 64]); qT4 = rb("qT4", [128, 4, 64]); kv = rb("kvB", [64, 2, 512]); gbb = rb("gbB", [64, 16])
        Rm = rb("Rm", [64, 4, 64]); gcs = rb("gcs", [64, 4]); gls = rb("gls", [128, 4]); egc = rb("egc", [64, 4]); ekt = rb("ekt", [64, 4]); egl = rb("egl", [128, 4])
        Dm = rb("Dm", [64, 4, 64]); A0 = rb("A0", [64, 4, 64]); AT = rb("AT", [64, 4, 64]); Q = rb("Qm", [64, 4, 64]); Qfin = rb("Qfin", [64, 4, 64]); attT = rb("attT", [64, 4, 64])
        tmpv = rb("tmpv", [64, 4, 128]); rr = rb("rr", [64, 4, 128]); vnew = rb("vnew", [64, 4, 128]); ktok = rb("ktok", [64, 4, 128]); ob = rb("obB", [64, 4, 128])
        pb = RB([ps(f"pb{k}", [128, 512], F32) for k in range(8)], "pb")

        def P64(p):
            return p[0:64, 0:256].rearrange("p (h i) -> p h i", h=4)

        def P64w(p):
            return p[0:64, :].rearrange("p (h i) -> p h i", h=4)

        def prep(c, d, R_):
            tri = cB[:, d, :]
            nstr = cB[:, 2 + d, :]
            k4, k4k = kT4.next(); q4, q4k = qT4.next(); kvt, kvk = kv.next(); g_, gk = gbb.next()
            c0 = c * 64
            dma("sp", k4[:], io["BKT"][:, :, c0:c0 + 64].rearrange("h d t -> d h t"), writes=[k4k])
            dma("sp", q4[:], io["BQT"][:, :, c0:c0 + 64].rearrange("h d t -> d h t"), writes=[q4k])
            dma("sp", kvt[:], io["BKV"][c0:c0 + 64, :, :], writes=[kvk])
            dma("sp", g_[:], io["GB"][c0:c0 + 64, :], writes=[gk])
            g4 = g_[:, 4 * d:4 * d + 4]
            b4 = g_[:, 8 + 4 * d:8 + 4 * d + 4]
            R, Rk = Rm.next()
            op("dve", lambda e: e.tensor_tensor(out=R[:], in0=tri.unsqueeze(1).to_broadcast([64, 4, 64]), in1=g4.unsqueeze(2).to_broadcast([64, 4, 64]), op=ALU.mult),
               ["cB", gk], [Rk])
            pgr, pgrk = pb.next()
            op("pe", lambda e: e.matmul(pgr[0:64, 0:256], lhsT=ones64[:, 0:64], rhs=R[:].rearrange("p h i -> p (h i)"), start=True, stop=True), ["ones64", Rk], [pgrk])
            psm, psmk = pb.next()
            op("pe", lambda e: e.matmul(psm[0:64, 0:4], lhsT=tri, rhs=g4, start=True, stop=True), ["cB", gk], [psmk])
            op("pe", lambda e: e.matmul(psm[:, 4:8], lhsT=ones64[:, :], rhs=g4, start=True, stop=True), ["ones64", gk], [psmk])
            gc, gck = gcs.next(); gl, glk = gls.next(); eg, egk = egc.next(); ek, ekk = ekt.next(); el, elk = egl.next()
            op("dve", lambda e: e.tensor_copy(out=gc[:], in_=psm[0:64, 0:4]), [psmk], [gck])
            op("dve", lambda e: e.tensor_copy(out=gl[:], in_=psm[:, 4:8]), [psmk], [glk])
            op("act", lambda e: e.activation(out=eg[:], in_=gc[:], func=AF.Exp), [gck], [egk])
            op("dve", lambda e: e.tensor_tensor(out=ek[:], in0=gl[0:64, :], in1=gc[:], op=ALU.subtract), [glk, gck], [ekk])
            op("act", lambda e: e.activation(out=ek[:], in_=ek[:], func=AF.Exp), [ekk], [ekk])
            op("act", lambda e: e.activation(out=el[:], in_=gl[:], func=AF.Exp), [glk], [elk])
            D_, Dk = Dm.next()
            op("dve", lambda e: e.tensor_tensor(out=D_[:], in0=P64(pgr), in1=gc[:, :].unsqueeze(2).to_broadcast([64, 4, 64]), op=ALU.subtract), [pgrk, gck], [Dk])
            op("dve", lambda e: e.tensor_scalar_min(out=D_[:], in0=D_[:], scalar1=0.0), [Dk], [Dk])
            op("act", lambda e: e.activation(out=D_[:], in_=D_[:], func=AF.Exp), [Dk], [Dk])
            op("dve", lambda e: e.tensor_tensor(out=D_[:], in0=D_[:], in1=tri.unsqueeze(1).to_broadcast([64, 4, 64]), op=ALU.mult), [Dk, "cB"], [Dk])
            yield
            pkk, pkkk = pb.next()
            pqk, pqkk = pb.next()
            for h in range(4):
                op("pe", lambda e: e.matmul(pkk[0:64, h * 64:(h + 1) * 64], lhsT=k4[:, h, :], rhs=k4[:, h, :], start=True, stop=True), [k4k], [pkkk])
            for h in range(4):
                op("pe", lambda e: e.matmul(pqk[0:64, h * 64:(h + 1) * 64], lhsT=k4[:, h, :], rhs=q4[:, h, :], start=True, stop=True), [k4k, q4k], [pqkk])
            a0, a0k = A0.next()
            op("dve", lambda e: e.tensor_tensor(out=a0[:], in0=P64(pkk), in1=D_[:], op=ALU.mult), [pkkk, Dk], [a0k])
            op("dve", lambda e: e.tensor_tensor(out=a0[:], in0=a0[:], in1=nstr.unsqueeze(1).to_broadcast([64, 4, 64]), op=ALU.mult), [a0k, "cB"], [a0k])
            op("dve", lambda e: e.tensor_tensor(out=a0[:], in0=a0[:], in1=b4.unsqueeze(2).to_broadcast([64, 4, 64]), op=ALU.mult), [a0k, gk], [a0k])
            at_, atk = attT.next()
            op("dve", lambda e: e.tensor_tensor(out=at_[:], in0=P64(pqk), in1=D_[:], op=ALU.mult), [pqkk, Dk], [atk])
            ptt, pttk = pb.next()
            for h in range(4):
                op("pe", lambda e: e.transpose(out=ptt[0:64, h * 64:(h + 1) * 64], in_=a0[:, h, :], identity=idf[0:64, 0:64]), [a0k, "idf"], [pttk])
            aT_, aTk = AT.next()
            op("act", lambda e: e.copy(out=aT_[:], in_=P64(ptt)), [pttk], [aTk])
            q_, qk_ = Q.next()
            op("dve", lambda e: e.tensor_tensor(out=q_[:], in0=a0[:], in1=idf[0:64, 0:64].unsqueeze(1).to_broadcast([64, 4, 64]), op=ALU.add), [a0k, "idf"], [qk_])
            yield
            am, amk, amT, amTk = a0, a0k, aT_, aTk
            for m in range(1, 6):
                pAT, pATk = pb.next()
                for h in range(4):
                    op("pe", lambda e: e.matmul(pAT[0:64, h * 64:(h + 1) * 64], lhsT=am[:, h, :], rhs=amT[:, h, :], start=True, stop=True), [amk, amTk], [pATk])
                if m < 5:
                    pA, pAk = pb.next()
                    for h in range(4):
                        op("pe", lambda e: e.matmul(pA[0:64, h * 64:(h + 1) * 64], lhsT=amT[:, h, :], rhs=am[:, h, :], start=True, stop=True), [amk, amTk], [pAk])
                nT, nTk = AT.next()
                op("act", lambda e: e.copy(out=nT[:], in_=P64(pAT)), [pATk], [nTk])
                if m < 5:
                    nA, nAk = A0.next()
                    op("dve", lambda e: e.tensor_copy(out=nA[:], in_=P64(pA)), [pAk], [nAk])
                else:
                    nA, nAk = None, None
                pQ, pQk = pb.next()
                for h in range(4):
                    op("pe", lambda e: e.matmul(pQ[0:64, h * 64:(h + 1) * 64], lhsT=nT[:, h, :], rhs=q_[:, h, :], start=True, stop=True), [nTk, qk_], [pQk])
                nq, nqk = (Q.next() if m < 5 else Qfin.next())
                op("dve", lambda e: e.tensor_tensor(out=nq[:], in0=P64(pQ), in1=q_[:], op=ALU.add), [pQk, qk_], [nqk])
                q_, qk_ = nq, nqk
                am, amk, amT, amTk = nA, nAk, nT, nTk
                yield
            R_.update(dict(k4=k4, k4k=k4k, q4=q4, q4k=q4k, kvt=kvt, kvk=kvk, gk=gk, b4=b4, eg=eg, egk=egk, ek=ek, ekk=ekk, el=el, elk=elk, at_=at_, atk=atk, q_=q_, qk_=qk_))

        def scan(R_, c, d):
            k4, k4k, q4, q4k, kvt, kvk, gk, b4 = R_["k4"], R_["k4k"], R_["q4"], R_["q4k"], R_["kvt"], R_["kvk"], R_["gk"], R_["b4"]
            eg, egk, ek, ekk, el, elk, at_, atk, q_, qk_ = R_["eg"], R_["egk"], R_["ek"], R_["ekk"], R_["el"], R_["elk"], R_["at_"], R_["atk"], R_["q_"], R_["qk_"]
            c0 = c * 64
            Sd, Sk = Sst[d], f"S{d}"
            pks, pksk = pb.next()
            for h in range(4):
                op("pe", lambda e: e.matmul(pks[0:64, h * 128:(h + 1) * 128], lhsT=k4[:, h, :], rhs=Sd[:, h, :], start=True, stop=True), [k4k, Sk], [pksk])
            pqs, pqsk = pb.next()
            for h in range(4):
                op("pe", lambda e: e.matmul(pqs[0:64, h * 128:(h + 1) * 128], lhsT=q4[:, h, :], rhs=Sd[:, h, :], start=True, stop=True), [q4k, Sk], [pqsk])
            tv, tvk = tmpv.next()
            op("dve", lambda e: e.tensor_tensor(out=tv[:], in0=P64w(pks), in1=eg[:, :].unsqueeze(2).to_broadcast([64, 4, 128]), op=ALU.mult), [pksk, egk], [tvk])
            r_, rk = rr.next()
            v4 = kvt[:, 1, :].rearrange("p (h d) -> p h d", h=4)
            kk4 = kvt[:, 0, :].rearrange("p (h d) -> p h d", h=4)
            op("dve", lambda e: e.tensor_tensor(out=r_[:], in0=v4, in1=tv[:], op=ALU.subtract), [kvk, tvk], [rk])
            o_, ok_ = ob.next()
            op("dve", lambda e: e.tensor_tensor(out=o_[:], in0=P64w(pqs), in1=eg[:, :].unsqueeze(2).to_broadcast([64, 4, 128]), op=ALU.mult), [pqsk, egk], [ok_])
            kt_, ktk = ktok.next()
            op("dve", lambda e: e.tensor_tensor(out=kt_[:], in0=kk4, in1=ek[:, :].unsqueeze(2).to_broadcast([64, 4, 128]), op=ALU.mult), [kvk, ekk], [ktk])
            yield
            pv, pvk = pb.next()
            for h in range(4):
                op("pe", lambda e: e.matmul(pv[0:64, h * 128:(h + 1) * 128], lhsT=q_[:, h, :], rhs=r_[:, h, :], start=True, stop=True), [qk_, rk], [pvk])
            vn, vnk = vnew.next()
            op("dve", lambda e: e.tensor_tensor(out=vn[:], in0=P64w(pv), in1=b4.unsqueeze(2).to_broadcast([64, 4, 128]), op=ALU.mult), [pvk, gk], [vnk])
            yield
            pav, pavk = pb.next()
            for h in range(4):
                op("pe", lambda e: e.matmul(pav[0:64, h * 128:(h + 1) * 128], lhsT=at_[:, h, :], rhs=vn[:, h, :], start=True, stop=True), [atk, vnk], [pavk])
            psn, psnk = pb.next()
            for h in range(4):
                op("pe", lambda e: e.matmul(psn[:, h * 128:(h + 1) * 128], lhsT=kt_[:, h, :], rhs=vn[:, h, :], start=True, stop=True), [ktk, vnk], [psnk])
            op("dve", lambda e: e.tensor_tensor(out=o_[:], in0=P64w(pav), in1=o_[:], op=ALU.add), [pavk, ok_], [ok_])
            dma("pool", io["OB"][d, c0:c0 + 64, :], o_[:].rearrange("p h d -> p (h d)"), reads=[ok_], writes=[("OB", d, c)])
            op("dve", lambda e: e.tensor_tensor(out=Sd[:], in0=Sd[:], in1=el[:, :].unsqueeze(2).to_broadcast([128, 4, 128]), op=ALU.mult), [Sk, elk], [Sk])
            op("dve", lambda e: e.tensor_tensor(out=Sd[:], in0=psn[:, :].rearrange("p (h d) -> p h d", h=4), in1=Sd[:], op=ALU.add), [psnk, Sk], [Sk])
            yield

        def lockstep(gens):
            live = list(gens)
            while live:
                nxt = []
                for g_ in live:
                    try:
                        next(g_)
                        nxt.append(g_)
                    except StopIteration:
                        pass
                live = nxt

        order_f = list(range(0, 4)) + list(range(4, NCH))
        order_b = list(range(3, -1, -1)) + list(range(NCH - 1, 3, -1))
        PR = {}
        PR[(0, 0)] = {}; PR[(0, 1)] = {}
        lockstep([prep(order_f[0], 0, PR[(0, 0)]), prep(order_b[0], 1, PR[(0, 1)])])
        for s_ in range(NCH):
            gens = [scan(PR[(s_, 0)], order_f[s_], 0), scan(PR[(s_, 1)], order_b[s_], 1)]
            if s_ + 1 < NCH:
                PR[(s_ + 1, 0)] = {}; PR[(s_ + 1, 1)] = {}
                gens += [prep(order_f[s_ + 1], 0, PR[(s_ + 1, 0)]), prep(order_b[s_ + 1], 1, PR[(s_ + 1, 1)])]
            lockstep(gens)
            PR.pop((s_, 0)); PR.pop((s_, 1))
        S.barrier()
        es.close()
        if AB_STOP == 4:
            return
        es = ExitStack()
        sb, ps = mk(es)
        wout = sb("woutab", [128, 8, 1024], BF16)
        dma("pool", wout[:], io["ab_w_out"][i].rearrange("(kc p) n -> p kc n", p=128), writes=["woutab"])
        mods = load_mods(sb, l, 1, ("gt", "lng", "lnb"))
        W = Work(sb, ps, nxt=3, npy=2, with_pT=True, ntmp=0, nhb=0, nz=2)
        gn = sb("gn", [128, 128], F32)
        dma("sp", gn[:], io["ab_gnorm"][i:i + 1, :].broadcast_to([128, 128]), writes=["gn"])
        of = RB([sb(f"of{k}", [128, 512], F32) for k in range(2)], "of")
        obk = RB([sb(f"obk{k}", [128, 512], F32) for k in range(2)], "obk")
        zz = RB([sb(f"zz{k}", [128, 512], F32) for k in range(2)], "zz")
        sq5 = sb("sq5", [128, 512], F32)
        ss5 = sb("ss5", [128, 4], F32)
        oc = RB([sb(f"oc{k}", [128, 1024], BF16) for k in range(2)], "oc")
        oT = RB([sb(f"oT5{k}", [128, 8, 128], BF16) for k in range(2)], "oT5")
        for t in q_tiles:
            r = 1 if t < 2 else 0
            rows = slice(t * 128, (t + 1) * 128)
            f_, fk = of.next(); b_, bk = obk.next(); z_, zk = zz.next(); o_, ok_ = oc.next()
            dma("sp", f_[:], io["OB"][0, rows, :], writes=[fk])
            dma("sp", b_[:], io["OB"][1, rows, :], writes=[bk])
            dma("sp", z_[:], io["BZ"][rows, :], writes=[zk])
            dma("sp", o_[:, 0:512], io["OA"][rows, :], writes=[ok_ + "a"])
            op("dve", lambda e: e.tensor_tensor(out=f_[:], in0=f_[:], in1=b_[:], op=ALU.add), [fk, bk], [fk])
            op("act", lambda e: e.activation(out=sq5[:], in_=f_[:], func=AF.Square), [fk], ["sq5"])
            op("dve", lambda e: e.tensor_reduce(out=ss5[:], in_=sq5[:].rearrange("p (h d) -> p h d", d=128), axis=AX.X, op=ALU.add), ["sq5"], ["ss5"])
            op("act", lambda e: e.activation(out=ss5[:], in_=ss5[:], func=AF.Sqrt, bias=epsb[:, 0:1], scale=1.0 / 128), ["ss5", "epsb"], ["ss5"])
            op("dve", lambda e: e.reciprocal(out=ss5[:], in_=ss5[:]), ["ss5"], ["ss5"])
            f3 = f_[:].rearrange("p (h d) -> p h d", d=128)
            op("dve", lambda e: e.tensor_tensor(out=f3, in0=f3, in1=ss5[:, :].unsqueeze(2).to_broadcast([128, 4, 128]), op=ALU.mult), [fk, "ss5"], [fk])
            op("dve", lambda e: e.tensor_tensor(out=f3, in0=f3, in1=gn[:, :].unsqueeze(1).to_broadcast([128, 4, 128]), op=ALU.mult), [fk, "gn"], [fk])
            op("dve", lambda e: e.tensor_tensor(out=o_[:, 512:1024], in0=f_[:], in1=z_[:], op=ALU.mult), [fk, zk], [ok_ + "b"])
            ot, otk = oT.next()
            pT, ptk2 = W.pT.next()
            for kc in range(8):
                op("pe", lambda e: e.transpose(out=pT[:, kc, :], in_=o_[:, kc * 128:(kc + 1) * 128], identity=idb[:]), [ok_ + "a", ok_ + "b", "idb"], [ptk2])
            op("act", lambda e: e.copy(out=ot[:], in_=pT[:]), [ptk2], [otk])
            xt, xk = load_x(W, src, t)
            out_proj(W, ot, otk, wout, "woutab", xt, xk, mods["gt"][r], f"mod_gt{r}", mods, dst, t)
        S.barrier()
        es.close()

    return phase


import numpy as np

D = 1024
SEQ = 4096
CTX = 256
NTOK = SEQ + CTX
NT = NTOK // 128
DFF = 2816
NJ = DFF // 128
DEPTH = 4
ALPHA = (2 * DEPTH) ** 0.25
EPS = 1e-6
AB_IN = 2832


class RB:
    def __init__(self, items, name):
        self.items = items
        self.name = name
        self.i = 0

    def next(self):
        k = self.i
        self.i = (k + 1) % len(self.items)
        return self.items[k], f"{self.name}{k}"


def build(nc, S, io, layers=(0, 1, 2, 3), out_mode="final", skip_ctx_last=True):
    from contextlib import ExitStack
    es0 = ExitStack()

    uid = [0]

    def mk(es):
        def sb(name, shape, dt):
            uid[0] += 1
            return es.enter_context(nc.sbuf_tensor(f"{name}_u{uid[0]}", shape, dt))

        def ps(name, shape, dt):
            uid[0] += 1
            return es.enter_context(nc.psum_tensor(f"{name}_u{uid[0]}", shape, dt))
        return sb, ps

    sb0, ps0 = mk(es0)
    op, dma = S.op, S.dma
    XS = io["XS"]
    MOD = io["MOD"]

    idf = sb0("idf", [128, 128], F32)
    idb = sb0("idb", [128, 128], BF16)
    epsb = sb0("epsb", [128, 1], F32)
    dma("sp", idf[:], io["ident"][:, :], writes=["idf"])
    op("dve", lambda e: e.tensor_copy(out=idb[:], in_=idf[:]), ["idf"], ["idb"])
    op("dve", lambda e: e.memset(epsb[:], EPS), [], ["epsb"])

    def xrows(ap, t):
        return ap[t * 128:(t + 1) * 128, :]

    def phase_mod():
        es = ExitStack()
        sb, ps = mk(es)
        scT = sb("scT", [128, 8, 2], F32)
        aw = [sb(f"aw{i}", [128, 8, 512], F32) for i in range(3)]
        awr = RB(aw, "aw")
        mrow = sb("mrow", [2, 9216], F32)
        brow = sb("brow", [2, 9216], F32)
        pm = [ps(f"pm{i}", [128, 512], F32)[0:2, :] for i in range(2)]
        pmr = RB(pm, "pm")
        dma("sp", scT[:], io["cvecT"][:, :, :], writes=["scT"])
        op("act", lambda e: e.activation(out=scT[:], in_=scT[:], func=AF.Silu), ["scT"], ["scT"])
        for l in layers:
            dma("sp", brow[:], io["ada_b"][l:l + 1, :].broadcast_to([2, 9216]), writes=["brow"])
            wv = io["ada_w"][l].rearrange("(kc p) n -> p kc n", p=128)
            for n in range(18):
                a, ak = awr.next()
                dma("sp" if n % 2 == 0 else "act", a[:], wv[:, :, n * 512:(n + 1) * 512], writes=[ak])
                p, pk = pmr.next()
                for kc in range(8):
                    op("pe", lambda e: e.matmul(p[:], lhsT=scT[:, kc, :], rhs=a[:, kc, :], start=(kc == 0), stop=(kc == 7)),
                       ["scT", ak], [pk])
                op("dve", lambda e: e.tensor_tensor(out=mrow[:, n * 512:(n + 1) * 512], in0=p[:], in1=brow[:, n * 512:(n + 1) * 512], op=ALU.add),
                   [pk, "brow"], ["mrow"])
            for s in range(3):
                c0 = (3 * s + 1) * 1024
                op("dve", lambda e: e.tensor_scalar_add(out=mrow[:, c0:c0 + 1024], in0=mrow[:, c0:c0 + 1024], scalar1=1.0), ["mrow"], ["mrow"])
                if s != 1:
                    c1 = (3 * s + 2) * 1024
                    op("dve", lambda e: e.tensor_scalar_mul(out=mrow[:, c1:c1 + 1024], in0=mrow[:, c1:c1 + 1024], scalar1=0.5), ["mrow"], ["mrow"])
            dma("sp", MOD[l, :, :], mrow[:], reads=["mrow"], writes=["MOD"])
        S.barrier()
        es.close()

    def load_mods(sb, l, s, which):
        out = {}
        for nm in which:
            if nm in ("sh", "sc", "gt"):
                k = {"sh": 0, "sc": 1, "gt": 2}[nm]
                tl = []
                for r in range(2):
                    t = sb(f"mod_{nm}{r}", [128, 1024], F32)
                    dma("sp", t[:], MOD[l, r:r + 1, (3 * s + k) * 1024:(3 * s + k + 1) * 1024].broadcast_to([128, 1024]),
                        reads=["MOD"], writes=[f"mod_{nm}{r}"])
                    tl.append(t)
                out[nm] = tl
            else:
                src = io["ln_g"] if nm == "lng" else io["ln_b"]
                t = sb(f"mod_{nm}", [128, 1024], F32)
                dma("sp", t[:], src[l, s:s + 1, :].broadcast_to([128, 1024]), writes=[f"mod_{nm}"])
                out[nm] = t
        return out

    class Work:
        def __init__(self, sb, ps, nxt=6, npy=2, with_pT=True, ntmp=2, nhb=2, nz=2):
            self.xt = RB([sb(f"xt{i}", [128, 1024], F32) for i in range(nxt)], "xt")
            self.tmp = RB([sb(f"mtmp{i}", [128, 1024], F32) for i in range(ntmp)], "mtmp")
            self.hb = RB([sb(f"hb{i}", [128, 1024], BF16) for i in range(nhb)], "hb")
            self.z = RB([sb(f"z{i}", [128, 1024], F32) for i in range(nz)], "z")
            self.st6 = sb("st6", [128, 2, 6], F32)
            self.mv = sb("mv", [128, 2], F32)
            self.rstd = sb("rstd", [128, 1], F32)
            self.nb = sb("nb", [128, 1], F32)
            if with_pT:
                self.pT = RB([ps(f"pT{i}", [128, 8, 128], BF16) for i in range(2)], "pT")
            self.py = RB([ps(f"py{i}", [128, 512], F32) for i in range(npy)], "py")

    def load_x(W, src, t):
        xt, xk = W.xt.next()
        dma("sp", xt[:], xrows(src, t), reads=[("X", t)], writes=[xk])
        return xt, xk

    def modulate(W, xt, xk, mods, r):
        tmp, tk = W.tmp.next()
        hb, hk = W.hb.next()
        op("pool", lambda e: e.tensor_tensor(out=tmp[:], in0=xt[:], in1=mods["sc"][r][:], op=ALU.mult), [xk, f"mod_sc{r}"], [tk])
        op("pool", lambda e: e.tensor_tensor(out=hb[:], in0=tmp[:], in1=mods["sh"][r][:], op=ALU.add), [tk, f"mod_sh{r}"], [hk])
        return hb, hk

    def transpose8(W, src, sk, dst_ap, dk, eng="act"):
        pT, pk = W.pT.next()
        for kc in range(8):
            op("pe", lambda e: e.transpose(out=pT[:, kc, :], in_=src[:, kc * 128:(kc + 1) * 128], identity=idb[:]), [sk, "idb"], [pk])
        if eng == "act":
            op("act", lambda e: e.copy(out=dst_ap, in_=pT[:]), [pk], [dk])
        else:
            op("dve", lambda e: e.tensor_copy(out=dst_ap, in_=pT[:]), [pk], [dk])

    def cast_wgu(l, j, slot):
        wv = io["ffn_w_gu"][l, j].rearrange("(kc p) n -> p kc n", p=128)
        dst = io["WGUB"][slot]
        for g in range(11):
            for u in range(2):
                dma("pool", dst[g, :, :, u * 256:(u + 1) * 256], wv[:, :, u * DFF + g * 256:u * DFF + (g + 1) * 256],
                    writes=[("WGUB", slot, g)])

    def phase_ffn(l, j, src, dst, tiles, slot, final=False):
        s = 0 if j == 0 else 2
        es = ExitStack()
        sb, ps = mk(es)
        wd = sb("wd", [128, NJ, 1024], BF16)
        wdv = io["ffn_w_down"][l, j].rearrange("(jc p) n -> p jc n", p=128)
        for q in range(2):
            dma("pool", wd[:, q * 11:(q + 1) * 11, :], wdv[:, q * 11:(q + 1) * 11, :], writes=[f"wd{q}"])
        mods = load_mods(sb, l, s, ("sh", "sc", "gt", "lng", "lnb"))
        W = Work(sb, ps, nxt=8, npy=2)
        wg = RB([sb(f"wg{i}", [128, 8, 512], BF16) for i in range(3)], "wg")
        hT = RB([sb(f"hT{i}", [128, 8, 512], BF16) for i in range(2)], "hT")
        aT = sb("aT", [128, NJ, 512], BF16)
        sg = RB([sb(f"sg{i}", [128, 512], F32) for i in range(2)], "sg")
        pg = RB([ps(f"pg{i}", [128, 512], F32) for i in range(2)], "pg")
        pu = RB([ps(f"pu{i}", [128, 512], F32) for i in range(2)], "pu")
        sts = [tiles[i:i + 4] for i in range(0, len(tiles), 4)]

        def prologue(st):
            h, hk = hT.next()
            xs = []
            for ti, t in enumerate(st):
                xt, xk = load_x(W, src, t)
                xs.append((xt, xk))
                r = 1 if t < 2 else 0
                hb, hbk = modulate(W, xt, xk, mods, r)
                transpose8(W, hb, hbk, h[:, :, ti * 128:(ti + 1) * 128], hk)
            return h, hk, xs

        nxt = prologue(sts[0])
        for si, st in enumerate(sts):
            n = 128 * len(st)
            h, hk, xs = nxt
            for g in range(11):
                w, wk = wg.next()
                dma("sp", w[:], io["WGUB"][slot][g], reads=[("WGUB", slot, g)], writes=[wk])
                for c in range(2):
                    jj = 2 * g + c
                    p1, p1k = pg.next()
                    p2, p2k = pu.next()
                    for kc in range(8):
                        op("pe", lambda e: e.matmul(p1[:, :n], lhsT=w[:, kc, c * 128:(c + 1) * 128], rhs=h[:, kc, :n], start=(kc == 0), stop=(kc == 7)),
                           [wk, hk], [p1k])
                    for kc in range(8):
                        op("pe", lambda e: e.matmul(p2[:, :n], lhsT=w[:, kc, 256 + c * 128:256 + (c + 1) * 128], rhs=h[:, kc, :n], start=(kc == 0), stop=(kc == 7)),
                           [wk, hk], [p2k])
                    s1, s1k = sg.next()
                    op("act", lambda e: e.activation(out=s1[:, :n], in_=p1[:, :n], func=AF.Silu), [p1k], [s1k])
                    op("dve", lambda e: e.tensor_tensor(out=aT[:, jj, :n], in0=p2[:, :n], in1=s1[:, :n], op=ALU.mult), [p2k, s1k], [("aT", jj)])
            if si + 1 < len(sts):
                nxt = prologue(sts[si + 1])
            for ti, t in enumerate(st):
                xt, xk = xs[ti]
                r = 1 if t < 2 else 0
                z, zk = W.z.next()
                for nh in range(2):
                    py, pk = W.py.next()
                    for k in range(NJ):
                        op("pe", lambda e: e.matmul(py[:], lhsT=aT[:, k, ti * 128:(ti + 1) * 128], rhs=wd[:, k, nh * 512:(nh + 1) * 512], start=(k == 0), stop=(k == NJ - 1)),
                           [("aT", k), f"wd{k // 11}"], [pk])
                    op("dve", lambda e: e.tensor_tensor(out=z[:, nh * 512:(nh + 1) * 512], in0=py[:], in1=mods["gt"][r][:, nh * 512:(nh + 1) * 512], op=ALU.mult),
                       [pk, f"mod_gt{r}"], [zk])
                ln_store(W, z, zk, xt, xk, mods, dst, t, final)
        S.barrier()
        es.close()

    def ln_store(W, z, zk, xt, xk, mods, dst, t, final):
        op("dve", lambda e: e.scalar_tensor_tensor(out=z[:], in0=xt[:], scalar=float(ALPHA), in1=z[:], op0=ALU.mult, op1=ALU.add), [xk, zk], [zk])
        for c in range(2):
            op("dve", lambda e: e.bn_stats(out=W.st6[:, c, :], in_=z[:, c * 512:(c + 1) * 512]), [zk], ["st6"])
        op("dve", lambda e: e.bn_aggr(out=W.mv[:], in_=W.st6[:]), ["st6"], ["mv"])
        op("act", lambda e: e.activation(out=W.rstd[:], in_=W.mv[:, 1:2], func=AF.Sqrt, bias=epsb[:, 0:1], scale=1.0), ["mv", "epsb"], ["rstd"])
        op("dve", lambda e: e.reciprocal(out=W.rstd[:], in_=W.rstd[:]), ["rstd"], ["rstd"])
        op("dve", lambda e: e.scalar_tensor_tensor(out=W.nb[:], in0=W.mv[:, 0:1], scalar=-1.0, in1=W.rstd[:], op0=ALU.mult, op1=ALU.mult), ["mv", "rstd"], ["nb"])
        op("act", lambda e: e.activation(out=z[:], in_=z[:], func=AF.Identity, bias=W.nb[:, 0:1], scale=W.rstd[:, 0:1]), [zk, "nb", "rstd"], [zk])
        op("pool", lambda e: e.tensor_tensor(out=z[:], in0=z[:], in1=mods["lng"][:], op=ALU.mult), [zk, "mod_lng"], [zk])
        op("pool", lambda e: e.tensor_tensor(out=z[:], in0=z[:], in1=mods["lnb"][:], op=ALU.add), [zk, "mod_lnb"], [zk])
        if final:
            dma("sp", io["out"][(t - 2) * 128:(t - 1) * 128, :], z[:], reads=[zk], writes=[("OUT", t)])
        else:
            dma("sp", xrows(dst, t), z[:], reads=[zk], writes=[("X", t)])

    def out_proj(W, oT, ok, wt, wk, xt, xk, gate, gk, mods, dst, t):
        z, zk = W.z.next()
        for nh in range(2):
            py, pk = W.py.next()
            for k in range(8):
                op("pe", lambda e: e.matmul(py[:], lhsT=oT[:, k, :], rhs=wt[:, k, nh * 512:(nh + 1) * 512], start=(k == 0), stop=(k == 7)),
                   [ok, wk], [pk])
            op("dve", lambda e: e.tensor_tensor(out=z[:, nh * 512:(nh + 1) * 512], in0=py[:], in1=gate[:, nh * 512:(nh + 1) * 512], op=ALU.mult),
               [pk, gk], [zk])
        ln_store(W, z, zk, xt, xk, mods, dst, t, False)

    def phase_mixer_c(l, src, dst, q_tiles):
        i = l // 2
        SC = 128 ** -0.5
        eso = ExitStack()
        sbo, pso = mk(eso)
        QT = sbo("QT", [128, NT, 8, 128], BF16)
        KT = sbo("KT", [128, 2, NTOK], BF16)
        V = sbo("V", [128, NT, 2, 130], BF16)
        op("pool", lambda e: e.memset(V[:, :, :, 128:130], 1.0), [], ["Vones"])
        es = ExitStack()
        sb, ps = mk(es)
        win = sb("win", [128, 8, 1536], BF16)
        dma("pool", win[:], io["c_w_in"][i].rearrange("(kc p) n -> p kc n", p=128), writes=["win"])
        gq = sb("gq", [128, 10, 128], F32)
        for h in range(10):
            srcg = io["c_q_norm"] if h < 8 else io["c_k_norm"]
            dma("sp", gq[:, h, :], srcg[i:i + 1, :].broadcast_to([128, 128]), writes=["gq"])
        mods = load_mods(sb, l, 1, ("sh", "sc"))
        W = Work(sb, ps, nxt=2, npy=1, ntmp=1, nhb=2, nz=0)
        hT = RB([sb(f"hTc{k}", [128, 8, 128], BF16) for k in range(2)], "hTc")
        qkv = RB([sb(f"qkv{k}", [128, 1536], F32) for k in range(2)], "qkv")
        sq = sb("sq", [128, 1280], F32)
        ss = sb("ss", [128, 10], F32)
        cs = RB([sb(f"cs{k}", [128, 2, 64], F32) for k in range(2)], "cs")
        ra = sb("ra", [128, 10, 64], F32)
        rb_ = sb("rb", [128, 10, 64], F32)
        qr = RB([sb(f"qr{k}", [128, 10, 128], BF16) for k in range(2)], "qr")
        pq = [ps(f"pq{k}", [128, 512], F32) for k in range(3)]
        ptr = ps("ptr", [128, 16, 128], BF16)
        for t in range(NT):
            r = 1 if t < 2 else 0
            xt, xk = load_x(W, src, t)
            hb, hbk = modulate(W, xt, xk, mods, r)
            h, hk = hT.next()
            transpose8(W, hb, hbk, h[:], hk)
            qv, qk = qkv.next()
            for n in range(3):
                for kc in range(8):
                    op("pe", lambda e: e.matmul(pq[n][:], lhsT=h[:, kc, :], rhs=win[:, kc, n * 512:(n + 1) * 512], start=(kc == 0), stop=(kc == 7)),
                       [hk, "win"], [f"pq{n}"])
                op("act", lambda e: e.copy(out=qv[:, n * 512:(n + 1) * 512], in_=pq[n][:]), [f"pq{n}"], [qk])
            op("act", lambda e: e.activation(out=sq[:], in_=qv[:, 0:1280], func=AF.Square), [qk], ["sq"])
            op("dve", lambda e: e.tensor_reduce(out=ss[:], in_=sq[:].rearrange("p (h d) -> p h d", d=128), axis=AX.X, op=ALU.add), ["sq"], ["ss"])
            op("act", lambda e: e.activation(out=ss[:], in_=ss[:], func=AF.Sqrt, bias=epsb[:, 0:1], scale=1.0 / 128), ["ss", "epsb"], ["ss"])
            op("dve", lambda e: e.reciprocal(out=ss[:], in_=ss[:]), ["ss"], ["ss"])
            q3 = qv[:, 0:1280].rearrange("p (h d) -> p h d", d=128)
            op("dve", lambda e: e.tensor_tensor(out=q3, in0=q3, in1=ss[:, :].unsqueeze(2).to_broadcast([128, 10, 128]), op=ALU.mult), [qk, "ss"], [qk])
            qo, qok = qr.next()
            if r == 1:
                op("dve", lambda e: e.tensor_tensor(out=qo[:], in0=q3, in1=gq[:], op=ALU.mult), [qk, "gq"], [qok])
            else:
                op("dve", lambda e: e.tensor_tensor(out=q3, in0=q3, in1=gq[:], op=ALU.mult), [qk, "gq"], [qk])
                c, ck = cs.next()
                p0 = (t - 2) * 128
                dma("sp", c[:, 0, :], io["cosC"][p0:p0 + 128, :], writes=[ck])
                dma("sp", c[:, 1, :], io["sinC"][p0:p0 + 128, :], writes=[ck])
                q4 = qv[:, 0:1280].rearrange("p (h two d) -> p h two d", two=2, d=64)
                o4 = qo[:].rearrange("p h (two d) -> p h two d", two=2)
                cosb = c[:, 0, :].unsqueeze(1).to_broadcast([128, 10, 64])
                sinb = c[:, 1, :].unsqueeze(1).to_broadcast([128, 10, 64])
                x1, x2 = q4[:, :, 0, :], q4[:, :, 1, :]
                op("dve", lambda e: e.tensor_tensor(out=ra[:], in0=x1, in1=cosb, op=ALU.mult), [qk, ck], ["ra"])
                op("dve", lambda e: e.tensor_tensor(out=rb_[:], in0=x2, in1=sinb, op=ALU.mult), [qk, ck], ["rb"])
                op("dve", lambda e: e.tensor_tensor(out=o4[:, :, 0, :], in0=ra[:], in1=rb_[:], op=ALU.subtract), ["ra", "rb"], [qok])
                op("dve", lambda e: e.tensor_tensor(out=ra[:], in0=x1, in1=sinb, op=ALU.mult), [qk, ck], ["ra"])
                op("dve", lambda e: e.tensor_tensor(out=rb_[:], in0=x2, in1=cosb, op=ALU.mult), [qk, ck], ["rb"])
                op("dve", lambda e: e.tensor_tensor(out=o4[:, :, 1, :], in0=ra[:], in1=rb_[:], op=ALU.add), ["ra", "rb"], [qok])
            for hh in range(10):
                op("pe", lambda e: e.transpose(out=ptr[:, hh, :], in_=qo[:, hh, :], identity=idb[:]), [qok, "idb"], ["ptr"])
            op("act", lambda e: e.copy(out=QT[:, t, :, :], in_=ptr[:, 0:8, :]), ["ptr"], [("QT", t)])
            op("dve", lambda e: e.tensor_copy(out=KT[:, :, t * 128:(t + 1) * 128], in_=ptr[:, 8:10, :]), ["ptr"], [("KT", t)])
            op("pool", lambda e: e.tensor_copy(out=V[:, t, :, 0:128], in_=qv[:, 1280:1536].rearrange("p (g d) -> p g d", g=2)), [qk], [("V", t)])
        S.barrier()
        es.close()
        es = ExitStack()
        sb, ps = mk(es)
        wout = sb("wout", [128, 8, 1024], BF16)
        dma("pool", wout[:], io["c_w_out"][i].rearrange("(kc p) n -> p kc n", p=128), writes=["wout"])
        mods = load_mods(sb, l, 1, ("gt", "lng", "lnb"))
        W = Work(sb, ps, nxt=3, npy=1, with_pT=False, ntmp=0, nhb=0, nz=2)
        PT = RB([sb(f"PT{k}", [128, 512], BF16) for k in range(3)], "PT")
        oT = RB([sb(f"oT{k}", [128, 8, 128], BF16) for k in range(2)], "oT")
        accs = RB([sb(f"accC{k}", [128, 512], F32) for k in range(2)], "accC")
        rdn = sb("rdn", [128, 512], F32)
        onesf = sb("onesf", [128, 128], F32)
        op("dve", lambda e: e.memset(onesf[:], 1.0), [], ["onesf"])
        pS = RB([ps(f"pS{k}", [128, 512], F32) for k in range(2)], "pS")
        pOT = [ps(f"pOT{k}", [128, 512], F32) for k in range(2)]
        pden = ps("pden", [128, 512], F32)
        for t in q_tiles:
            r = 1 if t < 2 else 0
            kts = [0, 1] if r == 1 else list(range(NT))
            ot, otk = oT.next()
            for g in range(2):
                a_, ak_ = accs.next()
                def issue_s(kt_):
                    p_, pk_ = pS.next()
                    op("pe", lambda e: e.matmul(p_[:], lhsT=KT[:, g, kt_ * 128:(kt_ + 1) * 128], rhs=QT[:, t, 4 * g:4 * g + 4, :], start=True, stop=True),
                       [("KT", kt_), ("QT", t)], [pk_])
                    return p_, pk_
                nxt_s = issue_s(kts[0])
                for ki, kt in enumerate(kts):
                    p, pk = nxt_s
                    if ki + 1 < len(kts):
                        nxt_s = issue_s(kts[ki + 1])
                    pt, ptk = PT.next()
                    op("act", lambda e: e.activation(out=pt[:], in_=p[:], func=AF.Exp, scale=float(SC)), [pk], [ptk])
                    op("pe", lambda e: e.matmul(pOT[g][:], lhsT=V[:, kt, g, 0:128], rhs=pt[:], start=(ki == 0), stop=(ki == len(kts) - 1)),
                       [ptk, ("V", kt)], [f"pOT{g}"])
                    if ki == 0:
                        op("dve", lambda e: e.tensor_copy(out=a_[:], in_=pt[:]), [ptk], [ak_])
                    else:
                        op("dve", lambda e: e.tensor_tensor(out=a_[:], in0=a_[:], in1=pt[:], op=ALU.add), [ptk, ak_], [ak_])
                op("pe", lambda e: e.matmul(pden[:], lhsT=onesf[:], rhs=a_[:], start=True, stop=True), ["onesf", ak_], ["pden"])
                op("dve", lambda e: e.reciprocal(out=rdn[:], in_=pden[:]), ["pden"], ["rdn"])
                op("dve", lambda e: e.tensor_tensor(out=ot[:, 4 * g:4 * g + 4, :], in0=pOT[g][:].rearrange("p (h q) -> p h q", h=4),
                                                    in1=rdn[:].rearrange("p (h q) -> p h q", h=4), op=ALU.mult), [f"pOT{g}", "rdn"], [otk])
            xt, xk = load_x(W, src, t)
            out_proj(W, ot, otk, wout, "wout", xt, xk, mods["gt"][r], f"mod_gt{r}", mods, dst, t)
        S.barrier()
        es.close()
        eso.close()

    io["mixer_ab"] = make_mixer_ab(nc, S, io, mk, idf, idb, epsb, (Work, load_x, modulate, transpose8, load_mods, out_proj))
    phase_mod()
    all_tiles = list(range(NT))
    lat_tiles = list(range(2, NT))
    ffn_list = [(l, j) for l in layers for j in range(2)]
    cast_wgu(ffn_list[0][0], ffn_list[0][1], 0)
    fi = 0
    cur = io["xin"]
    for li, l in enumerate(layers):
        last = (li == len(layers) - 1)
        if fi + 1 < len(ffn_list):
            cast_wgu(ffn_list[fi + 1][0], ffn_list[fi + 1][1], (fi + 1) % 2)
        phase_ffn(l, 0, cur, XS, all_tiles, fi % 2)
        fi += 1
        cur = XS
        qt = lat_tiles if (last and skip_ctx_last) else all_tiles
        if l % 2 == 1:
            phase_mixer_c(l, XS, XS, qt)
        else:
            io["mixer_ab"](l, XS, XS, qt)
        if fi + 1 < len(ffn_list):
            cast_wgu(ffn_list[fi + 1][0], ffn_list[fi + 1][1], (fi + 1) % 2)
        phase_ffn(l, 1, XS, XS, qt, fi % 2, final=last)
        fi += 1
    S.barrier()
    es0.close()


import numpy as np


def _consts():
    c = {}
    c["ident"] = np.eye(128, dtype=np.float32)

    def rope(hd):
        nf = hd // 4
        inv = (10000.0 ** (-np.arange(nf, dtype=np.float32) / nf)).astype(np.float32)
        r, col = np.meshgrid(np.arange(64, dtype=np.float32), np.arange(64, dtype=np.float32), indexing="ij")
        r, col = r.reshape(-1), col.reshape(-1)
        ang = np.concatenate([r[:, None] * inv, col[:, None] * inv], axis=-1).astype(np.float32)
        return np.cos(ang).astype(np.float32), np.sin(ang).astype(np.float32)
    c["cosC"], c["sinC"] = rope(128)
    c["cosA"], c["sinA"] = rope(64)
    j = np.arange(128)[:, None]
    i = np.arange(128)[None, :]
    c["cA"] = np.stack([(j >= i), (j <= i)], axis=1).astype(np.float32)
    j = np.arange(64)[:, None]
    i = np.arange(64)[None, :]
    c["cB"] = np.stack([(j <= i), (j >= i), -1.0 * (i > j), -1.0 * (i < j)], axis=1).astype(np.float32)
    return c


W_NAMES = ["ada_w", "ada_b", "ln_g", "ln_b", "ffn_w_gu", "ffn_w_down", "ab_w_in", "ab_conv_w", "ab_a_log",
           "ab_dt_bias", "ab_gnorm", "ab_sink", "ab_w_out", "c_w_in", "c_q_norm", "c_k_norm", "c_w_out"]


def make_program(shapes, layers=(0, 1, 2, 3), dbg=False):
    nc = bass.Bass("TRN2", target_bir_lowering=False)
    try:
        nc.allow_low_precision("bf16 matmul operands, fp32 accumulation")
    except Exception:
        pass
    io = {}
    io["xin"] = nc.dram_tensor("xin", [NTOK, D], F32, kind="ExternalInput").ap()
    io["cvecT"] = nc.dram_tensor("cvecT", [128, 8, 2], F32, kind="ExternalInput").ap()
    for k in W_NAMES:
        io[k] = nc.dram_tensor(k, list(shapes[k]), F32, kind="ExternalInput").ap()
    for k, v in _consts().items():
        io[k] = nc.dram_tensor(k, list(v.shape), F32, kind="ExternalInput").ap()
    io["out"] = nc.dram_tensor("out", [SEQ, D], F32, kind="ExternalOutput").ap()
    io["XS"] = nc.dram_tensor("XS", [NTOK, D], F32, kind="ExternalOutput" if dbg else "Internal").ap()
    io["MOD"] = nc.dram_tensor("MOD", [4, 2, 9216], F32, kind="ExternalOutput" if dbg else "Internal").ap()
    io["AQT"] = nc.dram_tensor("AQT", [NT, 64, 8, 128], BF16, kind="Internal").ap()
    io["BZ"] = nc.dram_tensor("BZ", [NTOK, 512], F32, kind="Internal").ap()
    io["GB"] = nc.dram_tensor("GB", [NTOK, 16], F32, kind="Internal").ap()
    io["BQT"] = nc.dram_tensor("BQT", [4, 128, NTOK], F32, kind="Internal").ap()
    io["BKT"] = nc.dram_tensor("BKT", [4, 128, NTOK], F32, kind="Internal").ap()
    io["BKV"] = nc.dram_tensor("BKV", [NTOK, 2, 512], F32, kind="Internal").ap()
    io["OA"] = nc.dram_tensor("OA", [NTOK, 512], BF16, kind="ExternalOutput" if dbg else "Internal").ap()
    io["OB"] = nc.dram_tensor("OB", [2, NTOK, 512], F32, kind="ExternalOutput" if dbg else "Internal").ap()
    io["WGUB"] = [nc.dram_tensor(f"WGUB{i}", [11, 128, 8, 512], BF16, kind="Internal").ap() for i in range(2)]
    S = Sched(nc)
    build(nc, S, io, layers=layers)
    return nc, S


def make_in_maps(inputs):
    x = np.asarray(inputs["x"], dtype=np.float32)
    c = np.asarray(inputs["c"], dtype=np.float32)
    ctx = np.asarray(inputs["ctx"], dtype=np.float32)
    c_ctx = np.asarray(inputs["c_ctx"], dtype=np.float32)
    consts = _consts()
    shared = {k: np.ascontiguousarray(np.asarray(inputs[k], dtype=np.float32)) for k in W_NAMES}
    shared.update(consts)
    maps = []
    for b in range(8):
        m = dict(shared)
        m["xin"] = np.ascontiguousarray(np.concatenate([ctx[b], x[b]], axis=0))
        cv = np.stack([c[b], c_ctx], axis=0)
        m["cvecT"] = np.ascontiguousarray(cv.reshape(2, 8, 128).transpose(2, 1, 0))
        maps.append(m)
    return maps


def kernel(**inputs):
    shapes = {k: np.asarray(inputs[k]).shape for k in W_NAMES}
    nc, S = make_program(shapes)
    maps = make_in_maps(inputs)
    res = run_bass_kernel_spmd(nc, maps, core_ids=list(range(8)))
    return np.stack([np.asarray(r["out"], dtype=np.float32) for r in res.results], axis=0)
```

```python
from concourse.bass_utils import run_bass_kernel_spmd
from contextlib import ExitStack
import numpy as np
import concourse.bass as bass
import concourse.mybir as mybir

F32 = mybir.dt.float32
BF16 = mybir.dt.bfloat16
AF = mybir.ActivationFunctionType
ALU = mybir.AluOpType
AX = mybir.AxisListType

ENGS = ("pe", "act", "dve", "pool", "sp")


class Sched:
    NDMA = 6

    def __init__(self, nc):
        self.nc = nc
        self.es = ExitStack()
        self.eng = {"pe": nc.tensor, "act": nc.scalar, "dve": nc.vector,
                    "pool": nc.gpsimd, "sp": nc.sync}
        self.sem = {}
        self.cnt = {}
        for e in ENGS:
            self.sem[e] = self.es.enter_context(nc.semaphore("c_" + e))
            self.cnt[e] = 0
        self.dq = {}
        for q in ("sp", "pool", "act"):
            sems = [self.es.enter_context(nc.semaphore(f"d_{q}{j}")) for j in range(self.NDMA)]
            self.dq[q] = {"sems": sems, "vals": [0] * self.NDMA, "next": 0}
        self.semh = dict(self.sem)
        for q, d in self.dq.items():
            for j, s in enumerate(d["sems"]):
                self.semh[("d", q, j)] = s
        self.seen = {e: {} for e in ENGS}
        self.res = {}
        self.nwait = 0
        self.nins = 0

    def _deps(self, e, reads, writes):
        need = {}

        def add(tok):
            if tok is None:
                return
            s, v = tok
            if e == "pe" and s == "pe":
                return
            if need.get(s, 0) < v:
                need[s] = v

        for r in reads:
            st = self.res.get(r)
            if st is not None:
                add(st["w"])
        for w in writes:
            st = self.res.get(w)
            if st is not None:
                add(st["w"])
                for s, v in st["r"].items():
                    add((s, v))
        seen = self.seen[e]
        for s, v in need.items():
            if seen.get(s, 0) < v:
                self.eng[e].wait_ge(self.semh[s], v)
                seen[s] = v
                self.nwait += 1

    def _mark(self, tok, reads, writes):
        s, v = tok
        for r in reads:
            st = self.res.setdefault(r, {"w": None, "r": {}})
            if st["r"].get(s, 0) < v:
                st["r"][s] = v
        for w in writes:
            self.res[w] = {"w": tok, "r": {}}

    def op(self, e, fn, reads=(), writes=()):
        self._deps(e, reads, writes)
        ins = fn(self.eng[e])
        self.cnt[e] += 1
        ins.then_inc(self.sem[e], 1)
        self.nins += 1
        self._mark((e, self.cnt[e]), reads, writes)
        return ins

    def dma(self, q, out, in_, reads=(), writes=(), **kw):
        d = self.dq[q]
        j = d["next"]
        d["next"] = (j + 1) % self.NDMA
        key = ("d", q, j)
        seen = self.seen[q]
        if seen.get(key, 0) < d["vals"][j]:
            self.eng[q].wait_ge(d["sems"][j], d["vals"][j])
            seen[key] = d["vals"][j]
            self.nwait += 1
        self._deps(q, reads, writes)
        ins = self.eng[q].dma_start(out=out, in_=in_, **kw)
        d["vals"][j] += 16
        ins.then_inc(d["sems"][j], 16)
        self.nins += 1
        self._mark((key, d["vals"][j]), reads, writes)
        return ins

    def barrier(self):
        for e in ENGS:
            seen = self.seen[e]
            for s in ENGS:
                if self.cnt[s] == 0:
                    continue
                if seen.get(s, 0) < self.cnt[s]:
                    self.eng[e].wait_ge(self.sem[s], self.cnt[s])
                    seen[s] = self.cnt[s]
            for q, d in self.dq.items():
                for j in range(self.NDMA):
                    key = ("d", q, j)
                    if seen.get(key, 0) < d["vals"][j]:
                        self.eng[e].wait_ge(d["sems"][j], d["vals"][j])
                        seen[key] = d["vals"][j]
        self.res = {}

    def close(self):
        self.es.close()


import os as _os
AB_STOP = int(_os.environ.get('AB_STOP', '0'))
AB_CUT = int(_os.environ.get('AB_CUT', '9'))
AB_SUB = int(_os.environ.get('AB_SUB', '9'))


def make_mixer_ab(nc, S, io, mk, idf, idb, epsb, helpers):
    from contextlib import ExitStack
    op, dma = S.op, S.dma
    Work, load_x, modulate, transpose8, load_mods, out_proj = helpers
    NCH = NTOK // 64

    def phase(l, src, dst, q_tiles):
        i = l // 2
        win = io["ab_w_in"][i].rearrange("(kc p) n -> p kc n", p=128)
        eso = ExitStack()
        sbo, pso = mk(eso)
        AKT = sbo("AKT", [64, 2, NTOK], BF16)
        AV = sbo("AV", [128, NT, 2, 128], BF16)
        op("pool", lambda e: e.memset(AV[:, :, :, 64:128], 1.0), [], ["AVones"])
        esm = ExitStack()
        sbm, psm = mk(esm)
        hTall = sbm("hTall", [128, 8, NTOK], BF16)
        es = ExitStack()
        sb, ps = mk(es)
        w1 = sb("w1", [128, 8, 1296], BF16)
        dma("pool", w1[:, :, 0:768], win[:, :, 0:768], writes=["w1a"])
        dma("pool", w1[:, :, 768:1296], win[:, :, 2304:2832], writes=["w1b"])
        mods = load_mods(sb, l, 1, ("sh", "sc"))
        W = Work(sb, ps, nxt=2, npy=0, ntmp=1, nhb=2, nz=0)
        dtb = sb("dtb", [128, 8], F32)
        nea = sb("nea", [128, 8], F32)
        one1 = sb("one1", [128, 1], F32)
        op("dve", lambda e: e.memset(one1[:], 1.0), [], ["one1"])
        dma("sp", dtb[:], io["ab_dt_bias"][i:i + 1].rearrange("o a b -> o (a b)").broadcast_to([128, 8]), writes=["dtb"])
        dma("sp", nea[:], io["ab_a_log"][i:i + 1].rearrange("o a b -> o (a b)").broadcast_to([128, 8]), writes=["nea"])
        op("act", lambda e: e.activation(out=nea[:], in_=nea[:], func=AF.Exp), ["nea"], ["nea"])
        op("dve", lambda e: e.tensor_scalar_mul(out=nea[:], in0=nea[:], scalar1=-1.0), ["nea"], ["nea"])
        qa = RB([sb(f"qa{k}", [128, 640], F32) for k in range(2)], "qa")
        qra = RB([sb(f"qra{k}", [128, 10, 64], BF16) for k in range(2)], "qra")
        csa = RB([sb(f"csa{k}", [128, 2, 32], F32) for k in range(2)], "csa")
        ra = sb("raA", [128, 10, 32], F32)
        rb_ = sb("rbA", [128, 10, 32], F32)
        aqt = RB([sb(f"aqt{k}", [64, 8, 128], BF16) for k in range(2)], "aqt")
        zs = RB([sb(f"zs{k}", [128, 512], F32) for k in range(2)], "zs")
        gbt = RB([sb(f"gbt{k}", [128, 16], F32) for k in range(2)], "gbt")
        gtmp = sb("gtmp", [128, 8], F32)
        pa0 = ps("pa0", [128, 512], F32)
        pa1f = ps("pa1", [128, 512], F32)
        pa1 = pa1f[:, 0:256]
        pz = ps("pz", [128, 512], F32)
        pgtf = ps("pgt", [128, 512], F32)
        pgt = pgtf[:, 0:16]
        ptrA = ps("ptrA", [64, 16, 128], BF16)
        for t in range(NT):
            r = 1 if t < 2 else 0
            xt, xk = load_x(W, src, t)
            hb, hbk = modulate(W, xt, xk, mods, r)
            transpose8(W, hb, hbk, hTall[:, :, t * 128:(t + 1) * 128], ("hT", t))
            h = hTall[:, :, t * 128:(t + 1) * 128]
            for (pp, pk, c0, c1, wk) in ((pa0, "pa0", 0, 512, "w1a"), (pa1, "pa1", 512, 768, "w1a"), (pz, "pz", 768, 1280, "w1b"), (pgt, "pgt", 1280, 1296, "w1b")):
                for kc in range(8):
                    op("pe", lambda e: e.matmul(pp[:], lhsT=h[:, kc, :], rhs=w1[:, kc, c0:c1], start=(kc == 0), stop=(kc == 7)), [("hT", t), wk], [pk])
            if AB_CUT < 2:
                continue
            q, qk = qa.next()
            op("act", lambda e: e.copy(out=q[:, 0:512], in_=pa0[:]), ["pa0"], [qk])
            op("act", lambda e: e.copy(out=q[:, 512:640], in_=pa1[:, 0:128]), ["pa1"], [qk])
            op("act", lambda e: e.copy(out=AV[:, t, :, 0:64], in_=pa1[:, 128:256].rearrange("p (g d) -> p g d", g=2)), ["pa1"], [("AV", t)])
            if AB_SUB < 2:
                continue
            qo, qok = qra.next()
            q3 = q[:].rearrange("p (h d) -> p h d", d=64)
            if r == 1:
                op("dve", lambda e: e.tensor_copy(out=qo[:], in_=q3), [qk], [qok])
            else:
                c, ck = csa.next()
                p0 = (t - 2) * 128
                dma("sp", c[:, 0, :], io["cosA"][p0:p0 + 128, :], writes=[ck])
                dma("sp", c[:, 1, :], io["sinA"][p0:p0 + 128, :], writes=[ck])
                q4 = q[:].rearrange("p (h two d) -> p h two d", two=2, d=32)
                o4 = qo[:].rearrange("p h (two d) -> p h two d", two=2)
                cosb = c[:, 0, :].unsqueeze(1).to_broadcast([128, 10, 32])
                sinb = c[:, 1, :].unsqueeze(1).to_broadcast([128, 10, 32])
                x1, x2 = q4[:, :, 0, :], q4[:, :, 1, :]
                op("dve", lambda e: e.tensor_tensor(out=ra[:], in0=x1, in1=cosb, op=ALU.mult), [qk, ck], ["raA"])
                op("dve", lambda e: e.tensor_tensor(out=rb_[:], in0=x2, in1=sinb, op=ALU.mult), [qk, ck], ["rbA"])
                op("dve", lambda e: e.tensor_tensor(out=o4[:, :, 0, :], in0=ra[:], in1=rb_[:], op=ALU.subtract), ["raA", "rbA"], [qok])
                op("dve", lambda e: e.tensor_tensor(out=ra[:], in0=x1, in1=sinb, op=ALU.mult), [qk, ck], ["raA"])
                op("dve", lambda e: e.tensor_tensor(out=rb_[:], in0=x2, in1=cosb, op=ALU.mult), [qk, ck], ["rbA"])
                op("dve", lambda e: e.tensor_tensor(out=o4[:, :, 1, :], in0=ra[:], in1=rb_[:], op=ALU.add), ["raA", "rbA"], [qok])
            if AB_SUB < 3:
                continue
            for hh in range(10):
                op("pe", lambda e: e.transpose(out=ptrA[:, hh, :], in_=qo[:, hh, :], identity=idb[:]), [qok, "idb"], ["ptrA"])
            a, ak = aqt.next()
            op("act", lambda e: e.copy(out=a[:], in_=ptrA[:, 0:8, :]), ["ptrA"], [ak])
            dma("sp", io["AQT"][t], a[:], reads=[ak], writes=[("AQT", t)])
            op("dve", lambda e: e.tensor_copy(out=AKT[:, :, t * 128:(t + 1) * 128], in_=ptrA[:, 8:10, :]), ["ptrA"], [("AKT", t)])
            if AB_CUT < 3:
                continue
            z, zk = zs.next()
            op("act", lambda e: e.activation(out=z[:], in_=pz[:], func=AF.Silu), ["pz"], [zk])
            dma("sp", io["BZ"][t * 128:(t + 1) * 128, :], z[:], reads=[zk], writes=[("BZ", t)])
            if AB_CUT < 4:
                continue
            gb, gbk = gbt.next()
            op("act", lambda e: e.activation(out=gb[:, 8:16], in_=pgt[:, 0:8], func=AF.Sigmoid), ["pgt"], [gbk])
            op("dve", lambda e: e.tensor_tensor(out=gtmp[:], in0=pgt[:, 8:16], in1=dtb[:], op=ALU.add), ["pgt", "dtb"], ["gtmp"])
            op("act", lambda e: e.activation(out=gtmp[:], in_=gtmp[:], func=AF.Exp), ["gtmp"], ["gtmp"])
            op("act", lambda e: e.activation(out=gtmp[:], in_=gtmp[:], func=AF.Ln, bias=one1[:, 0:1], scale=1.0), ["gtmp", "one1"], ["gtmp"])
            op("dve", lambda e: e.tensor_tensor(out=gb[:, 0:8], in0=gtmp[:], in1=nea[:], op=ALU.mult), ["gtmp", "nea"], [gbk])
            dma("sp", io["GB"][t * 128:(t + 1) * 128, :], gb[:], reads=[gbk], writes=[("GB", t)])
        S.barrier()
        es.close()
        if AB_STOP == 1:
            esm.close(); eso.close(); return
        es = ExitStack()
        sb, ps = mk(es)
        w2 = sb("w2", [128, 8, 1536], BF16)
        dma("pool", w2[:], win[:, :, 768:2304], writes=["w2"])
        cw = sb("cw", [128, 12, 5], F32)
        cwv = io["ab_conv_w"][i].rearrange("k (cc p) -> p cc k", p=128)
        for cc in range(12):
            dma("sp", cw[:, cc, :], cwv[:, cc, :], writes=["cw"], allow_slow_non_contiguous=True)
        onesb = sb("onesb", [128, 128], BF16)
        op("dve", lambda e: e.memset(onesb[:], 1.0), [], ["onesb"])
        RAWW = 4360
        raw = sb("raw", [128, RAWW], F32)
        acc = sb("acc", [128, RAWW], F32)
        sqb = sb("sqb", [128, RAWW], BF16)
        op("pool", lambda e: e.memset(raw[:], 0.0), [], ["raw"])
        fm = RB([sb(f"fm{k}", [128, 512], F32) for k in range(2)], "fm")
        rn = RB([sb(f"rn{k}", [128, 512], F32) for k in range(2)], "rn")
        tk = RB([sb(f"tk{k}", [128, 4, 128], F32) for k in range(2)], "tk")
        pc = RB([ps(f"pc{k}", [128, 512], F32) for k in range(2)], "pc")
        pn = RB([ps(f"pn{k}", [128, 512], F32) for k in range(2)], "pn")
        ptk = RB([ps(f"ptk{k}", [128, 4, 128], F32) for k in range(2)], "ptk")
        blocks = [(0, 256, 2)] + [(256 + b * 512, 512, 256 + b * 512 + 6) for b in range(8)]
        for cc in range(12):
            kind, hd = ("q", "k", "v")[cc // 4], cc % 4
            for (tok0, n, rc) in blocks:
                p, pk = pc.next()
                for kc in range(8):
                    op("pe", lambda e: e.matmul(p[:, :n], lhsT=w2[:, kc, cc * 128:(cc + 1) * 128], rhs=hTall[:, kc, tok0:tok0 + n], start=(kc == 0), stop=(kc == 7)),
                       ["w2"], [pk])
                op("act", lambda e: e.copy(out=raw[:, rc:rc + n], in_=p[:, :n]), [pk], ["raw"])
            lo, hi = 2, RAWW - 2
            op("dve", lambda e: e.tensor_scalar_mul(out=acc[:, lo:hi], in0=raw[:, lo - 2:hi - 2], scalar1=cw[:, cc, 0:1]), ["raw", "cw"], ["acc"])
            for k in range(1, 5):
                op("dve", lambda e: e.scalar_tensor_tensor(out=acc[:, lo:hi], in0=raw[:, lo + k - 2:hi + k - 2], scalar=cw[:, cc, k:k + 1], in1=acc[:, lo:hi],
                                                            op0=ALU.mult, op1=ALU.add), ["raw", "cw", "acc"], ["acc"])
            op("act", lambda e: e.activation(out=acc[:, lo:hi], in_=acc[:, lo:hi], func=AF.Silu), ["acc"], ["acc"])
            if kind != "v":
                op("act", lambda e: e.activation(out=sqb[:, lo:hi], in_=acc[:, lo:hi], func=AF.Square), ["acc"], ["sqb"])
            for (tok0, n, rc) in blocks:
                if kind != "v":
                    p, pk = pn.next()
                    op("pe", lambda e: e.matmul(p[:, :n], lhsT=onesb[:], rhs=sqb[:, rc:rc + n], start=True, stop=True), ["onesb", "sqb"], [pk])
                    r_, rk = rn.next()
                    op("act", lambda e: e.activation(out=r_[:, :n], in_=p[:, :n], func=AF.Sqrt, bias=epsb[:, 0:1], scale=1.0), [pk, "epsb"], [rk])
                    op("dve", lambda e: e.reciprocal(out=r_[:, :n], in_=r_[:, :n]), [rk], [rk])
                    f, fk = fm.next()
                    sc_ = float(128 ** -0.5) if kind == "q" else 1.0
                    op("dve", lambda e: e.scalar_tensor_tensor(out=f[:, :n], in0=acc[:, rc:rc + n], scalar=sc_, in1=r_[:, :n], op0=ALU.mult, op1=ALU.mult),
                       ["acc", rk], [fk])
                    dst_fm = io["BQT"] if kind == "q" else io["BKT"]
                    dma("sp", dst_fm[hd, :, tok0:tok0 + n], f[:, :n], reads=[fk], writes=[("BFM", cc)])
                    srcT, srck = f, fk
                    off = 0
                else:
                    srcT, srck = acc, "acc"
                    off = rc
                if kind != "q":
                    nt_ = n // 128
                    pt_, ptk_ = ptk.next()
                    for j in range(nt_):
                        op("pe", lambda e: e.transpose(out=pt_[:, j, :], in_=srcT[:, off + j * 128:off + (j + 1) * 128], identity=idf[:]), [srck, "idf"], [ptk_])
                    tt, ttk = tk.next()
                    op("dve" if kind == "k" else "act", (lambda e: e.tensor_copy(out=tt[:, :nt_, :], in_=pt_[:, :nt_, :])) if kind == "k" else
                       (lambda e: e.copy(out=tt[:, :nt_, :], in_=pt_[:, :nt_, :])), [ptk_], [ttk])
                    kvi = 0 if kind == "k" else 1
                    dma("sp", io["BKV"][tok0:tok0 + n, kvi, hd * 128:(hd + 1) * 128].rearrange("(j p) d -> p j d", p=128), tt[:, :nt_, :],
                        reads=[ttk], writes=[("BKVw", cc)])
        S.barrier()
        es.close()
        esm.close()
        if AB_STOP == 2:
            eso.close(); return
        es = ExitStack()
        sb, ps = mk(es)
        mk32 = sb("mk32", [128, 2, 128], F32)
        mkb = sb("mkb", [128, 2, 128], BF16)
        dma("sp", mk32[:], io["cA"][:, :, :], writes=["mk32"])
        op("dve", lambda e: e.tensor_copy(out=mkb[:], in_=mk32[:]), ["mk32"], ["mkb"])
        mk4 = sb("mk4", [128, 2, 4, 128], BF16)
        for mi_ in range(2):
            for hh_ in range(4):
                op("dve", lambda e: e.tensor_copy(out=mk4[:, mi_, hh_, :], in_=mk32[:, mi_, :]), ["mk32"], ["mk4"])
        esink = sb("esink", [128, 8], F32)
        dma("sp", esink[:], io["ab_sink"][i:i + 1, :].broadcast_to([128, 8]), writes=["esink"])
        op("act", lambda e: e.activation(out=esink[:], in_=esink[:], func=AF.Exp), ["esink"], ["esink"])
        aq = RB([sb(f"aq{k}", [64, 8, 128], BF16) for k in range(2)], "aq")
        PT = RB([sb(f"PTa{k}", [128, 4, 128], BF16) for k in range(3)], "PTa")
        oa = RB([sb(f"oa{k}", [128, 8, 64], BF16) for k in range(2)], "oa")
        den = sb("den", [128, 4], F32)
        pS = RB([ps(f"pSa{k}", [128, 512], F32) for k in range(2)], "pSa")
        pO = [ps(f"pOa{k}", [128, 4, 128], F32) for k in range(2)]
        for t in q_tiles:
            a, ak = aq.next()
            dma("sp", a[:], io["AQT"][t], reads=[("AQT", t)], writes=[ak])
            if t < 2:
                kl = [(0, None), (1, None)]
            else:
                kl = []
                if t - 1 >= 2:
                    kl.append((t - 1, 0))
                kl.append((t, None))
                if t + 1 < NT:
                    kl.append((t + 1, 1))
                kl += [(0, None), (1, None)]
            o, ok = oa.next()
            for g in range(2):
                def issue_s(kt_):
                    p_, pk_ = pS.next()
                    op("pe", lambda e: e.matmul(p_[:], lhsT=AKT[:, g, kt_ * 128:(kt_ + 1) * 128], rhs=a[:, 4 * g:4 * g + 4, :], start=True, stop=True),
                       [("AKT", kt_), ak], [pk_])
                    return p_, pk_
                nxt_s = issue_s(kl[0][0])
                for ki, (kt, mi) in enumerate(kl):
                    p, pk = nxt_s
                    if ki + 1 < len(kl):
                        nxt_s = issue_s(kl[ki + 1][0])
                    pt, ptk_ = PT.next()
                    op("act", lambda e: e.activation(out=pt[:].rearrange("p h q -> p (h q)"), in_=p[:], func=AF.Exp, scale=0.125), [pk], [ptk_])
                    if mi is not None:
                        op("dve", lambda e: e.tensor_tensor(out=pt[:], in0=pt[:], in1=mk4[:, mi, :, :], op=ALU.mult), [ptk_, "mk4"], [ptk_])
                    for hh in range(4):
                        op("pe", lambda e: e.matmul(pO[g][:, hh, 0:65], lhsT=pt[:, hh, :], rhs=AV[:, kt, g, 0:65], start=(ki == 0 and hh == 0), stop=(ki == len(kl) - 1), skip_group_check=True),
                           [ptk_, ("AV", kt), "AVones"], [f"pOa{g}"])
                op("dve", lambda e: e.tensor_tensor(out=den[:], in0=pO[g][:, :, 64], in1=esink[:, 4 * g:4 * g + 4], op=ALU.add), [f"pOa{g}", "esink"], ["den"])
                op("dve", lambda e: e.reciprocal(out=den[:], in_=den[:]), ["den"], ["den"])
                op("dve", lambda e: e.tensor_tensor(out=o[:, 4 * g:4 * g + 4, :], in0=pO[g][:, :, 0:64], in1=den[:, :].unsqueeze(2).to_broadcast([128, 4, 64]), op=ALU.mult),
                   [f"pOa{g}", "den"], [ok])
            dma("sp", io["OA"][t * 128:(t + 1) * 128, :], o[:].rearrange("p h d -> p (h d)"), reads=[ok], writes=[("OA", t)])
        S.barrier()
        es.close()
        eso.close()
        if AB_STOP == 3:
            return
        es = ExitStack()
        sb, ps = mk(es)
        cB = sb("cB", [64, 4, 64], F32)
        dma("sp", cB[:], io["cB"][:, :, :], writes=["cB"])
        ones64 = sb("ones64", [64, 128], F32)
        op("dve", lambda e: e.memset(ones64[:], 1.0), [], ["ones64"])
        Sst = [sb(f"Sst{d}", [128, 4, 128], F32) for d in range(2)]
        for d in range(2):
            op("pool", lambda e: e.memset(Sst[d][:], 0.0), [], [f"S{d}"])
        NB = 6
        def rb(name, shape, n=NB):
            return RB([sb(f"{name}{k}", shape, F32) for k in range(n)], name)
        kT4 = rb("kT4", [128, 4, 64]); qT4 = rb("qT4", [128, 4, 64]); kv = rb("kvB", [64, 2, 512]); gbb = rb("gbB", [64, 16])
        Rm = rb("Rm", [64, 4, 64]); gcs = rb("gcs", [64, 4]); gls = rb("gls", [128, 4]); egc = rb("egc", [64, 4]); ekt = rb("ekt", [64, 4]); egl = rb("egl", [128, 4])
        Dm = rb("Dm", [64, 4, 64]); A0 = rb("A0", [64, 4, 64]); AT = rb("AT", [64, 4, 64]); Q = rb("Qm", [64, 4, 64]); Qfin = rb("Qfin", [64, 4, 64]); attT = rb("attT", [64, 4, 64])
        tmpv = rb("tmpv", [64, 4, 128]); rr = rb("rr", [64, 4, 128]); vnew = rb("vnew", [64, 4, 128]); ktok = rb("ktok", [64, 4, 128]); ob = rb("obB", [64, 4, 128])
        pb = RB([ps(f"pb{k}", [128, 512], F32) for k in range(8)], "pb")

        def P64(p):
            return p[0:64, 0:256].rearrange("p (h i) -> p h i", h=4)

        def P64w(p):
            return p[0:64, :].rearrange("p (h i) -> p h i", h=4)

        def prep(c, d, R_):
            tri = cB[:, d, :]
            nstr = cB[:, 2 + d, :]
            k4, k4k = kT4.next(); q4, q4k = qT4.next(); kvt, kvk = kv.next(); g_, gk = gbb.next()
            c0 = c * 64
            dma("sp", k4[:], io["BKT"][:, :, c0:c0 + 64].rearrange("h d t -> d h t"), writes=[k4k])
            dma("sp", q4[:], io["BQT"][:, :, c0:c0 + 64].rearrange("h d t -> d h t"), writes=[q4k])
            dma("sp", kvt[:], io["BKV"][c0:c0 + 64, :, :], writes=[kvk])
            dma("sp", g_[:], io["GB"][c0:c0 + 64, :], writes=[gk])
            g4 = g_[:, 4 * d:4 * d + 4]
            b4 = g_[:, 8 + 4 * d:8 + 4 * d + 4]
            R, Rk = Rm.next()
            op("dve", lambda e: e.tensor_tensor(out=R[:], in0=tri.unsqueeze(1).to_broadcast([64, 4, 64]), in1=g4.unsqueeze(2).to_broadcast([64, 4, 64]), op=ALU.mult),
               ["cB", gk], [Rk])
            pgr, pgrk = pb.next()
            op("pe", lambda e: e.matmul(pgr[0:64, 0:256], lhsT=ones64[:, 0:64], rhs=R[:].rearrange("p h i -> p (h i)"), start=True, stop=True), ["ones64", Rk], [pgrk])
            psm, psmk = pb.next()
            op("pe", lambda e: e.matmul(psm[0:64, 0:4], lhsT=tri, rhs=g4, start=True, stop=True), ["cB", gk], [psmk])
            op("pe", lambda e: e.matmul(psm[:, 4:8], lhsT=ones64[:, :], rhs=g4, start=True, stop=True), ["ones64", gk], [psmk])
            gc, gck = gcs.next(); gl, glk = gls.next(); eg, egk = egc.next(); ek, ekk = ekt.next(); el, elk = egl.next()
            op("dve", lambda e: e.tensor_copy(out=gc[:], in_=psm[0:64, 0:4]), [psmk], [gck])
            op("dve", lambda e: e.tensor_copy(out=gl[:], in_=psm[:, 4:8]), [psmk], [glk])
            op("act", lambda e: e.activation(out=eg[:], in_=gc[:], func=AF.Exp), [gck], [egk])
            op("dve", lambda e: e.tensor_tensor(out=ek[:], in0=gl[0:64, :], in1=gc[:], op=ALU.subtract), [glk, gck], [ekk])
            op("act", lambda e: e.activation(out=ek[:], in_=ek[:], func=AF.Exp), [ekk], [ekk])
            op("act", lambda e: e.activation(out=el[:], in_=gl[:], func=AF.Exp), [glk], [elk])
            D_, Dk = Dm.next()
            op("dve", lambda e: e.tensor_tensor(out=D_[:], in0=P64(pgr), in1=gc[:, :].unsqueeze(2).to_broadcast([64, 4, 64]), op=ALU.subtract), [pgrk, gck], [Dk])
            op("dve", lambda e: e.tensor_scalar_min(out=D_[:], in0=D_[:], scalar1=0.0), [Dk], [Dk])
            op("act", lambda e: e.activation(out=D_[:], in_=D_[:], func=AF.Exp), [Dk], [Dk])
            op("dve", lambda e: e.tensor_tensor(out=D_[:], in0=D_[:], in1=tri.unsqueeze(1).to_broadcast([64, 4, 64]), op=ALU.mult), [Dk, "cB"], [Dk])
            yield
            pkk, pkkk = pb.next()
            pqk, pqkk = pb.next()
            for h in range(4):
                op("pe", lambda e: e.matmul(pkk[0:64, h * 64:(h + 1) * 64], lhsT=k4[:, h, :], rhs=k4[:, h, :], start=True, stop=True), [k4k], [pkkk])
            for h in range(4):
                op("pe", lambda e: e.matmul(pqk[0:64, h * 64:(h + 1) * 64], lhsT=k4[:, h, :], rhs=q4[:, h, :], start=True, stop=True), [k4k, q4k], [pqkk])
            a0, a0k = A0.next()
            op("dve", lambda e: e.tensor_tensor(out=a0[:], in0=P64(pkk), in1=D_[:], op=ALU.mult), [pkkk, Dk], [a0k])
            op("dve", lambda e: e.tensor_tensor(out=a0[:], in0=a0[:], in1=nstr.unsqueeze(1).to_broadcast([64, 4, 64]), op=ALU.mult), [a0k, "cB"], [a0k])
            op("dve", lambda e: e.tensor_tensor(out=a0[:], in0=a0[:], in1=b4.unsqueeze(2).to_broadcast([64, 4, 64]), op=ALU.mult), [a0k, gk], [a0k])
            at_, atk = attT.next()
            op("dve", lambda e: e.tensor_tensor(out=at_[:], in0=P64(pqk), in1=D_[:], op=ALU.mult), [pqkk, Dk], [atk])
            ptt, pttk = pb.next()
            for h in range(4):
                op("pe", lambda e: e.transpose(out=ptt[0:64, h * 64:(h + 1) * 64], in_=a0[:, h, :], identity=idf[0:64, 0:64]), [a0k, "idf"], [pttk])
            aT_, aTk = AT.next()
            op("act", lambda e: e.copy(out=aT_[:], in_=P64(ptt)), [pttk], [aTk])
            q_, qk_ = Q.next()
            op("dve", lambda e: e.tensor_tensor(out=q_[:], in0=a0[:], in1=idf[0:64, 0:64].unsqueeze(1).to_broadcast([64, 4, 64]), op=ALU.add), [a0k, "idf"], [qk_])
            yield
            am, amk, amT, amTk = a0, a0k, aT_, aTk
            for m in range(1, 6):
                pAT, pATk = pb.next()
                for h in range(4):
                    op("pe", lambda e: e.matmul(pAT[0:64, h * 64:(h + 1) * 64], lhsT=am[:, h, :], rhs=amT[:, h, :], start=True, stop=True), [amk, amTk], [pATk])
                if m < 5:
                    pA, pAk = pb.next()
                    for h in range(4):
                        op("pe", lambda e: e.matmul(pA[0:64, h * 64:(h + 1) * 64], lhsT=amT[:, h, :], rhs=am[:, h, :], start=True, stop=True), [amk, amTk], [pAk])
                nT, nTk = AT.next()
                op("act", lambda e: e.copy(out=nT[:], in_=P64(pAT)), [pATk], [nTk])
                if m < 5:
                    nA, nAk = A0.next()
                    op("dve", lambda e: e.tensor_copy(out=nA[:], in_=P64(pA)), [pAk], [nAk])
                else:
                    nA, nAk = None, None
                pQ, pQk = pb.next()
                for h in range(4):
                    op("pe", lambda e: e.matmul(pQ[0:64, h * 64:(h + 1) * 64], lhsT=nT[:, h, :], rhs=q_[:, h, :], start=True, stop=True), [nTk, qk_], [pQk])
                nq, nqk = (Q.next() if m < 5 else Qfin.next())
                op("dve", lambda e: e.tensor_tensor(out=nq[:], in0=P64(pQ), in1=q_[:], op=ALU.add), [pQk, qk_], [nqk])
                q_, qk_ = nq, nqk
                am, amk, amT, amTk = nA, nAk, nT, nTk
                yield
            R_.update(dict(k4=k4, k4k=k4k, q4=q4, q4k=q4k, kvt=kvt, kvk=kvk, gk=gk, b4=b4, eg=eg, egk=egk, ek=ek, ekk=ekk, el=el, elk=elk, at_=at_, atk=atk, q_=q_, qk_=qk_))

        def scan(R_, c, d):
            k4, k4k, q4, q4k, kvt, kvk, gk, b4 = R_["k4"], R_["k4k"], R_["q4"], R_["q4k"], R_["kvt"], R_["kvk"], R_["gk"], R_["b4"]
            eg, egk, ek, ekk, el, elk, at_, atk, q_, qk_ = R_["eg"], R_["egk"], R_["ek"], R_["ekk"], R_["el"], R_["elk"], R_["at_"], R_["atk"], R_["q_"], R_["qk_"]
            c0 = c * 64
            Sd, Sk = Sst[d], f"S{d}"
            pks, pksk = pb.next()
            for h in range(4):
                op("pe", lambda e: e.matmul(pks[0:64, h * 128:(h + 1) * 128], lhsT=k4[:, h, :], rhs=Sd[:, h, :], start=True, stop=True), [k4k, Sk], [pksk])
            pqs, pqsk = pb.next()
            for h in range(4):
                op("pe", lambda e: e.matmul(pqs[0:64, h * 128:(h + 1) * 128], lhsT=q4[:, h, :], rhs=Sd[:, h, :], start=True, stop=True), [q4k, Sk], [pqsk])
            tv, tvk = tmpv.next()
            op("dve", lambda e: e.tensor_tensor(out=tv[:], in0=P64w(pks), in1=eg[:, :].unsqueeze(2).to_broadcast([64, 4, 128]), op=ALU.mult), [pksk, egk], [tvk])
            r_, rk = rr.next()
            v4 = kvt[:, 1, :].rearrange("p (h d) -> p h d", h=4)
            kk4 = kvt[:, 0, :].rearrange("p (h d) -> p h d", h=4)
            op("dve", lambda e: e.tensor_tensor(out=r_[:], in0=v4, in1=tv[:], op=ALU.subtract), [kvk, tvk], [rk])
            o_, ok_ = ob.next()
            op("dve", lambda e: e.tensor_tensor(out=o_[:], in0=P64w(pqs), in1=eg[:, :].unsqueeze(2).to_broadcast([64, 4, 128]), op=ALU.mult), [pqsk, egk], [ok_])
            kt_, ktk = ktok.next()
            op("dve", lambda e: e.tensor_tensor(out=kt_[:], in0=kk4, in1=ek[:, :].unsqueeze(2).to_broadcast([64, 4, 128]), op=ALU.mult), [kvk, ekk], [ktk])
            yield
            pv, pvk = pb.next()
            for h in range(4):
                op("pe", lambda e: e.matmul(pv[0:64, h * 128:(h + 1) * 128], lhsT=q_[:, h, :], rhs=r_[:, h, :], start=True, stop=True), [qk_, rk], [pvk])
            vn, vnk = vnew.next()
            op("dve", lambda e: e.tensor_tensor(out=vn[:], in0=P64w(pv), in1=b4.unsqueeze(2).to_broadcast([64, 4, 128]), op=ALU.mult), [pvk, gk], [vnk])
            yield
            pav, pavk = pb.next()
            for h in range(4):
                op("pe", lambda e: e.matmul(pav[0:64, h * 128:(h + 1) * 128], lhsT=at_[:, h, :], rhs=vn[:, h, :], start=True, stop=True), [atk, vnk], [pavk])
            psn, psnk = pb.next()
            for h in range(4):
                op("pe", lambda e: e.matmul(psn[:, h * 128:(h + 1) * 128], lhsT=kt_[:, h, :], rhs=vn[:, h, :], start=True, stop=True), [ktk, vnk], [psnk])
            op("dve", lambda e: e.tensor_tensor(out=o_[:], in0=P64w(pav), in1=o_[:], op=ALU.add), [pavk, ok_], [ok_])
            dma("pool", io["OB"][d, c0:c0 + 64, :], o_[:].rearrange("p h d -> p (h d)"), reads=[ok_], writes=[("OB", d, c)])
            op("dve", lambda e: e.tensor_tensor(out=Sd[:], in0=Sd[:], in1=el[:, :].unsqueeze(2).to_broadcast([128, 4, 128]), op=ALU.mult), [Sk, elk], [Sk])
            op("dve", lambda e: e.tensor_tensor(out=Sd[:], in0=psn[:, :].rearrange("p (h d) -> p h d", h=4), in1=Sd[:], op=ALU.add), [psnk, Sk], [Sk])
            yield

        def lockstep(gens):
            live = list(gens)
            while live:
                nxt = []
                for g_ in live:
                    try:
                        next(g_)
                        nxt.append(g_)
                    except StopIteration:
                        pass
                live = nxt

        order_f = list(range(0, 4)) + list(range(4, NCH))
        order_b = list(range(3, -1, -1)) + list(range(NCH - 1, 3, -1))
        PR = {}
        PR[(0, 0)] = {}; PR[(0, 1)] = {}
        lockstep([prep(order_f[0], 0, PR[(0, 0)]), prep(order_b[0], 1, PR[(0, 1)])])
        for s_ in range(NCH):
            gens = [scan(PR[(s_, 0)], order_f[s_], 0), scan(PR[(s_, 1)], order_b[s_], 1)]
            if s_ + 1 < NCH:
                PR[(s_ + 1, 0)] = {}; PR[(s_ + 1, 1)] = {}
                gens += [prep(order_f[s_ + 1], 0, PR[(s_ + 1, 0)]), prep(order_b[s_ + 1], 1, PR[(s_ + 1, 1)])]
            lockstep(gens)
            PR.pop((s_, 0)); PR.pop((s_, 1))
        S.barrier()
        es.close()
        if AB_STOP == 4:
            return
        es = ExitStack()
        sb, ps = mk(es)
        wout = sb("woutab", [128, 8, 1024], BF16)
        dma("pool", wout[:], io["ab_w_out"][i].rearrange("(kc p) n -> p kc n", p=128), writes=["woutab"])
        mods = load_mods(sb, l, 1, ("gt", "lng", "lnb"))
        W = Work(sb, ps, nxt=3, npy=2, with_pT=True, ntmp=0, nhb=0, nz=2)
        gn = sb("gn", [128, 128], F32)
        dma("sp", gn[:], io["ab_gnorm"][i:i + 1, :].broadcast_to([128, 128]), writes=["gn"])
        of = RB([sb(f"of{k}", [128, 512], F32) for k in range(2)], "of")
        obk = RB([sb(f"obk{k}", [128, 512], F32) for k in range(2)], "obk")
        zz = RB([sb(f"zz{k}", [128, 512], F32) for k in range(2)], "zz")
        sq5 = sb("sq5", [128, 512], F32)
        ss5 = sb("ss5", [128, 4], F32)
        oc = RB([sb(f"oc{k}", [128, 1024], BF16) for k in range(2)], "oc")
        oT = RB([sb(f"oT5{k}", [128, 8, 128], BF16) for k in range(2)], "oT5")
        for t in q_tiles:
            r = 1 if t < 2 else 0
            rows = slice(t * 128, (t + 1) * 128)
            f_, fk = of.next(); b_, bk = obk.next(); z_, zk = zz.next(); o_, ok_ = oc.next()
            dma("sp", f_[:], io["OB"][0, rows, :], writes=[fk])
            dma("sp", b_[:], io["OB"][1, rows, :], writes=[bk])
            dma("sp", z_[:], io["BZ"][rows, :], writes=[zk])
            dma("sp", o_[:, 0:512], io["OA"][rows, :], writes=[ok_ + "a"])
            op("dve", lambda e: e.tensor_tensor(out=f_[:], in0=f_[:], in1=b_[:], op=ALU.add), [fk, bk], [fk])
            op("act", lambda e: e.activation(out=sq5[:], in_=f_[:], func=AF.Square), [fk], ["sq5"])
            op("dve", lambda e: e.tensor_reduce(out=ss5[:], in_=sq5[:].rearrange("p (h d) -> p h d", d=128), axis=AX.X, op=ALU.add), ["sq5"], ["ss5"])
            op("act", lambda e: e.activation(out=ss5[:], in_=ss5[:], func=AF.Sqrt, bias=epsb[:, 0:1], scale=1.0 / 128), ["ss5", "epsb"], ["ss5"])
            op("dve", lambda e: e.reciprocal(out=ss5[:], in_=ss5[:]), ["ss5"], ["ss5"])
            f3 = f_[:].rearrange("p (h d) -> p h d", d=128)
            op("dve", lambda e: e.tensor_tensor(out=f3, in0=f3, in1=ss5[:, :].unsqueeze(2).to_broadcast([128, 4, 128]), op=ALU.mult), [fk, "ss5"], [fk])
            op("dve", lambda e: e.tensor_tensor(out=f3, in0=f3, in1=gn[:, :].unsqueeze(1).to_broadcast([128, 4, 128]), op=ALU.mult), [fk, "gn"], [fk])
            op("dve", lambda e: e.tensor_tensor(out=o_[:, 512:1024], in0=f_[:], in1=z_[:], op=ALU.mult), [fk, zk], [ok_ + "b"])
            ot, otk = oT.next()
            pT, ptk2 = W.pT.next()
            for kc in range(8):
                op("pe", lambda e: e.transpose(out=pT[:, kc, :], in_=o_[:, kc * 128:(kc + 1) * 128], identity=idb[:]), [ok_ + "a", ok_ + "b", "idb"], [ptk2])
            op("act", lambda e: e.copy(out=ot[:], in_=pT[:]), [ptk2], [otk])
            xt, xk = load_x(W, src, t)
            out_proj(W, ot, otk, wout, "woutab", xt, xk, mods["gt"][r], f"mod_gt{r}", mods, dst, t)
        S.barrier()
        es.close()

    return phase


import numpy as np

D = 1024
SEQ = 4096
CTX = 256
NTOK = SEQ + CTX
NT = NTOK // 128
DFF = 2816
NJ = DFF // 128
DEPTH = 4
ALPHA = (2 * DEPTH) ** 0.25
EPS = 1e-6
AB_IN = 2832


class RB:
    def __init__(self, items, name):
        self.items = items
        self.name = name
        self.i = 0

    def next(self):
        k = self.i
        self.i = (k + 1) % len(self.items)
        return self.items[k], f"{self.name}{k}"


def build(nc, S, io, layers=(0, 1, 2, 3), out_mode="final", skip_ctx_last=True):
    from contextlib import ExitStack
    es0 = ExitStack()

    uid = [0]

    def mk(es):
        def sb(name, shape, dt):
            uid[0] += 1
            return es.enter_context(nc.sbuf_tensor(f"{name}_u{uid[0]}", shape, dt))

        def ps(name, shape, dt):
            uid[0] += 1
            return es.enter_context(nc.psum_tensor(f"{name}_u{uid[0]}", shape, dt))
        return sb, ps

    sb0, ps0 = mk(es0)
    op, dma = S.op, S.dma
    XS = io["XS"]
    MOD = io["MOD"]

    idf = sb0("idf", [128, 128], F32)
    idb = sb0("idb", [128, 128], BF16)
    epsb = sb0("epsb", [128, 1], F32)
    dma("sp", idf[:], io["ident"][:, :], writes=["idf"])
    op("dve", lambda e: e.tensor_copy(out=idb[:], in_=idf[:]), ["idf"], ["idb"])
    op("dve", lambda e: e.memset(epsb[:], EPS), [], ["epsb"])

    def xrows(ap, t):
        return ap[t * 128:(t + 1) * 128, :]

    def phase_mod():
        es = ExitStack()
        sb, ps = mk(es)
        scT = sb("scT", [128, 8, 2], F32)
        aw = [sb(f"aw{i}", [128, 8, 512], F32) for i in range(3)]
        awr = RB(aw, "aw")
        mrow = sb("mrow", [2, 9216], F32)
        brow = sb("brow", [2, 9216], F32)
        pm = [ps(f"pm{i}", [128, 512], F32)[0:2, :] for i in range(2)]
        pmr = RB(pm, "pm")
        dma("sp", scT[:], io["cvecT"][:, :, :], writes=["scT"])
        op("act", lambda e: e.activation(out=scT[:], in_=scT[:], func=AF.Silu), ["scT"], ["scT"])
        for l in layers:
            dma("sp", brow[:], io["ada_b"][l:l + 1, :].broadcast_to([2, 9216]), writes=["brow"])
            wv = io["ada_w"][l].rearrange("(kc p) n -> p kc n", p=128)
            for n in range(18):
                a, ak = awr.next()
                dma("sp" if n % 2 == 0 else "act", a[:], wv[:, :, n * 512:(n + 1) * 512], writes=[ak])
                p, pk = pmr.next()
                for kc in range(8):
                    op("pe", lambda e: e.matmul(p[:], lhsT=scT[:, kc, :], rhs=a[:, kc, :], start=(kc == 0), stop=(kc == 7)),
                       ["scT", ak], [pk])
                op("dve", lambda e: e.tensor_tensor(out=mrow[:, n * 512:(n + 1) * 512], in0=p[:], in1=brow[:, n * 512:(n + 1) * 512], op=ALU.add),
                   [pk, "brow"], ["mrow"])
            for s in range(3):
                c0 = (3 * s + 1) * 1024
                op("dve", lambda e: e.tensor_scalar_add(out=mrow[:, c0:c0 + 1024], in0=mrow[:, c0:c0 + 1024], scalar1=1.0), ["mrow"], ["mrow"])
                if s != 1:
                    c1 = (3 * s + 2) * 1024
                    op("dve", lambda e: e.tensor_scalar_mul(out=mrow[:, c1:c1 + 1024], in0=mrow[:, c1:c1 + 1024], scalar1=0.5), ["mrow"], ["mrow"])
            dma("sp", MOD[l, :, :], mrow[:], reads=["mrow"], writes=["MOD"])
        S.barrier()
        es.close()

    def load_mods(sb, l, s, which):
        out = {}
        for nm in which:
            if nm in ("sh", "sc", "gt"):
                k = {"sh": 0, "sc": 1, "gt": 2}[nm]
                tl = []
                for r in range(2):
                    t = sb(f"mod_{nm}{r}", [128, 1024], F32)
                    dma("sp", t[:], MOD[l, r:r + 1, (3 * s + k) * 1024:(3 * s + k + 1) * 1024].broadcast_to([128, 1024]),
                        reads=["MOD"], writes=[f"mod_{nm}{r}"])
                    tl.append(t)
                out[nm] = tl
            else:
                src = io["ln_g"] if nm == "lng" else io["ln_b"]
                t = sb(f"mod_{nm}", [128, 1024], F32)
                dma("sp", t[:], src[l, s:s + 1, :].broadcast_to([128, 1024]), writes=[f"mod_{nm}"])
                out[nm] = t
        return out

    class Work:
        def __init__(self, sb, ps, nxt=6, npy=2, with_pT=True, ntmp=2, nhb=2, nz=2):
            self.xt = RB([sb(f"xt{i}", [128, 1024], F32) for i in range(nxt)], "xt")
            self.tmp = RB([sb(f"mtmp{i}", [128, 1024], F32) for i in range(ntmp)], "mtmp")
            self.hb = RB([sb(f"hb{i}", [128, 1024], BF16) for i in range(nhb)], "hb")
            self.z = RB([sb(f"z{i}", [128, 1024], F32) for i in range(nz)], "z")
            self.st6 = sb("st6", [128, 2, 6], F32)
            self.mv = sb("mv", [128, 2], F32)
            self.rstd = sb("rstd", [128, 1], F32)
            self.nb = sb("nb", [128, 1], F32)
            if with_pT:
                self.pT = RB([ps(f"pT{i}", [128, 8, 128], BF16) for i in range(2)], "pT")
            self.py = RB([ps(f"py{i}", [128, 512], F32) for i in range(npy)], "py")

    def load_x(W, src, t):
        xt, xk = W.xt.next()
        dma("sp", xt[:], xrows(src, t), reads=[("X", t)], writes=[xk])
        return xt, xk

    def modulate(W, xt, xk, mods, r):
        tmp, tk = W.tmp.next()
        hb, hk = W.hb.next()
        op("pool", lambda e: e.tensor_tensor(out=tmp[:], in0=xt[:], in1=mods["sc"][r][:], op=ALU.mult), [xk, f"mod_sc{r}"], [tk])
        op("pool", lambda e: e.tensor_tensor(out=hb[:], in0=tmp[:], in1=mods["sh"][r][:], op=ALU.add), [tk, f"mod_sh{r}"], [hk])
        return hb, hk

    def transpose8(W, src, sk, dst_ap, dk, eng="act"):
        pT, pk = W.pT.next()
        for kc in range(8):
            op("pe", lambda e: e.transpose(out=pT[:, kc, :], in_=src[:, kc * 128:(kc + 1) * 128], identity=idb[:]), [sk, "idb"], [pk])
        if eng == "act":
            op("act", lambda e: e.copy(out=dst_ap, in_=pT[:]), [pk], [dk])
        else:
            op("dve", lambda e: e.tensor_copy(out=dst_ap, in_=pT[:]), [pk], [dk])

    def cast_wgu(l, j, slot):
        wv = io["ffn_w_gu"][l, j].rearrange("(kc p) n -> p kc n", p=128)
        dst = io["WGUB"][slot]
        for g in range(11):
            for u in range(2):
                dma("pool", dst[g, :, :, u * 256:(u + 1) * 256], wv[:, :, u * DFF + g * 256:u * DFF + (g + 1) * 256],
                    writes=[("WGUB", slot, g)])

    def phase_ffn(l, j, src, dst, tiles, slot, final=False):
        s = 0 if j == 0 else 2
        es = ExitStack()
        sb, ps = mk(es)
        wd = sb("wd", [128, NJ, 1024], BF16)
        wdv = io["ffn_w_down"][l, j].rearrange("(jc p) n -> p jc n", p=128)
        for q in range(2):
            dma("pool", wd[:, q * 11:(q + 1) * 11, :], wdv[:, q * 11:(q + 1) * 11, :], writes=[f"wd{q}"])
        mods = load_mods(sb, l, s, ("sh", "sc", "gt", "lng", "lnb"))
        W = Work(sb, ps, nxt=8, npy=2)
        wg = RB([sb(f"wg{i}", [128, 8, 512], BF16) for i in range(3)], "wg")
        hT = RB([sb(f"hT{i}", [128, 8, 512], BF16) for i in range(2)], "hT")
        aT = sb("aT", [128, NJ, 512], BF16)
        sg = RB([sb(f"sg{i}", [128, 512], F32) for i in range(2)], "sg")
        pg = RB([ps(f"pg{i}", [128, 512], F32) for i in range(2)], "pg")
        pu = RB([ps(f"pu{i}", [128, 512], F32) for i in range(2)], "pu")
        sts = [tiles[i:i + 4] for i in range(0, len(tiles), 4)]

        def prologue(st):
            h, hk = hT.next()
            xs = []
            for ti, t in enumerate(st):
                xt, xk = load_x(W, src, t)
                xs.append((xt, xk))
                r = 1 if t < 2 else 0
                hb, hbk = modulate(W, xt, xk, mods, r)
                transpose8(W, hb, hbk, h[:, :, ti * 128:(ti + 1) * 128], hk)
            return h, hk, xs

        nxt = prologue(sts[0])
        for si, st in enumerate(sts):
            n = 128 * len(st)
            h, hk, xs = nxt
            for g in range(11):
                w, wk = wg.next()
                dma("sp", w[:], io["WGUB"][slot][g], reads=[("WGUB", slot, g)], writes=[wk])
                for c in range(2):
                    jj = 2 * g + c
                    p1, p1k = pg.next()
                    p2, p2k = pu.next()
                    for kc in range(8):
                        op("pe", lambda e: e.matmul(p1[:, :n], lhsT=w[:, kc, c * 128:(c + 1) * 128], rhs=h[:, kc, :n], start=(kc == 0), stop=(kc == 7)),
                           [wk, hk], [p1k])
                    for kc in range(8):
                        op("pe", lambda e: e.matmul(p2[:, :n], lhsT=w[:, kc, 256 + c * 128:256 + (c + 1) * 128], rhs=h[:, kc, :n], start=(kc == 0), stop=(kc == 7)),
                           [wk, hk], [p2k])
                    s1, s1k = sg.next()
                    op("act", lambda e: e.activation(out=s1[:, :n], in_=p1[:, :n], func=AF.Silu), [p1k], [s1k])
                    op("dve", lambda e: e.tensor_tensor(out=aT[:, jj, :n], in0=p2[:, :n], in1=s1[:, :n], op=ALU.mult), [p2k, s1k], [("aT", jj)])
            if si + 1 < len(sts):
                nxt = prologue(sts[si + 1])
            for ti, t in enumerate(st):
                xt, xk = xs[ti]
                r = 1 if t < 2 else 0
                z, zk = W.z.next()
                for nh in range(2):
                    py, pk = W.py.next()
                    for k in range(NJ):
                        op("pe", lambda e: e.matmul(py[:], lhsT=aT[:, k, ti * 128:(ti + 1) * 128], rhs=wd[:, k, nh * 512:(nh + 1) * 512], start=(k == 0), stop=(k == NJ - 1)),
                           [("aT", k), f"wd{k // 11}"], [pk])
                    op("dve", lambda e: e.tensor_tensor(out=z[:, nh * 512:(nh + 1) * 512], in0=py[:], in1=mods["gt"][r][:, nh * 512:(nh + 1) * 512], op=ALU.mult),
                       [pk, f"mod_gt{r}"], [zk])
                ln_store(W, z, zk, xt, xk, mods, dst, t, final)
        S.barrier()
        es.close()

    def ln_store(W, z, zk, xt, xk, mods, dst, t, final):
        op("dve", lambda e: e.scalar_tensor_tensor(out=z[:], in0=xt[:], scalar=float(ALPHA), in1=z[:], op0=ALU.mult, op1=ALU.add), [xk, zk], [zk])
        for c in range(2):
            op("dve", lambda e: e.bn_stats(out=W.st6[:, c, :], in_=z[:, c * 512:(c + 1) * 512]), [zk], ["st6"])
        op("dve", lambda e: e.bn_aggr(out=W.mv[:], in_=W.st6[:]), ["st6"], ["mv"])
        op("act", lambda e: e.activation(out=W.rstd[:], in_=W.mv[:, 1:2], func=AF.Sqrt, bias=epsb[:, 0:1], scale=1.0), ["mv", "epsb"], ["rstd"])
        op("dve", lambda e: e.reciprocal(out=W.rstd[:], in_=W.rstd[:]), ["rstd"], ["rstd"])
        op("dve", lambda e: e.scalar_tensor_tensor(out=W.nb[:], in0=W.mv[:, 0:1], scalar=-1.0, in1=W.rstd[:], op0=ALU.mult, op1=ALU.mult), ["mv", "rstd"], ["nb"])
        op("act", lambda e: e.activation(out=z[:], in_=z[:], func=AF.Identity, bias=W.nb[:, 0:1], scale=W.rstd[:, 0:1]), [zk, "nb", "rstd"], [zk])
        op("pool", lambda e: e.tensor_tensor(out=z[:], in0=z[:], in1=mods["lng"][:], op=ALU.mult), [zk, "mod_lng"], [zk])
        op("pool", lambda e: e.tensor_tensor(out=z[:], in0=z[:], in1=mods["lnb"][:], op=ALU.add), [zk, "mod_lnb"], [zk])
        if final:
            dma("sp", io["out"][(t - 2) * 128:(t - 1) * 128, :], z[:], reads=[zk], writes=[("OUT", t)])
        else:
            dma("sp", xrows(dst, t), z[:], reads=[zk], writes=[("X", t)])

    def out_proj(W, oT, ok, wt, wk, xt, xk, gate, gk, mods, dst, t):
        z, zk = W.z.next()
        for nh in range(2):
            py, pk = W.py.next()
            for k in range(8):
                op("pe", lambda e: e.matmul(py[:], lhsT=oT[:, k, :], rhs=wt[:, k, nh * 512:(nh + 1) * 512], start=(k == 0), stop=(k == 7)),
                   [ok, wk], [pk])
            op("dve", lambda e: e.tensor_tensor(out=z[:, nh * 512:(nh + 1) * 512], in0=py[:], in1=gate[:, nh * 512:(nh + 1) * 512], op=ALU.mult),
               [pk, gk], [zk])
        ln_store(W, z, zk, xt, xk, mods, dst, t, False)

    def phase_mixer_c(l, src, dst, q_tiles):
        i = l // 2
        SC = 128 ** -0.5
        eso = ExitStack()
        sbo, pso = mk(eso)
        QT = sbo("QT", [128, NT, 8, 128], BF16)
        KT = sbo("KT", [128, 2, NTOK], BF16)
        V = sbo("V", [128, NT, 2, 130], BF16)
        op("pool", lambda e: e.memset(V[:, :, :, 128:130], 1.0), [], ["Vones"])
        es = ExitStack()
        sb, ps = mk(es)
        win = sb("win", [128, 8, 1536], BF16)
        dma("pool", win[:], io["c_w_in"][i].rearrange("(kc p) n -> p kc n", p=128), writes=["win"])
        gq = sb("gq", [128, 10, 128], F32)
        for h in range(10):
            srcg = io["c_q_norm"] if h < 8 else io["c_k_norm"]
            dma("sp", gq[:, h, :], srcg[i:i + 1, :].broadcast_to([128, 128]), writes=["gq"])
        mods = load_mods(sb, l, 1, ("sh", "sc"))
        W = Work(sb, ps, nxt=2, npy=1, ntmp=1, nhb=2, nz=0)
        hT = RB([sb(f"hTc{k}", [128, 8, 128], BF16) for k in range(2)], "hTc")
        qkv = RB([sb(f"qkv{k}", [128, 1536], F32) for k in range(2)], "qkv")
        sq = sb("sq", [128, 1280], F32)
        ss = sb("ss", [128, 10], F32)
        cs = RB([sb(f"cs{k}", [128, 2, 64], F32) for k in range(2)], "cs")
        ra = sb("ra", [128, 10, 64], F32)
        rb_ = sb("rb", [128, 10, 64], F32)
        qr = RB([sb(f"qr{k}", [128, 10, 128], BF16) for k in range(2)], "qr")
        pq = [ps(f"pq{k}", [128, 512], F32) for k in range(3)]
        ptr = ps("ptr", [128, 16, 128], BF16)
        for t in range(NT):
            r = 1 if t < 2 else 0
            xt, xk = load_x(W, src, t)
            hb, hbk = modulate(W, xt, xk, mods, r)
            h, hk = hT.next()
            transpose8(W, hb, hbk, h[:], hk)
            qv, qk = qkv.next()
            for n in range(3):
                for kc in range(8):
                    op("pe", lambda e: e.matmul(pq[n][:], lhsT=h[:, kc, :], rhs=win[:, kc, n * 512:(n + 1) * 512], start=(kc == 0), stop=(kc == 7)),
                       [hk, "win"], [f"pq{n}"])
                op("act", lambda e: e.copy(out=qv[:, n * 512:(n + 1) * 512], in_=pq[n][:]), [f"pq{n}"], [qk])
            op("act", lambda e: e.activation(out=sq[:], in_=qv[:, 0:1280], func=AF.Square), [qk], ["sq"])
            op("dve", lambda e: e.tensor_reduce(out=ss[:], in_=sq[:].rearrange("p (h d) -> p h d", d=128), axis=AX.X, op=ALU.add), ["sq"], ["ss"])
            op("act", lambda e: e.activation(out=ss[:], in_=ss[:], func=AF.Sqrt, bias=epsb[:, 0:1], scale=1.0 / 128), ["ss", "epsb"], ["ss"])
            op("dve", lambda e: e.reciprocal(out=ss[:], in_=ss[:]), ["ss"], ["ss"])
            q3 = qv[:, 0:1280].rearrange("p (h d) -> p h d", d=128)
            op("dve", lambda e: e.tensor_tensor(out=q3, in0=q3, in1=ss[:, :].unsqueeze(2).to_broadcast([128, 10, 128]), op=ALU.mult), [qk, "ss"], [qk])
            qo, qok = qr.next()
            if r == 1:
                op("dve", lambda e: e.tensor_tensor(out=qo[:], in0=q3, in1=gq[:], op=ALU.mult), [qk, "gq"], [qok])
            else:
                op("dve", lambda e: e.tensor_tensor(out=q3, in0=q3, in1=gq[:], op=ALU.mult), [qk, "gq"], [qk])
                c, ck = cs.next()
                p0 = (t - 2) * 128
                dma("sp", c[:, 0, :], io["cosC"][p0:p0 + 128, :], writes=[ck])
                dma("sp", c[:, 1, :], io["sinC"][p0:p0 + 128, :], writes=[ck])
                q4 = qv[:, 0:1280].rearrange("p (h two d) -> p h two d", two=2, d=64)
                o4 = qo[:].rearrange("p h (two d) -> p h two d", two=2)
                cosb = c[:, 0, :].unsqueeze(1).to_broadcast([128, 10, 64])
                sinb = c[:, 1, :].unsqueeze(1).to_broadcast([128, 10, 64])
                x1, x2 = q4[:, :, 0, :], q4[:, :, 1, :]
                op("dve", lambda e: e.tensor_tensor(out=ra[:], in0=x1, in1=cosb, op=ALU.mult), [qk, ck], ["ra"])
                op("dve", lambda e: e.tensor_tensor(out=rb_[:], in0=x2, in1=sinb, op=ALU.mult), [qk, ck], ["rb"])
                op("dve", lambda e: e.tensor_tensor(out=o4[:, :, 0, :], in0=ra[:], in1=rb_[:], op=ALU.subtract), ["ra", "rb"], [qok])
                op("dve", lambda e: e.tensor_tensor(out=ra[:], in0=x1, in1=sinb, op=ALU.mult), [qk, ck], ["ra"])
                op("dve", lambda e: e.tensor_tensor(out=rb_[:], in0=x2, in1=cosb, op=ALU.mult), [qk, ck], ["rb"])
                op("dve", lambda e: e.tensor_tensor(out=o4[:, :, 1, :], in0=ra[:], in1=rb_[:], op=ALU.add), ["ra", "rb"], [qok])
            for hh in range(10):
                op("pe", lambda e: e.transpose(out=ptr[:, hh, :], in_=qo[:, hh, :], identity=idb[:]), [qok, "idb"], ["ptr"])
            op("act", lambda e: e.copy(out=QT[:, t, :, :], in_=ptr[:, 0:8, :]), ["ptr"], [("QT", t)])
            op("dve", lambda e: e.tensor_copy(out=KT[:, :, t * 128:(t + 1) * 128], in_=ptr[:, 8:10, :]), ["ptr"], [("KT", t)])
            op("pool", lambda e: e.tensor_copy(out=V[:, t, :, 0:128], in_=qv[:, 1280:1536].rearrange("p (g d) -> p g d", g=2)), [qk], [("V", t)])
        S.barrier()
        es.close()
        es = ExitStack()
        sb, ps = mk(es)
        wout = sb("wout", [128, 8, 1024], BF16)
        dma("pool", wout[:], io["c_w_out"][i].rearrange("(kc p) n -> p kc n", p=128), writes=["wout"])
        mods = load_mods(sb, l, 1, ("gt", "lng", "lnb"))
        W = Work(sb, ps, nxt=3, npy=1, with_pT=False, ntmp=0, nhb=0, nz=2)
        PT = RB([sb(f"PT{k}", [128, 512], BF16) for k in range(3)], "PT")
        oT = RB([sb(f"oT{k}", [128, 8, 128], BF16) for k in range(2)], "oT")
        accs = RB([sb(f"accC{k}", [128, 512], F32) for k in range(2)], "accC")
        rdn = sb("rdn", [128, 512], F32)
        onesf = sb("onesf", [128, 128], F32)
        op("dve", lambda e: e.memset(onesf[:], 1.0), [], ["onesf"])
        pS = RB([ps(f"pS{k}", [128, 512], F32) for k in range(2)], "pS")
        pOT = [ps(f"pOT{k}", [128, 512], F32) for k in range(2)]
        pden = ps("pden", [128, 512], F32)
        for t in q_tiles:
            r = 1 if t < 2 else 0
            kts = [0, 1] if r == 1 else list(range(NT))
            ot, otk = oT.next()
            for g in range(2):
                a_, ak_ = accs.next()
                def issue_s(kt_):
                    p_, pk_ = pS.next()
                    op("pe", lambda e: e.matmul(p_[:], lhsT=KT[:, g, kt_ * 128:(kt_ + 1) * 128], rhs=QT[:, t, 4 * g:4 * g + 4, :], start=True, stop=True),
                       [("KT", kt_), ("QT", t)], [pk_])
                    return p_, pk_
                nxt_s = issue_s(kts[0])
                for ki, kt in enumerate(kts):
                    p, pk = nxt_s
                    if ki + 1 < len(kts):
                        nxt_s = issue_s(kts[ki + 1])
                    pt, ptk = PT.next()
                    op("act", lambda e: e.activation(out=pt[:], in_=p[:], func=AF.Exp, scale=float(SC)), [pk], [ptk])
                    op("pe", lambda e: e.matmul(pOT[g][:], lhsT=V[:, kt, g, 0:128], rhs=pt[:], start=(ki == 0), stop=(ki == len(kts) - 1)),
                       [ptk, ("V", kt)], [f"pOT{g}"])
                    if ki == 0:
                        op("dve", lambda e: e.tensor_copy(out=a_[:], in_=pt[:]), [ptk], [ak_])
                    else:
                        op("dve", lambda e: e.tensor_tensor(out=a_[:], in0=a_[:], in1=pt[:], op=ALU.add), [ptk, ak_], [ak_])
                op("pe", lambda e: e.matmul(pden[:], lhsT=onesf[:], rhs=a_[:], start=True, stop=True), ["onesf", ak_], ["pden"])
                op("dve", lambda e: e.reciprocal(out=rdn[:], in_=pden[:]), ["pden"], ["rdn"])
                op("dve", lambda e: e.tensor_tensor(out=ot[:, 4 * g:4 * g + 4, :], in0=pOT[g][:].rearrange("p (h q) -> p h q", h=4),
                                                    in1=rdn[:].rearrange("p (h q) -> p h q", h=4), op=ALU.mult), [f"pOT{g}", "rdn"], [otk])
            xt, xk = load_x(W, src, t)
            out_proj(W, ot, otk, wout, "wout", xt, xk, mods["gt"][r], f"mod_gt{r}", mods, dst, t)
        S.barrier()
        es.close()
        eso.close()

    io["mixer_ab"] = make_mixer_ab(nc, S, io, mk, idf, idb, epsb, (Work, load_x, modulate, transpose8, load_mods, out_proj))
    phase_mod()
    all_tiles = list(range(NT))
    lat_tiles = list(range(2, NT))
    ffn_list = [(l, j) for l in layers for j in range(2)]
    cast_wgu(ffn_list[0][0], ffn_list[0][1], 0)
    fi = 0
    cur = io["xin"]
    for li, l in enumerate(layers):
        last = (li == len(layers) - 1)
        if fi + 1 < len(ffn_list):
            cast_wgu(ffn_list[fi + 1][0], ffn_list[fi + 1][1], (fi + 1) % 2)
        phase_ffn(l, 0, cur, XS, all_tiles, fi % 2)
        fi += 1
        cur = XS
        qt = lat_tiles if (last and skip_ctx_last) else all_tiles
        if l % 2 == 1:
            phase_mixer_c(l, XS, XS, qt)
        else:
            io["mixer_ab"](l, XS, XS, qt)
        if fi + 1 < len(ffn_list):
            cast_wgu(ffn_list[fi + 1][0], ffn_list[fi + 1][1], (fi + 1) % 2)
        phase_ffn(l, 1, XS, XS, qt, fi % 2, final=last)
        fi += 1
    S.barrier()
    es0.close()


import numpy as np


def _consts():
    c = {}
    c["ident"] = np.eye(128, dtype=np.float32)

    def rope(hd):
        nf = hd // 4
        inv = (10000.0 ** (-np.arange(nf, dtype=np.float32) / nf)).astype(np.float32)
        r, col = np.meshgrid(np.arange(64, dtype=np.float32), np.arange(64, dtype=np.float32), indexing="ij")
        r, col = r.reshape(-1), col.reshape(-1)
        ang = np.concatenate([r[:, None] * inv, col[:, None] * inv], axis=-1).astype(np.float32)
        return np.cos(ang).astype(np.float32), np.sin(ang).astype(np.float32)
    c["cosC"], c["sinC"] = rope(128)
    c["cosA"], c["sinA"] = rope(64)
    j = np.arange(128)[:, None]
    i = np.arange(128)[None, :]
    c["cA"] = np.stack([(j >= i), (j <= i)], axis=1).astype(np.float32)
    j = np.arange(64)[:, None]
    i = np.arange(64)[None, :]
    c["cB"] = np.stack([(j <= i), (j >= i), -1.0 * (i > j), -1.0 * (i < j)], axis=1).astype(np.float32)
    return c


W_NAMES = ["ada_w", "ada_b", "ln_g", "ln_b", "ffn_w_gu", "ffn_w_down", "ab_w_in", "ab_conv_w", "ab_a_log",
           "ab_dt_bias", "ab_gnorm", "ab_sink", "ab_w_out", "c_w_in", "c_q_norm", "c_k_norm", "c_w_out"]


def make_program(shapes, layers=(0, 1, 2, 3), dbg=False):
    nc = bass.Bass("TRN2", target_bir_lowering=False)
    try:
        nc.allow_low_precision("bf16 matmul operands, fp32 accumulation")
    except Exception:
        pass
    io = {}
    io["xin"] = nc.dram_tensor("xin", [NTOK, D], F32, kind="ExternalInput").ap()
    io["cvecT"] = nc.dram_tensor("cvecT", [128, 8, 2], F32, kind="ExternalInput").ap()
    for k in W_NAMES:
        io[k] = nc.dram_tensor(k, list(shapes[k]), F32, kind="ExternalInput").ap()
    for k, v in _consts().items():
        io[k] = nc.dram_tensor(k, list(v.shape), F32, kind="ExternalInput").ap()
    io["out"] = nc.dram_tensor("out", [SEQ, D], F32, kind="ExternalOutput").ap()
    io["XS"] = nc.dram_tensor("XS", [NTOK, D], F32, kind="ExternalOutput" if dbg else "Internal").ap()
    io["MOD"] = nc.dram_tensor("MOD", [4, 2, 9216], F32, kind="ExternalOutput" if dbg else "Internal").ap()
    io["AQT"] = nc.dram_tensor("AQT", [NT, 64, 8, 128], BF16, kind="Internal").ap()
    io["BZ"] = nc.dram_tensor("BZ", [NTOK, 512], F32, kind="Internal").ap()
    io["GB"] = nc.dram_tensor("GB", [NTOK, 16], F32, kind="Internal").ap()
    io["BQT"] = nc.dram_tensor("BQT", [4, 128, NTOK], F32, kind="Internal").ap()
    io["BKT"] = nc.dram_tensor("BKT", [4, 128, NTOK], F32, kind="Internal").ap()
    io["BKV"] = nc.dram_tensor("BKV", [NTOK, 2, 512], F32, kind="Internal").ap()
    io["OA"] = nc.dram_tensor("OA", [NTOK, 512], BF16, kind="ExternalOutput" if dbg else "Internal").ap()
    io["OB"] = nc.dram_tensor("OB", [2, NTOK, 512], F32, kind="ExternalOutput" if dbg else "Internal").ap()
    io["WGUB"] = [nc.dram_tensor(f"WGUB{i}", [11, 128, 8, 512], BF16, kind="Internal").ap() for i in range(2)]
    S = Sched(nc)
    build(nc, S, io, layers=layers)
    return nc, S


def make_in_maps(inputs):
    x = np.asarray(inputs["x"], dtype=np.float32)
    c = np.asarray(inputs["c"], dtype=np.float32)
    ctx = np.asarray(inputs["ctx"], dtype=np.float32)
    c_ctx = np.asarray(inputs["c_ctx"], dtype=np.float32)
    consts = _consts()
    shared = {k: np.ascontiguousarray(np.asarray(inputs[k], dtype=np.float32)) for k in W_NAMES}
    shared.update(consts)
    maps = []
    for b in range(8):
        m = dict(shared)
        m["xin"] = np.ascontiguousarray(np.concatenate([ctx[b], x[b]], axis=0))
        cv = np.stack([c[b], c_ctx], axis=0)
        m["cvecT"] = np.ascontiguousarray(cv.reshape(2, 8, 128).transpose(2, 1, 0))
        maps.append(m)
    return maps


def kernel(**inputs):
    shapes = {k: np.asarray(inputs[k]).shape for k in W_NAMES}
    nc, S = make_program(shapes)
    maps = make_in_maps(inputs)
    res = run_bass_kernel_spmd(nc, maps, core_ids=list(range(8)))
    return np.stack([np.asarray(r["out"], dtype=np.float32) for r in res.results], axis=0)
```

```python
from concourse.bass_utils import run_bass_kernel_spmd
from contextlib import ExitStack
import numpy as np
import concourse.bass as bass
import concourse.mybir as mybir

F32 = mybir.dt.float32
BF16 = mybir.dt.bfloat16
AF = mybir.ActivationFunctionType
ALU = mybir.AluOpType
AX = mybir.AxisListType

ENGS = ("pe", "act", "dve", "pool", "sp")


class Sched:
    NDMA = 6

    def __init__(self, nc):
        self.nc = nc
        self.es = ExitStack()
        self.eng = {"pe": nc.tensor, "act": nc.scalar, "dve": nc.vector,
                    "pool": nc.gpsimd, "sp": nc.sync}
        self.sem = {}
        self.cnt = {}
        for e in ENGS:
            self.sem[e] = self.es.enter_context(nc.semaphore("c_" + e))
            self.cnt[e] = 0
        self.dq = {}
        for q in ("sp", "pool", "act"):
            sems = [self.es.enter_context(nc.semaphore(f"d_{q}{j}")) for j in range(self.NDMA)]
            self.dq[q] = {"sems": sems, "vals": [0] * self.NDMA, "next": 0}
        self.semh = dict(self.sem)
        for q, d in self.dq.items():
            for j, s in enumerate(d["sems"]):
                self.semh[("d", q, j)] = s
        self.seen = {e: {} for e in ENGS}
        self.res = {}
        self.nwait = 0
        self.nins = 0

    def _deps(self, e, reads, writes):
        need = {}

        def add(tok):
            if tok is None:
                return
            s, v = tok
            if e == "pe" and s == "pe":
                return
            if need.get(s, 0) < v:
                need[s] = v

        for r in reads:
            st = self.res.get(r)
            if st is not None:
                add(st["w"])
        for w in writes:
            st = self.res.get(w)
            if st is not None:
                add(st["w"])
                for s, v in st["r"].items():
                    add((s, v))
        seen = self.seen[e]
        for s, v in need.items():
            if seen.get(s, 0) < v:
                self.eng[e].wait_ge(self.semh[s], v)
                seen[s] = v
                self.nwait += 1

    def _mark(self, tok, reads, writes):
        s, v = tok
        for r in reads:
            st = self.res.setdefault(r, {"w": None, "r": {}})
            if st["r"].get(s, 0) < v:
                st["r"][s] = v
        for w in writes:
            self.res[w] = {"w": tok, "r": {}}

    def op(self, e, fn, reads=(), writes=()):
        self._deps(e, reads, writes)
        ins = fn(self.eng[e])
        self.cnt[e] += 1
        ins.then_inc(self.sem[e], 1)
        self.nins += 1
        self._mark((e, self.cnt[e]), reads, writes)
        return ins

    def dma(self, q, out, in_, reads=(), writes=(), **kw):
        d = self.dq[q]
        j = d["next"]
        d["next"] = (j + 1) % self.NDMA
        key = ("d", q, j)
        seen = self.seen[q]
        if seen.get(key, 0) < d["vals"][j]:
            self.eng[q].wait_ge(d["sems"][j], d["vals"][j])
            seen[key] = d["vals"][j]
            self.nwait += 1
        self._deps(q, reads, writes)
        ins = self.eng[q].dma_start(out=out, in_=in_, **kw)
        d["vals"][j] += 16
        ins.then_inc(d["sems"][j], 16)
        self.nins += 1
        self._mark((key, d["vals"][j]), reads, writes)
        return ins

    def barrier(self):
        for e in ENGS:
            seen = self.seen[e]
            for s in ENGS:
                if self.cnt[s] == 0:
                    continue
                if seen.get(s, 0) < self.cnt[s]:
                    self.eng[e].wait_ge(self.sem[s], self.cnt[s])
                    seen[s] = self.cnt[s]
            for q, d in self.dq.items():
                for j in range(self.NDMA):
                    key = ("d", q, j)
                    if seen.get(key, 0) < d["vals"][j]:
                        self.eng[e].wait_ge(d["sems"][j], d["vals"][j])
                        seen[key] = d["vals"][j]
        self.res = {}

    def close(self):
        self.es.close()


import os as _os
AB_STOP = int(_os.environ.get('AB_STOP', '0'))
AB_CUT = int(_os.environ.get('AB_CUT', '9'))
AB_SUB = int(_os.environ.get('AB_SUB', '9'))


def make_mixer_ab(nc, S, io, mk, idf, idb, epsb, helpers):
    from contextlib import ExitStack
    op, dma = S.op, S.dma
    Work, load_x, modulate, transpose8, load_mods, out_proj = helpers
    NCH = NTOK // 64

    def phase(l, src, dst, q_tiles):
        i = l // 2
        win = io["ab_w_in"][i].rearrange("(kc p) n -> p kc n", p=128)
        eso = ExitStack()
        sbo, pso = mk(eso)
        AKT = sbo("AKT", [64, 2, NTOK], BF16)
        AV = sbo("AV", [128, NT, 2, 128], BF16)
        op("pool", lambda e: e.memset(AV[:, :, :, 64:128], 1.0), [], ["AVones"])
        esm = ExitStack()
        sbm, psm = mk(esm)
        hTall = sbm("hTall", [128, 8, NTOK], BF16)
        es = ExitStack()
        sb, ps = mk(es)
        w1 = sb("w1", [128, 8, 1296], BF16)
        dma("pool", w1[:, :, 0:768], win[:, :, 0:768], writes=["w1a"])
        dma("pool", w1[:, :, 768:1296], win[:, :, 2304:2832], writes=["w1b"])
        mods = load_mods(sb, l, 1, ("sh", "sc"))
        W = Work(sb, ps, nxt=2, npy=0, ntmp=1, nhb=2, nz=0)
        dtb = sb("dtb", [128, 8], F32)
        nea = sb("nea", [128, 8], F32)
        one1 = sb("one1", [128, 1], F32)
        op("dve", lambda e: e.memset(one1[:], 1.0), [], ["one1"])
        dma("sp", dtb[:], io["ab_dt_bias"][i:i + 1].rearrange("o a b -> o (a b)").broadcast_to([128, 8]), writes=["dtb"])
        dma("sp", nea[:], io["ab_a_log"][i:i + 1].rearrange("o a b -> o (a b)").broadcast_to([128, 8]), writes=["nea"])
        op("act", lambda e: e.activation(out=nea[:], in_=nea[:], func=AF.Exp), ["nea"], ["nea"])
        op("dve", lambda e: e.tensor_scalar_mul(out=nea[:], in0=nea[:], scalar1=-1.0), ["nea"], ["nea"])
        qa = RB([sb(f"qa{k}", [128, 640], F32) for k in range(2)], "qa")
        qra = RB([sb(f"qra{k}", [128, 10, 64], BF16) for k in range(2)], "qra")
        csa = RB([sb(f"csa{k}", [128, 2, 32], F32) for k in range(2)], "csa")
        ra = sb("raA", [128, 10, 32], F32)
        rb_ = sb("rbA", [128, 10, 32], F32)
        aqt = RB([sb(f"aqt{k}", [64, 8, 128], BF16) for k in range(2)], "aqt")
        zs = RB([sb(f"zs{k}", [128, 512], F32) for k in range(2)], "zs")
        gbt = RB([sb(f"gbt{k}", [128, 16], F32) for k in range(2)], "gbt")
        gtmp = sb("gtmp", [128, 8], F32)
        pa0 = ps("pa0", [128, 512], F32)
        pa1f = ps("pa1", [128, 512], F32)
        pa1 = pa1f[:, 0:256]
        pz = ps("pz", [128, 512], F32)
        pgtf = ps("pgt", [128, 512], F32)
        pgt = pgtf[:, 0:16]
        ptrA = ps("ptrA", [64, 16, 128], BF16)
        for t in range(NT):
            r = 1 if t < 2 else 0
            xt, xk = load_x(W, src, t)
            hb, hbk = modulate(W, xt, xk, mods, r)
            transpose8(W, hb, hbk, hTall[:, :, t * 128:(t + 1) * 128], ("hT", t))
            h = hTall[:, :, t * 128:(t + 1) * 128]
            for (pp, pk, c0, c1, wk) in ((pa0, "pa0", 0, 512, "w1a"), (pa1, "pa1", 512, 768, "w1a"), (pz, "pz", 768, 1280, "w1b"), (pgt, "pgt", 1280, 1296, "w1b")):
                for kc in range(8):
                    op("pe", lambda e: e.matmul(pp[:], lhsT=h[:, kc, :], rhs=w1[:, kc, c0:c1], start=(kc == 0), stop=(kc == 7)), [("hT", t), wk], [pk])
            if AB_CUT < 2:
                continue
            q, qk = qa.next()
            op("act", lambda e: e.copy(out=q[:, 0:512], in_=pa0[:]), ["pa0"], [qk])
            op("act", lambda e: e.copy(out=q[:, 512:640], in_=pa1[:, 0:128]), ["pa1"], [qk])
            op("act", lambda e: e.copy(out=AV[:, t, :, 0:64], in_=pa1[:, 128:256].rearrange("p (g d) -> p g d", g=2)), ["pa1"], [("AV", t)])
            if AB_SUB < 2:
                continue
            qo, qok = qra.next()
            q3 = q[:].rearrange("p (h d) -> p h d", d=64)
            if r == 1:
                op("dve", lambda e: e.tensor_copy(out=qo[:], in_=q3), [qk], [qok])
            else:
                c, ck = csa.next()
                p0 = (t - 2) * 128
                dma("sp", c[:, 0, :], io["cosA"][p0:p0 + 128, :], writes=[ck])
                dma("sp", c[:, 1, :], io["sinA"][p0:p0 + 128, :], writes=[ck])
                q4 = q[:].rearrange("p (h two d) -> p h two d", two=2, d=32)
                o4 = qo[:].rearrange("p h (two d) -> p h two d", two=2)
                cosb = c[:, 0, :].unsqueeze(1).to_broadcast([128, 10, 32])
                sinb = c[:, 1, :].unsqueeze(1).to_broadcast([128, 10, 32])
                x1, x2 = q4[:, :, 0, :], q4[:, :, 1, :]
                op("dve", lambda e: e.tensor_tensor(out=ra[:], in0=x1, in1=cosb, op=ALU.mult), [qk, ck], ["raA"])
                op("dve", lambda e: e.tensor_tensor(out=rb_[:], in0=x2, in1=sinb, op=ALU.mult), [qk, ck], ["rbA"])
                op("dve", lambda e: e.tensor_tensor(out=o4[:, :, 0, :], in0=ra[:], in1=rb_[:], op=ALU.subtract), ["raA", "rbA"], [qok])
                op("dve", lambda e: e.tensor_tensor(out=ra[:], in0=x1, in1=sinb, op=ALU.mult), [qk, ck], ["raA"])
                op("dve", lambda e: e.tensor_tensor(out=rb_[:], in0=x2, in1=cosb, op=ALU.mult), [qk, ck], ["rbA"])
                op("dve", lambda e: e.tensor_tensor(out=o4[:, :, 1, :], in0=ra[:], in1=rb_[:], op=ALU.add), ["raA", "rbA"], [qok])
            if AB_SUB < 3:
                continue
            for hh in range(10):
                op("pe", lambda e: e.transpose(out=ptrA[:, hh, :], in_=qo[:, hh, :], identity=idb[:]), [qok, "idb"], ["ptrA"])
            a, ak = aqt.next()
            op("act", lambda e: e.copy(out=a[:], in_=ptrA[:, 0:8, :]), ["ptrA"], [ak])
            dma("sp", io["AQT"][t], a[:], reads=[ak], writes=[("AQT", t)])
            op("dve", lambda e: e.tensor_copy(out=AKT[:, :, t * 128:(t + 1) * 128], in_=ptrA[:, 8:10, :]), ["ptrA"], [("AKT", t)])
            if AB_CUT < 3:
                continue
            z, zk = zs.next()
            op("act", lambda e: e.activation(out=z[:], in_=pz[:], func=AF.Silu), ["pz"], [zk])
            dma("sp", io["BZ"][t * 128:(t + 1) * 128, :], z[:], reads=[zk], writes=[("BZ", t)])
            if AB_CUT < 4:
                continue
            gb, gbk = gbt.next()
            op("act", lambda e: e.activation(out=gb[:, 8:16], in_=pgt[:, 0:8], func=AF.Sigmoid), ["pgt"], [gbk])
            op("dve", lambda e: e.tensor_tensor(out=gtmp[:], in0=pgt[:, 8:16], in1=dtb[:], op=ALU.add), ["pgt", "dtb"], ["gtmp"])
            op("act", lambda e: e.activation(out=gtmp[:], in_=gtmp[:], func=AF.Exp), ["gtmp"], ["gtmp"])
            op("act", lambda e: e.activation(out=gtmp[:], in_=gtmp[:], func=AF.Ln, bias=one1[:, 0:1], scale=1.0), ["gtmp", "one1"], ["gtmp"])
            op("dve", lambda e: e.tensor_tensor(out=gb[:, 0:8], in0=gtmp[:], in1=nea[:], op=ALU.mult), ["gtmp", "nea"], [gbk])
            dma("sp", io["GB"][t * 128:(t + 1) * 128, :], gb[:], reads=[gbk], writes=[("GB", t)])
        S.barrier()
        es.close()
        if AB_STOP == 1:
            esm.close(); eso.close(); return
        es = ExitStack()
        sb, ps = mk(es)
        w2 = sb("w2", [128, 8, 1536], BF16)
        dma("pool", w2[:], win[:, :, 768:2304], writes=["w2"])
        cw = sb("cw", [128, 12, 5], F32)
        cwv = io["ab_conv_w"][i].rearrange("k (cc p) -> p cc k", p=128)
        for cc in range(12):
            dma("sp", cw[:, cc, :], cwv[:, cc, :], writes=["cw"], allow_slow_non_contiguous=True)
        onesb = sb("onesb", [128, 128], BF16)
        op("dve", lambda e: e.memset(onesb[:], 1.0), [], ["onesb"])
        RAWW = 4360
        raw = sb("raw", [128, RAWW], F32)
        acc = sb("acc", [128, RAWW], F32)
        sqb = sb("sqb", [128, RAWW], BF16)
        op("pool", lambda e: e.memset(raw[:], 0.0), [], ["raw"])
        fm = RB([sb(f"fm{k}", [128, 512], F32) for k in range(2)], "fm")
        rn = RB([sb(f"rn{k}", [128, 512], F32) for k in range(2)], "rn")
        tk = RB([sb(f"tk{k}", [128, 4, 128], F32) for k in range(2)], "tk")
        pc = RB([ps(f"pc{k}", [128, 512], F32) for k in range(2)], "pc")
        pn = RB([ps(f"pn{k}", [128, 512], F32) for k in range(2)], "pn")
        ptk = RB([ps(f"ptk{k}", [128, 4, 128], F32) for k in range(2)], "ptk")
        blocks = [(0, 256, 2)] + [(256 + b * 512, 512, 256 + b * 512 + 6) for b in range(8)]
        for cc in range(12):
            kind, hd = ("q", "k", "v")[cc // 4], cc % 4
            for (tok0, n, rc) in blocks:
                p, pk = pc.next()
                for kc in range(8):
                    op("pe", lambda e: e.matmul(p[:, :n], lhsT=w2[:, kc, cc * 128:(cc + 1) * 128], rhs=hTall[:, kc, tok0:tok0 + n], start=(kc == 0), stop=(kc == 7)),
                       ["w2"], [pk])
                op("act", lambda e: e.copy(out=raw[:, rc:rc + n], in_=p[:, :n]), [pk], ["raw"])
            lo, hi = 2, RAWW - 2
            op("dve", lambda e: e.tensor_scalar_mul(out=acc[:, lo:hi], in0=raw[:, lo - 2:hi - 2], scalar1=cw[:, cc, 0:1]), ["raw", "cw"], ["acc"])
            for k in range(1, 5):
                op("dve", lambda e: e.scalar_tensor_tensor(out=acc[:, lo:hi], in0=raw[:, lo + k - 2:hi + k - 2], scalar=cw[:, cc, k:k + 1], in1=acc[:, lo:hi],
                                                            op0=ALU.mult, op1=ALU.add), ["raw", "cw", "acc"], ["acc"])
            op("act", lambda e: e.activation(out=acc[:, lo:hi], in_=acc[:, lo:hi], func=AF.Silu), ["acc"], ["acc"])
            if kind != "v":
                op("act", lambda e: e.activation(out=sqb[:, lo:hi], in_=acc[:, lo:hi], func=AF.Square), ["acc"], ["sqb"])
            for (tok0, n, rc) in blocks:
                if kind != "v":
                    p, pk = pn.next()
                    op("pe", lambda e: e.matmul(p[:, :n], lhsT=onesb[:], rhs=sqb[:, rc:rc + n], start=True, stop=True), ["onesb", "sqb"], [pk])
                    r_, rk = rn.next()
                    op("act", lambda e: e.activation(out=r_[:, :n], in_=p[:, :n], func=AF.Sqrt, bias=epsb[:, 0:1], scale=1.0), [pk, "epsb"], [rk])
                    op("dve", lambda e: e.reciprocal(out=r_[:, :n], in_=r_[:, :n]), [rk], [rk])
                    f, fk = fm.next()
                    sc_ = float(128 ** -0.5) if kind == "q" else 1.0
                    op("dve", lambda e: e.scalar_tensor_tensor(out=f[:, :n], in0=acc[:, rc:rc + n], scalar=sc_, in1=r_[:, :n], op0=ALU.mult, op1=ALU.mult),
                       ["acc", rk], [fk])
                    dst_fm = io["BQT"] if kind == "q" else io["BKT"]
                    dma("sp", dst_fm[hd, :, tok0:tok0 + n], f[:, :n], reads=[fk], writes=[("BFM", cc)])
                    srcT, srck = f, fk
                    off = 0
                else:
                    srcT, srck = acc, "acc"
                    off = rc
                if kind != "q":
                    nt_ = n // 128
                    pt_, ptk_ = ptk.next()
                    for j in range(nt_):
                        op("pe", lambda e: e.transpose(out=pt_[:, j, :], in_=srcT[:, off + j * 128:off + (j + 1) * 128], identity=idf[:]), [srck, "idf"], [ptk_])
                    tt, ttk = tk.next()
                    op("dve" if kind == "k" else "act", (lambda e: e.tensor_copy(out=tt[:, :nt_, :], in_=pt_[:, :nt_, :])) if kind == "k" else
                       (lambda e: e.copy(out=tt[:, :nt_, :], in_=pt_[:, :nt_, :])), [ptk_], [ttk])
                    kvi = 0 if kind == "k" else 1
                    dma("sp", io["BKV"][tok0:tok0 + n, kvi, hd * 128:(hd + 1) * 128].rearrange("(j p) d -> p j d", p=128), tt[:, :nt_, :],
                        reads=[ttk], writes=[("BKVw", cc)])
        S.barrier()
        es.close()
        esm.close()
        if AB_STOP == 2:
            eso.close(); return
        es = ExitStack()
        sb, ps = mk(es)
        mk32 = sb("mk32", [128, 2, 128], F32)
        mkb = sb("mkb", [128, 2, 128], BF16)
        dma("sp", mk32[:], io["cA"][:, :, :], writes=["mk32"])
        op("dve", lambda e: e.tensor_copy(out=mkb[:], in_=mk32[:]), ["mk32"], ["mkb"])
        mk4 = sb("mk4", [128, 2, 4, 128], BF16)
        for mi_ in range(2):
            for hh_ in range(4):
                op("dve", lambda e: e.tensor_copy(out=mk4[:, mi_, hh_, :], in_=mk32[:, mi_, :]), ["mk32"], ["mk4"])
        esink = sb("esink", [128, 8], F32)
        dma("sp", esink[:], io["ab_sink"][i:i + 1, :].broadcast_to([128, 8]), writes=["esink"])
        op("act", lambda e: e.activation(out=esink[:], in_=esink[:], func=AF.Exp), ["esink"], ["esink"])
        aq = RB([sb(f"aq{k}", [64, 8, 128], BF16) for k in range(2)], "aq")
        PT = RB([sb(f"PTa{k}", [128, 4, 128], BF16) for k in range(3)], "PTa")
        oa = RB([sb(f"oa{k}", [128, 8, 64], BF16) for k in range(2)], "oa")
        den = sb("den", [128, 4], F32)
        pS = RB([ps(f"pSa{k}", [128, 512], F32) for k in range(2)], "pSa")
        pO = [ps(f"pOa{k}", [128, 4, 128], F32) for k in range(2)]
        for t in q_tiles:
            a, ak = aq.next()
            dma("sp", a[:], io["AQT"][t], reads=[("AQT", t)], writes=[ak])
            if t < 2:
                kl = [(0, None), (1, None)]
            else:
                kl = []
                if t - 1 >= 2:
                    kl.append((t - 1, 0))
                kl.append((t, None))
                if t + 1 < NT:
                    kl.append((t + 1, 1))
                kl += [(0, None), (1, None)]
            o, ok = oa.next()
            for g in range(2):
                def issue_s(kt_):
                    p_, pk_ = pS.next()
                    op("pe", lambda e: e.matmul(p_[:], lhsT=AKT[:, g, kt_ * 128:(kt_ + 1) * 128], rhs=a[:, 4 * g:4 * g + 4, :], start=True, stop=True),
                       [("AKT", kt_), ak], [pk_])
                    return p_, pk_
                nxt_s = issue_s(kl[0][0])
                for ki, (kt, mi) in enumerate(kl):
                    p, pk = nxt_s
                    if ki + 1 < len(kl):
                        nxt_s = issue_s(kl[ki + 1][0])
                    pt, ptk_ = PT.next()
                    op("act", lambda e: e.activation(out=pt[:].rearrange("p h q -> p (h q)"), in_=p[:], func=AF.Exp, scale=0.125), [pk], [ptk_])
                    if mi is not None:
                        op("dve", lambda e: e.tensor_tensor(out=pt[:], in0=pt[:], in1=mk4[:, mi, :, :], op=ALU.mult), [ptk_, "mk4"], [ptk_])
                    for hh in range(4):
                        op("pe", lambda e: e.matmul(pO[g][:, hh, 0:65], lhsT=pt[:, hh, :], rhs=AV[:, kt, g, 0:65], start=(ki == 0 and hh == 0), stop=(ki == len(kl) - 1), skip_group_check=True),
                           [ptk_, ("AV", kt), "AVones"], [f"pOa{g}"])
                op("dve", lambda e: e.tensor_tensor(out=den[:], in0=pO[g][:, :, 64], in1=esink[:, 4 * g:4 * g + 4], op=ALU.add), [f"pOa{g}", "esink"], ["den"])
                op("dve", lambda e: e.reciprocal(out=den[:], in_=den[:]), ["den"], ["den"])
                op("dve", lambda e: e.tensor_tensor(out=o[:, 4 * g:4 * g + 4, :], in0=pO[g][:, :, 0:64], in1=den[:, :].unsqueeze(2).to_broadcast([128, 4, 64]), op=ALU.mult),
                   [f"pOa{g}", "den"], [ok])
            dma("sp", io["OA"][t * 128:(t + 1) * 128, :], o[:].rearrange("p h d -> p (h d)"), reads=[ok], writes=[("OA", t)])
        S.barrier()
        es.close()
        eso.close()
        if AB_STOP == 3:
            return
        es = ExitStack()
        sb, ps = mk(es)
        cB = sb("cB", [64, 4, 64], F32)
        dma("sp", cB[:], io["cB"][:, :, :], writes=["cB"])
        ones64 = sb("ones64", [64, 128], F32)
        op("dve", lambda e: e.memset(ones64[:], 1.0), [], ["ones64"])
        Sst = [sb(f"Sst{d}", [128, 4, 128], F32) for d in range(2)]
        for d in range(2):
            op("pool", lambda e: e.memset(Sst[d][:], 0.0), [], [f"S{d}"])
        NB = 6
        def rb(name, shape, n=NB):
            return RB([sb(f"{name}{k}", shape, F32) for k in range(n)], name)
        kT4 = rb("kT4", [128, 4, 64]); qT4 = rb("qT4", [128, 4, 64]); kv = rb("kvB", [64, 2, 512]); gbb = rb("gbB", [64, 16])
        Rm = rb("Rm", [64, 4, 64]); gcs = rb("gcs", [64, 4]); gls = rb("gls", [128, 4]); egc = rb("egc", [64, 4]); ekt = rb("ekt", [64, 4]); egl = rb("egl", [128, 4])
        Dm = rb("Dm", [64, 4, 64]); A0 = rb("A0", [64, 4, 64]); AT = rb("AT", [64, 4, 64]); Q = rb("Qm", [64, 4, 64]); Qfin = rb("Qfin", [64, 4, 64]); attT = rb("attT", [64, 4, 64])
        tmpv = rb("tmpv", [64, 4, 128]); rr = rb("rr", [64, 4, 128]); vnew = rb("vnew", [64, 4, 128]); ktok = rb("ktok", [64, 4, 128]); ob = rb("obB", [64, 4, 128])
        pb = RB([ps(f"pb{k}", [128, 512], F32) for k in range(8)], "pb")

        def P64(p):
            return p[0:64, 0:256].rearrange("p (h i) -> p h i", h=4)

        def P64w(p):
            return p[0:64, :].rearrange("p (h i) -> p h i", h=4)

        def prep(c, d, R_):
            tri = cB[:, d, :]
            nstr = cB[:, 2 + d, :]
            k4, k4k = kT4.next(); q4, q4k = qT4.next(); kvt, kvk = kv.next(); g_, gk = gbb.next()
            c0 = c * 64
            dma("sp", k4[:], io["BKT"][:, :, c0:c0 + 64].rearrange("h d t -> d h t"), writes=[k4k])
            dma("sp", q4[:], io["BQT"][:, :, c0:c0 + 64].rearrange("h d t -> d h t"), writes=[q4k])
            dma("sp", kvt[:], io["BKV"][c0:c0 + 64, :, :], writes=[kvk])
            dma("sp", g_[:], io["GB"][c0:c0 + 64, :], writes=[gk])
            g4 = g_[:, 4 * d:4 * d + 4]
            b4 = g_[:, 8 + 4 * d:8 + 4 * d + 4]
            R, Rk = Rm.next()
            op("dve", lambda e: e.tensor_tensor(out=R[:], in0=tri.unsqueeze(1).to_broadcast([64, 4, 64]), in1=g4.unsqueeze(2).to_broadcast([64, 4, 64]), op=ALU.mult),
               ["cB", gk], [Rk])
            pgr, pgrk = pb.next()
            op("pe", lambda e: e.matmul(pgr[0:64, 0:256], lhsT=ones64[:, 0:64], rhs=R[:].rearrange("p h i -> p (h i)"), start=True, stop=True), ["ones64", Rk], [pgrk])
            psm, psmk = pb.next()
            op("pe", lambda e: e.matmul(psm[0:64, 0:4], lhsT=tri, rhs=g4, start=True, stop=True), ["cB", gk], [psmk])
            op("pe", lambda e: e.matmul(psm[:, 4:8], lhsT=ones64[:, :], rhs=g4, start=True, stop=True), ["ones64", gk], [psmk])
            gc, gck = gcs.next(); gl, glk = gls.next(); eg, egk = egc.next(); ek, ekk = ekt.next(); el, elk = egl.next()
            op("dve", lambda e: e.tensor_copy(out=gc[:], in_=psm[0:64, 0:4]), [psmk], [gck])
            op("dve", lambda e: e.tensor_copy(out=gl[:], in_=psm[:, 4:8]), [psmk], [glk])
            op("act", lambda e: e.activation(out=eg[:], in_=gc[:], func=AF.Exp), [gck], [egk])
            op("dve", lambda e: e.tensor_tensor(out=ek[:], in0=gl[0:64, :], in1=gc[:], op=ALU.subtract), [glk, gck], [ekk])
            op("act", lambda e: e.activation(out=ek[:], in_=ek[:], func=AF.Exp), [ekk], [ekk])
            op("act", lambda e: e.activation(out=el[:], in_=gl[:], func=AF.Exp), [glk], [elk])
            D_, Dk = Dm.next()
            op("dve", lambda e: e.tensor_tensor(out=D_[:], in0=P64(pgr), in1=gc[:, :].unsqueeze(2).to_broadcast([64, 4, 64]), op=ALU.subtract), [pgrk, gck], [Dk])
            op("dve", lambda e: e.tensor_scalar_min(out=D_[:], in0=D_[:], scalar1=0.0), [Dk], [Dk])
            op("act", lambda e: e.activation(out=D_[:], in_=D_[:], func=AF.Exp), [Dk], [Dk])
            op("dve", lambda e: e.tensor_tensor(out=D_[:], in0=D_[:], in1=tri.unsqueeze(1).to_broadcast([64, 4, 64]), op=ALU.mult), [Dk, "cB"], [Dk])
            yield
            pkk, pkkk = pb.next()
            pqk, pqkk = pb.next()
            for h in range(4):
                op("pe", lambda e: e.matmul(pkk[0:64, h * 64:(h + 1) * 64], lhsT=k4[:, h, :], rhs=k4[:, h, :], start=True, stop=True), [k4k], [pkkk])
            for h in range(4):
                op("pe", lambda e: e.matmul(pqk[0:64, h * 64:(h + 1) * 64], lhsT=k4[:, h, :], rhs=q4[:, h, :], start=True, stop=True), [k4k, q4k], [pqkk])
            a0, a0k = A0.next()
            op("dve", lambda e: e.tensor_tensor(out=a0[:], in0=P64(pkk), in1=D_[:], op=ALU.mult), [pkkk, Dk], [a0k])
            op("dve", lambda e: e.tensor_tensor(out=a0[:], in0=a0[:], in1=nstr.unsqueeze(1).to_broadcast([64, 4, 64]), op=ALU.mult), [a0k, "cB"], [a0k])
            op("dve", lambda e: e.tensor_tensor(out=a0[:], in0=a0[:], in1=b4.unsqueeze(2).to_broadcast([64, 4, 64]), op=ALU.mult), [a0k, gk], [a0k])
            at_, atk = attT.next()
            op("dve", lambda e: e.tensor_tensor(out=at_[:], in0=P64(pqk), in1=D_[:], op=ALU.mult), [pqkk, Dk], [atk])
            ptt, pttk = pb.next()
            for h in range(4):
                op("pe", lambda e: e.transpose(out=ptt[0:64, h * 64:(h + 1) * 64], in_=a0[:, h, :], identity=idf[0:64, 0:64]), [a0k, "idf"], [pttk])
            aT_, aTk = AT.next()
            op("act", lambda e: e.copy(out=aT_[:], in_=P64(ptt)), [pttk], [aTk])
            q_, qk_ = Q.next()
            op("dve", lambda e: e.tensor_tensor(out=q_[:], in0=a0[:], in1=idf[0:64, 0:64].unsqueeze(1).to_broadcast([64, 4, 64]), op=ALU.add), [a0k, "idf"], [qk_])
            yield
            am, amk, amT, amTk = a0, a0k, aT_, aTk
            for m in range(1, 6):
                pAT, pATk = pb.next()
                for h in range(4):
                    op("pe", lambda e: e.matmul(pAT[0:64, h * 64:(h + 1) * 64], lhsT=am[:, h, :], rhs=amT[:, h, :], start=True, stop=True), [amk, amTk], [pATk])
                if m < 5:
                    pA, pAk = pb.next()
                    for h in range(4):
                        op("pe", lambda e: e.matmul(pA[0:64, h * 64:(h + 1) * 64], lhsT=amT[:, h, :], rhs=am[:, h, :], start=True, stop=True), [amk, amTk], [pAk])
                nT, nTk = AT.next()
                op("act", lambda e: e.copy(out=nT[:], in_=P64(pAT)), [pATk], [nTk])
                if m < 5:
                    nA, nAk = A0.next()
                    op("dve", lambda e: e.tensor_copy(out=nA[:], in_=P64(pA)), [pAk], [nAk])
                else:
                    nA, nAk = None, None
                pQ, pQk = pb.next()
                for h in range(4):
                    op("pe", lambda e: e.matmul(pQ[0:64, h * 64:(h + 1) * 64], lhsT=nT[:, h, :], rhs=q_[:, h, :], start=True, stop=True), [nTk, qk_], [pQk])
                nq, nqk = (Q.next() if m < 5 else Qfin.next())
                op("dve", lambda e: e.tensor_tensor(out=nq[:], in0=P64(pQ), in1=q_[:], op=ALU.add), [pQk, qk_], [nqk])
                q_, qk_ = nq, nqk
                am, amk, amT, amTk = nA, nAk, nT, nTk
                yield
            R_.update(dict(k4=k4, k4k=k4k, q4=q4, q4k=q4k, kvt=kvt, kvk=kvk, gk=gk, b4=b4, eg=eg, egk=egk, ek=ek, ekk=ekk, el=el, elk=elk, at_=at_, atk=atk, q_=q_, qk_=qk_))

        def scan(R_, c, d):
            k4, k4k, q4, q4k, kvt, kvk, gk, b4 = R_["k4"], R_["k4k"], R_["q4"], R_["q4k"], R_["kvt"], R_["kvk"], R_["gk"], R_["b4"]
            eg, egk, ek, ekk, el, elk, at_, atk, q_, qk_ = R_["eg"], R_["egk"], R_["ek"], R_["ekk"], R_["el"], R_["elk"], R_["at_"], R_["atk"], R_["q_"], R_["qk_"]
            c0 = c * 64
            Sd, Sk = Sst[d], f"S{d}"
            pks, pksk = pb.next()
            for h in range(4):
                op("pe", lambda e: e.matmul(pks[0:64, h * 128:(h + 1) * 128], lhsT=k4[:, h, :], rhs=Sd[:, h, :], start=True, stop=True), [k4k, Sk], [pksk])
            pqs, pqsk = pb.next()
            for h in range(4):
                op("pe", lambda e: e.matmul(pqs[0:64, h * 128:(h + 1) * 128], lhsT=q4[:, h, :], rhs=Sd[:, h, :], start=True, stop=True), [q4k, Sk], [pqsk])
            tv, tvk = tmpv.next()
            op("dve", lambda e: e.tensor_tensor(out=tv[:], in0=P64w(pks), in1=eg[:, :].unsqueeze(2).to_broadcast([64, 4, 128]), op=ALU.mult), [pksk, egk], [tvk])
            r_, rk = rr.next()
            v4 = kvt[:, 1, :].rearrange("p (h d) -> p h d", h=4)
            kk4 = kvt[:, 0, :].rearrange("p (h d) -> p h d", h=4)
            op("dve", lambda e: e.tensor_tensor(out=r_[:], in0=v4, in1=tv[:], op=ALU.subtract), [kvk, tvk], [rk])
            o_, ok_ = ob.next()
            op("dve", lambda e: e.tensor_tensor(out=o_[:], in0=P64w(pqs), in1=eg[:, :].unsqueeze(2).to_broadcast([64, 4, 128]), op=ALU.mult), [pqsk, egk], [ok_])
            kt_, ktk = ktok.next()
            op("dve", lambda e: e.tensor_tensor(out=kt_[:], in0=kk4, in1=ek[:, :].unsqueeze(2).to_broadcast([64, 4, 128]), op=ALU.mult), [kvk, ekk], [ktk])
            yield
            pv, pvk = pb.next()
            for h in range(4):
                op("pe", lambda e: e.matmul(pv[0:64, h * 128:(h + 1) * 128], lhsT=q_[:, h, :], rhs=r_[:, h, :], start=True, stop=True), [qk_, rk], [pvk])
            vn, vnk = vnew.next()
            op("dve", lambda e: e.tensor_tensor(out=vn[:], in0=P64w(pv), in1=b4.unsqueeze(2).to_broadcast([64, 4, 128]), op=ALU.mult), [pvk, gk], [vnk])
            yield
            pav, pavk = pb.next()
            for h in range(4):
                op("pe", lambda e: e.matmul(pav[0:64, h * 128:(h + 1) * 128], lhsT=at_[:, h, :], rhs=vn[:, h, :], start=True, stop=True), [atk, vnk], [pavk])
            psn, psnk = pb.next()
            for h in range(4):
                op("pe", lambda e: e.matmul(psn[:, h * 128:(h + 1) * 128], lhsT=kt_[:, h, :], rhs=vn[:, h, :], start=True, stop=True), [ktk, vnk], [psnk])
            op("dve", lambda e: e.tensor_tensor(out=o_[:], in0=P64w(pav), in1=o_[:], op=ALU.add), [pavk, ok_], [ok_])
            dma("pool", io["OB"][d, c0:c0 + 64, :], o_[:].rearrange("p h d -> p (h d)"), reads=[ok_], writes=[("OB", d, c)])
            op("dve", lambda e: e.tensor_tensor(out=Sd[:], in0=Sd[:], in1=el[:, :].unsqueeze(2).to_broadcast([128, 4, 128]), op=ALU.mult), [Sk, elk], [Sk])
            op("dve", lambda e: e.tensor_tensor(out=Sd[:], in0=psn[:, :].rearrange("p (h d) -> p h d", h=4), in1=Sd[:], op=ALU.add), [psnk, Sk], [Sk])
            yield

        def lockstep(gens):
            live = list(gens)
            while live:
                nxt = []
                for g_ in live:
                    try:
                        next(g_)
                        nxt.append(g_)
                    except StopIteration:
                        pass
                live = nxt

        order_f = list(range(0, 4)) + list(range(4, NCH))
        order_b = list(range(3, -1, -1)) + list(range(NCH - 1, 3, -1))
        PR = {}
        PR[(0, 0)] = {}; PR[(0, 1)] = {}
        lockstep([prep(order_f[0], 0, PR[(0, 0)]), prep(order_b[0], 1, PR[(0, 1)])])
        for s_ in range(NCH):
            gens = [scan(PR[(s_, 0)], order_f[s_], 0), scan(PR[(s_, 1)], order_b[s_], 1)]
            if s_ + 1 < NCH:
                PR[(s_ + 1, 0)] = {}; PR[(s_ + 1, 1)] = {}
                gens += [prep(order_f[s_ + 1], 0, PR[(s_ + 1, 0)]), prep(order_b[s_ + 1], 1, PR[(s_ + 1, 1)])]
            lockstep(gens)
            PR.pop((s_, 0)); PR.pop((s_, 1))
        S.barrier()
        es.close()
        if AB_STOP == 4:
            return
        es = ExitStack()
        sb, ps = mk(es)
        wout = sb("woutab", [128, 8, 1024], BF16)
        dma("pool", wout[:], io["ab_w_out"][i].rearrange("(kc p) n -> p kc n", p=128), writes=["woutab"])
        mods = load_mods(sb, l, 1, ("gt", "lng", "lnb"))
        W = Work(sb, ps, nxt=3, npy=2, with_pT=True, ntmp=0, nhb=0, nz=2)
        gn = sb("gn", [128, 128], F32)
        dma("sp", gn[:], io["ab_gnorm"][i:i + 1, :].broadcast_to([128, 128]), writes=["gn"])
        of = RB([sb(f"of{k}", [128, 512], F32) for k in range(2)], "of")
        obk = RB([sb(f"obk{k}", [128, 512], F32) for k in range(2)], "obk")
        zz = RB([sb(f"zz{k}", [128, 512], F32) for k in range(2)], "zz")
        sq5 = sb("sq5", [128, 512], F32)
        ss5 = sb("ss5", [128, 4], F32)
        oc = RB([sb(f"oc{k}", [128, 1024], BF16) for k in range(2)], "oc")
        oT = RB([sb(f"oT5{k}", [128, 8, 128], BF16) for k in range(2)], "oT5")
        for t in q_tiles:
            r = 1 if t < 2 else 0
            rows = slice(t * 128, (t + 1) * 128)
            f_, fk = of.next(); b_, bk = obk.next(); z_, zk = zz.next(); o_, ok_ = oc.next()
            dma("sp", f_[:], io["OB"][0, rows, :], writes=[fk])
            dma("sp", b_[:], io["OB"][1, rows, :], writes=[bk])
            dma("sp", z_[:], io["BZ"][rows, :], writes=[zk])
            dma("sp", o_[:, 0:512], io["OA"][rows, :], writes=[ok_ + "a"])
            op("dve", lambda e: e.tensor_tensor(out=f_[:], in0=f_[:], in1=b_[:], op=ALU.add), [fk, bk], [fk])
            op("act", lambda e: e.activation(out=sq5[:], in_=f_[:], func=AF.Square), [fk], ["sq5"])
            op("dve", lambda e: e.tensor_reduce(out=ss5[:], in_=sq5[:].rearrange("p (h d) -> p h d", d=128), axis=AX.X, op=ALU.add), ["sq5"], ["ss5"])
            op("act", lambda e: e.activation(out=ss5[:], in_=ss5[:], func=AF.Sqrt, bias=epsb[:, 0:1], scale=1.0 / 128), ["ss5", "epsb"], ["ss5"])
            op("dve", lambda e: e.reciprocal(out=ss5[:], in_=ss5[:]), ["ss5"], ["ss5"])
            f3 = f_[:].rearrange("p (h d) -> p h d", d=128)
            op("dve", lambda e: e.tensor_tensor(out=f3, in0=f3, in1=ss5[:, :].unsqueeze(2).to_broadcast([128, 4, 128]), op=ALU.mult), [fk, "ss5"], [fk])
            op("dve", lambda e: e.tensor_tensor(out=f3, in0=f3, in1=gn[:, :].unsqueeze(1).to_broadcast([128, 4, 128]), op=ALU.mult), [fk, "gn"], [fk])
            op("dve", lambda e: e.tensor_tensor(out=o_[:, 512:1024], in0=f_[:], in1=z_[:], op=ALU.mult), [fk, zk], [ok_ + "b"])
            ot, otk = oT.next()
            pT, ptk2 = W.pT.next()
            for kc in range(8):
                op("pe", lambda e: e.transpose(out=pT[:, kc, :], in_=o_[:, kc * 128:(kc + 1) * 128], identity=idb[:]), [ok_ + "a", ok_ + "b", "idb"], [ptk2])
            op("act", lambda e: e.copy(out=ot[:], in_=pT[:]), [ptk2], [otk])
            xt, xk = load_x(W, src, t)
            out_proj(W, ot, otk, wout, "woutab", xt, xk, mods["gt"][r], f"mod_gt{r}", mods, dst, t)
        S.barrier()
        es.close()

    return phase


import numpy as np

D = 1024
SEQ = 4096
CTX = 256
NTOK = SEQ + CTX
NT = NTOK // 128
DFF = 2816
NJ = DFF // 128
DEPTH = 4
ALPHA = (2 * DEPTH) ** 0.25
EPS = 1e-6
AB_IN = 2832


class RB:
    def __init__(self, items, name):
        self.items = items
        self.name = name
        self.i = 0

    def next(self):
        k = self.i
        self.i = (k + 1) % len(self.items)
        return self.items[k], f"{self.name}{k}"


def build(nc, S, io, layers=(0, 1, 2, 3), out_mode="final", skip_ctx_last=True):
    from contextlib import ExitStack
    es0 = ExitStack()

    uid = [0]

    def mk(es):
        def sb(name, shape, dt):
            uid[0] += 1
            return es.enter_context(nc.sbuf_tensor(f"{name}_u{uid[0]}", shape, dt))

        def ps(name, shape, dt):
            uid[0] += 1
            return es.enter_context(nc.psum_tensor(f"{name}_u{uid[0]}", shape, dt))
        return sb, ps

    sb0, ps0 = mk(es0)
    op, dma = S.op, S.dma
    XS = io["XS"]
    MOD = io["MOD"]

    idf = sb0("idf", [128, 128], F32)
    idb = sb0("idb", [128, 128], BF16)
    epsb = sb0("epsb", [128, 1], F32)
    dma("sp", idf[:], io["ident"][:, :], writes=["idf"])
    op("dve", lambda e: e.tensor_copy(out=idb[:], in_=idf[:]), ["idf"], ["idb"])
    op("dve", lambda e: e.memset(epsb[:], EPS), [], ["epsb"])

    def xrows(ap, t):
        return ap[t * 128:(t + 1) * 128, :]

    def phase_mod():
        es = ExitStack()
        sb, ps = mk(es)
        scT = sb("scT", [128, 8, 2], F32)
        aw = [sb(f"aw{i}", [128, 8, 512], F32) for i in range(3)]
        awr = RB(aw, "aw")
        mrow = sb("mrow", [2, 9216], F32)
        brow = sb("brow", [2, 9216], F32)
        pm = [ps(f"pm{i}", [128, 512], F32)[0:2, :] for i in range(2)]
        pmr = RB(pm, "pm")
        dma("sp", scT[:], io["cvecT"][:, :, :], writes=["scT"])
        op("act", lambda e: e.activation(out=scT[:], in_=scT[:], func=AF.Silu), ["scT"], ["scT"])
        for l in layers:
            dma("sp", brow[:], io["ada_b"][l:l + 1, :].broadcast_to([2, 9216]), writes=["brow"])
            wv = io["ada_w"][l].rearrange("(kc p) n -> p kc n", p=128)
            for n in range(18):
                a, ak = awr.next()
                dma("sp" if n % 2 == 0 else "act", a[:], wv[:, :, n * 512:(n + 1) * 512], writes=[ak])
                p, pk = pmr.next()
                for kc in range(8):
                    op("pe", lambda e: e.matmul(p[:], lhsT=scT[:, kc, :], rhs=a[:, kc, :], start=(kc == 0), stop=(kc == 7)),
                       ["scT", ak], [pk])
                op("dve", lambda e: e.tensor_tensor(out=mrow[:, n * 512:(n + 1) * 512], in0=p[:], in1=brow[:, n * 512:(n + 1) * 512], op=ALU.add),
                   [pk, "brow"], ["mrow"])
            for s in range(3):
                c0 = (3 * s + 1) * 1024
                op("dve", lambda e: e.tensor_scalar_add(out=mrow[:, c0:c0 + 1024], in0=mrow[:, c0:c0 + 1024], scalar1=1.0), ["mrow"], ["mrow"])
                if s != 1:
                    c1 = (3 * s + 2) * 1024
                    op("dve", lambda e: e.tensor_scalar_mul(out=mrow[:, c1:c1 + 1024], in0=mrow[:, c1:c1 + 1024], scalar1=0.5), ["mrow"], ["mrow"])
            dma("sp", MOD[l, :, :], mrow[:], reads=["mrow"], writes=["MOD"])
        S.barrier()
        es.close()

    def load_mods(sb, l, s, which):
        out = {}
        for nm in which:
            if nm in ("sh", "sc", "gt"):
                k = {"sh": 0, "sc": 1, "gt": 2}[nm]
                tl = []
                for r in range(2):
                    t = sb(f"mod_{nm}{r}", [128, 1024], F32)
                    dma("sp", t[:], MOD[l, r:r + 1, (3 * s + k) * 1024:(3 * s + k + 1) * 1024].broadcast_to([128, 1024]),
                        reads=["MOD"], writes=[f"mod_{nm}{r}"])
                    tl.append(t)
                out[nm] = tl
            else:
                src = io["ln_g"] if nm == "lng" else io["ln_b"]
                t = sb(f"mod_{nm}", [128, 1024], F32)
                dma("sp", t[:], src[l, s:s + 1, :].broadcast_to([128, 1024]), writes=[f"mod_{nm}"])
                out[nm] = t
        return out

    class Work:
        def __init__(self, sb, ps, nxt=6, npy=2, with_pT=True, ntmp=2, nhb=2, nz=2):
            self.xt = RB([sb(f"xt{i}", [128, 1024], F32) for i in range(nxt)], "xt")
            self.tmp = RB([sb(f"mtmp{i}", [128, 1024], F32) for i in range(ntmp)], "mtmp")
            self.hb = RB([sb(f"hb{i}", [128, 1024], BF16) for i in range(nhb)], "hb")
            self.z = RB([sb(f"z{i}", [128, 1024], F32) for i in range(nz)], "z")
            self.st6 = sb("st6", [128, 2, 6], F32)
            self.mv = sb("mv", [128, 2], F32)
            self.rstd = sb("rstd", [128, 1], F32)
            self.nb = sb("nb", [128, 1], F32)
            if with_pT:
                self.pT = RB([ps(f"pT{i}", [128, 8, 128], BF16) for i in range(2)], "pT")
            self.py = RB([ps(f"py{i}", [128, 512], F32) for i in range(npy)], "py")

    def load_x(W, src, t):
        xt, xk = W.xt.next()
        dma("sp", xt[:], xrows(src, t), reads=[("X", t)], writes=[xk])
        return xt, xk

    def modulate(W, xt, xk, mods, r):
        tmp, tk = W.tmp.next()
        hb, hk = W.hb.next()
        op("pool", lambda e: e.tensor_tensor(out=tmp[:], in0=xt[:], in1=mods["sc"][r][:], op=ALU.mult), [xk, f"mod_sc{r}"], [tk])
        op("pool", lambda e: e.tensor_tensor(out=hb[:], in0=tmp[:], in1=mods["sh"][r][:], op=ALU.add), [tk, f"mod_sh{r}"], [hk])
        return hb, hk

    def transpose8(W, src, sk, dst_ap, dk, eng="act"):
        pT, pk = W.pT.next()
        for kc in range(8):
            op("pe", lambda e: e.transpose(out=pT[:, kc, :], in_=src[:, kc * 128:(kc + 1) * 128], identity=idb[:]), [sk, "idb"], [pk])
        if eng == "act":
            op("act", lambda e: e.copy(out=dst_ap, in_=pT[:]), [pk], [dk])
        else:
            op("dve", lambda e: e.tensor_copy(out=dst_ap, in_=pT[:]), [pk], [dk])

    def cast_wgu(l, j, slot):
        wv = io["ffn_w_gu"][l, j].rearrange("(kc p) n -> p kc n", p=128)
        dst = io["WGUB"][slot]
        for g in range(11):
            for u in range(2):
                dma("pool", dst[g, :, :, u * 256:(u + 1) * 256], wv[:, :, u * DFF + g * 256:u * DFF + (g + 1) * 256],
                    writes=[("WGUB", slot, g)])

    def phase_ffn(l, j, src, dst, tiles, slot, final=False, prefetch=None):
        s = 0 if j == 0 else 2
        es = ExitStack()
        sb, ps = mk(es)
        wd = sb("wd", [128, NJ, 1024], BF16)
        wdv = io["ffn_w_down"][l, j].rearrange("(jc p) n -> p jc n", p=128)
        for q in range(2):
            dma("pool", wd[:, q * 11:(q + 1) * 11, :], wdv[:, q * 11:(q + 1) * 11, :], writes=[f"wd{q}"])
        if prefetch is not None:
            prefetch()
        mods = load_mods(sb, l, s, ("sh", "sc", "gt", "lng", "lnb"))
        W = Work(sb, ps, nxt=8, npy=2)
        wg = RB([sb(f"wg{i}", [128, 8, 512], BF16) for i in range(3)], "wg")
        hT = RB([sb(f"hT{i}", [128, 8, 512], BF16) for i in range(2)], "hT")
        aT = sb("aT", [128, NJ, 512], BF16)
        sg = RB([sb(f"sg{i}", [128, 512], F32) for i in range(2)], "sg")
        pg = RB([ps(f"pg{i}", [128, 512], F32) for i in range(2)], "pg")
        pu = RB([ps(f"pu{i}", [128, 512], F32) for i in range(2)], "pu")
        sts = [tiles[i:i + 4] for i in range(0, len(tiles), 4)]

        def prologue(st):
            h, hk = hT.next()
            xs = []
            for ti, t in enumerate(st):
                xt, xk = load_x(W, src, t)
                xs.append((xt, xk))
                r = 1 if t < 2 else 0
                hb, hbk = modulate(W, xt, xk, mods, r)
                transpose8(W, hb, hbk, h[:, :, ti * 128:(ti + 1) * 128], hk)
            return h, hk, xs

        nxt = prologue(sts[0])
        for si, st in enumerate(sts):
            n = 128 * len(st)
            h, hk, xs = nxt
            for g in range(11):
                w, wk = wg.next()
                dma("sp", w[:], io["WGUB"][slot][g], reads=[("WGUB", slot, g)], writes=[wk])
                for c in range(2):
                    jj = 2 * g + c
                    p1, p1k = pg.next()
                    p2, p2k = pu.next()
                    for kc in range(8):
                        op("pe", lambda e: e.matmul(p1[:, :n], lhsT=w[:, kc, c * 128:(c + 1) * 128], rhs=h[:, kc, :n], start=(kc == 0), stop=(kc == 7)),
                           [wk, hk], [p1k])
                    for kc in range(8):
                        op("pe", lambda e: e.matmul(p2[:, :n], lhsT=w[:, kc, 256 + c * 128:256 + (c + 1) * 128], rhs=h[:, kc, :n], start=(kc == 0), stop=(kc == 7)),
                           [wk, hk], [p2k])
                    s1, s1k = sg.next()
                    op("act", lambda e: e.activation(out=s1[:, :n], in_=p1[:, :n], func=AF.Silu), [p1k], [s1k])
                    op("dve", lambda e: e.tensor_tensor(out=aT[:, jj, :n], in0=p2[:, :n], in1=s1[:, :n], op=ALU.mult), [p2k, s1k], [("aT", jj)])
            if si + 1 < len(sts):
                nxt = prologue(sts[si + 1])
            for ti, t in enumerate(st):
                xt, xk = xs[ti]
                r = 1 if t < 2 else 0
                z, zk = W.z.next()
                for nh in range(2):
                    py, pk = W.py.next()
                    for k in range(NJ):
                        op("pe", lambda e: e.matmul(py[:], lhsT=aT[:, k, ti * 128:(ti + 1) * 128], rhs=wd[:, k, nh * 512:(nh + 1) * 512], start=(k == 0), stop=(k == NJ - 1)),
                           [("aT", k), f"wd{k // 11}"], [pk])
                    op("dve", lambda e: e.tensor_tensor(out=z[:, nh * 512:(nh + 1) * 512], in0=py[:], in1=mods["gt"][r][:, nh * 512:(nh + 1) * 512], op=ALU.mult),
                       [pk, f"mod_gt{r}"], [zk])
                ln_store(W, z, zk, xt, xk, mods, dst, t, final)
        S.barrier()
        es.close()

    def ln_store(W, z, zk, xt, xk, mods, dst, t, final):
        op("dve", lambda e: e.scalar_tensor_tensor(out=z[:], in0=xt[:], scalar=float(ALPHA), in1=z[:], op0=ALU.mult, op1=ALU.add), [xk, zk], [zk])
        for c in range(2):
            op("dve", lambda e: e.bn_stats(out=W.st6[:, c, :], in_=z[:, c * 512:(c + 1) * 512]), [zk], ["st6"])
        op("dve", lambda e: e.bn_aggr(out=W.mv[:], in_=W.st6[:]), ["st6"], ["mv"])
        op("act", lambda e: e.activation(out=W.rstd[:], in_=W.mv[:, 1:2], func=AF.Sqrt, bias=epsb[:, 0:1], scale=1.0), ["mv", "epsb"], ["rstd"])
        op("dve", lambda e: e.reciprocal(out=W.rstd[:], in_=W.rstd[:]), ["rstd"], ["rstd"])
        op("dve", lambda e: e.scalar_tensor_tensor(out=W.nb[:], in0=W.mv[:, 0:1], scalar=-1.0, in1=W.rstd[:], op0=ALU.mult, op1=ALU.mult), ["mv", "rstd"], ["nb"])
        op("act", lambda e: e.activation(out=z[:], in_=z[:], func=AF.Identity, bias=W.nb[:, 0:1], scale=W.rstd[:, 0:1]), [zk, "nb", "rstd"], [zk])
        op("pool", lambda e: e.tensor_tensor(out=z[:], in0=z[:], in1=mods["lng"][:], op=ALU.mult), [zk, "mod_lng"], [zk])
        op("pool", lambda e: e.tensor_tensor(out=z[:], in0=z[:], in1=mods["lnb"][:], op=ALU.add), [zk, "mod_lnb"], [zk])
        if final:
            dma("sp", io["out"][(t - 2) * 128:(t - 1) * 128, :], z[:], reads=[zk], writes=[("OUT", t)])
        else:
            dma("sp", xrows(dst, t), z[:], reads=[zk], writes=[("X", t)])

    def out_proj(W, oT, ok, wt, wk, xt, xk, gate, gk, mods, dst, t):
        z, zk = W.z.next()
        for nh in range(2):
            py, pk = W.py.next()
            for k in range(8):
                op("pe", lambda e: e.matmul(py[:], lhsT=oT[:, k, :], rhs=wt[:, k, nh * 512:(nh + 1) * 512], start=(k == 0), stop=(k == 7)),
                   [ok, wk], [pk])
            op("dve", lambda e: e.tensor_tensor(out=z[:, nh * 512:(nh + 1) * 512], in0=py[:], in1=gate[:, nh * 512:(nh + 1) * 512], op=ALU.mult),
               [pk, gk], [zk])
        ln_store(W, z, zk, xt, xk, mods, dst, t, False)

    def phase_mixer_c(l, src, dst, q_tiles):
        i = l // 2
        SC = 128 ** -0.5
        eso = ExitStack()
        sbo, pso = mk(eso)
        QT = sbo("QT", [128, NT, 8, 128], BF16)
        KT = sbo("KT", [128, 2, NTOK], BF16)
        V = sbo("V", [128, NT, 2, 130], BF16)
        op("pool", lambda e: e.memset(V[:, :, :, 128:130], 1.0), [], ["Vones"])
        es = ExitStack()
        sb, ps = mk(es)
        win = sb("win", [128, 8, 1536], BF16)
        dma("pool", win[:], io["c_w_in"][i].rearrange("(kc p) n -> p kc n", p=128), writes=["win"])
        gq = sb("gq", [128, 10, 128], F32)
        for h in range(10):
            srcg = io["c_q_norm"] if h < 8 else io["c_k_norm"]
            dma("sp", gq[:, h, :], srcg[i:i + 1, :].broadcast_to([128, 128]), writes=["gq"])
        mods = load_mods(sb, l, 1, ("sh", "sc"))
        W = Work(sb, ps, nxt=2, npy=1, ntmp=1, nhb=2, nz=0)
        hT = RB([sb(f"hTc{k}", [128, 8, 128], BF16) for k in range(2)], "hTc")
        qkv = RB([sb(f"qkv{k}", [128, 1536], F32) for k in range(2)], "qkv")
        sq = sb("sq", [128, 1280], F32)
        ss = sb("ss", [128, 10], F32)
        cs = RB([sb(f"cs{k}", [128, 2, 64], F32) for k in range(2)], "cs")
        ra = sb("ra", [128, 10, 64], F32)
        rb_ = sb("rb", [128, 10, 64], F32)
        qr = RB([sb(f"qr{k}", [128, 10, 128], BF16) for k in range(2)], "qr")
        pq = [ps(f"pq{k}", [128, 512], F32) for k in range(3)]
        ptr = ps("ptr", [128, 16, 128], BF16)
        for t in range(NT):
            r = 1 if t < 2 else 0
            xt, xk = load_x(W, src, t)
            hb, hbk = modulate(W, xt, xk, mods, r)
            h, hk = hT.next()
            transpose8(W, hb, hbk, h[:], hk)
            qv, qk = qkv.next()
            for n in range(3):
                for kc in range(8):
                    op("pe", lambda e: e.matmul(pq[n][:], lhsT=h[:, kc, :], rhs=win[:, kc, n * 512:(n + 1) * 512], start=(kc == 0), stop=(kc == 7)),
                       [hk, "win"], [f"pq{n}"])
                op("act", lambda e: e.copy(out=qv[:, n * 512:(n + 1) * 512], in_=pq[n][:]), [f"pq{n}"], [qk])
            op("act", lambda e: e.activation(out=sq[:], in_=qv[:, 0:1280], func=AF.Square), [qk], ["sq"])
            op("dve", lambda e: e.tensor_reduce(out=ss[:], in_=sq[:].rearrange("p (h d) -> p h d", d=128), axis=AX.X, op=ALU.add), ["sq"], ["ss"])
            op("act", lambda e: e.activation(out=ss[:], in_=ss[:], func=AF.Sqrt, bias=epsb[:, 0:1], scale=1.0 / 128), ["ss", "epsb"], ["ss"])
            op("dve", lambda e: e.reciprocal(out=ss[:], in_=ss[:]), ["ss"], ["ss"])
            q3 = qv[:, 0:1280].rearrange("p (h d) -> p h d", d=128)
            op("dve", lambda e: e.tensor_tensor(out=q3, in0=q3, in1=ss[:, :].unsqueeze(2).to_broadcast([128, 10, 128]), op=ALU.mult), [qk, "ss"], [qk])
            qo, qok = qr.next()
            if r == 1:
                op("dve", lambda e: e.tensor_tensor(out=qo[:], in0=q3, in1=gq[:], op=ALU.mult), [qk, "gq"], [qok])
            else:
                op("dve", lambda e: e.tensor_tensor(out=q3, in0=q3, in1=gq[:], op=ALU.mult), [qk, "gq"], [qk])
                c, ck = cs.next()
                p0 = (t - 2) * 128
                dma("sp", c[:, 0, :], io["cosC"][p0:p0 + 128, :], writes=[ck])
                dma("sp", c[:, 1, :], io["sinC"][p0:p0 + 128, :], writes=[ck])
                q4 = qv[:, 0:1280].rearrange("p (h two d) -> p h two d", two=2, d=64)
                o4 = qo[:].rearrange("p h (two d) -> p h two d", two=2)
                cosb = c[:, 0, :].unsqueeze(1).to_broadcast([128, 10, 64])
                sinb = c[:, 1, :].unsqueeze(1).to_broadcast([128, 10, 64])
                x1, x2 = q4[:, :, 0, :], q4[:, :, 1, :]
                op("dve", lambda e: e.tensor_tensor(out=ra[:], in0=x1, in1=cosb, op=ALU.mult), [qk, ck], ["ra"])
                op("dve", lambda e: e.tensor_tensor(out=rb_[:], in0=x2, in1=sinb, op=ALU.mult), [qk, ck], ["rb"])
                op("dve", lambda e: e.tensor_tensor(out=o4[:, :, 0, :], in0=ra[:], in1=rb_[:], op=ALU.subtract), ["ra", "rb"], [qok])
                op("dve", lambda e: e.tensor_tensor(out=ra[:], in0=x1, in1=sinb, op=ALU.mult), [qk, ck], ["ra"])
                op("dve", lambda e: e.tensor_tensor(out=rb_[:], in0=x2, in1=cosb, op=ALU.mult), [qk, ck], ["rb"])
                op("dve", lambda e: e.tensor_tensor(out=o4[:, :, 1, :], in0=ra[:], in1=rb_[:], op=ALU.add), ["ra", "rb"], [qok])
            for hh in range(10):
                op("pe", lambda e: e.transpose(out=ptr[:, hh, :], in_=qo[:, hh, :], identity=idb[:]), [qok, "idb"], ["ptr"])
            op("act", lambda e: e.copy(out=QT[:, t, :, :], in_=ptr[:, 0:8, :]), ["ptr"], [("QT", t)])
            op("dve", lambda e: e.tensor_copy(out=KT[:, :, t * 128:(t + 1) * 128], in_=ptr[:, 8:10, :]), ["ptr"], [("KT", t)])
            op("pool", lambda e: e.tensor_copy(out=V[:, t, :, 0:128], in_=qv[:, 1280:1536].rearrange("p (g d) -> p g d", g=2)), [qk], [("V", t)])
        S.barrier()
        es.close()
        es = ExitStack()
        sb, ps = mk(es)
        wout = sb("wout", [128, 8, 1024], BF16)
        dma("pool", wout[:], io["c_w_out"][i].rearrange("(kc p) n -> p kc n", p=128), writes=["wout"])
        mods = load_mods(sb, l, 1, ("gt", "lng", "lnb"))
        W = Work(sb, ps, nxt=3, npy=1, with_pT=False, ntmp=0, nhb=0, nz=2)
        PT = RB([sb(f"PT{k}", [128, 512], BF16) for k in range(3)], "PT")
        oT = RB([sb(f"oT{k}", [128, 8, 128], BF16) for k in range(2)], "oT")
        accs = RB([sb(f"accC{k}", [128, 512], F32) for k in range(2)], "accC")
        rdn = sb("rdn", [128, 512], F32)
        onesf = sb("onesf", [128, 128], F32)
        op("dve", lambda e: e.memset(onesf[:], 1.0), [], ["onesf"])
        pS = RB([ps(f"pS{k}", [128, 512], F32) for k in range(2)], "pS")
        pOT = [ps(f"pOT{k}", [128, 512], F32) for k in range(2)]
        pden = ps("pden", [128, 512], F32)
        for t in q_tiles:
            r = 1 if t < 2 else 0
            kts = [0, 1] if r == 1 else list(range(NT))
            ot, otk = oT.next()
            for g in range(2):
                a_, ak_ = accs.next()
                def issue_s(kt_):
                    p_, pk_ = pS.next()
                    op("pe", lambda e: e.matmul(p_[:], lhsT=KT[:, g, kt_ * 128:(kt_ + 1) * 128], rhs=QT[:, t, 4 * g:4 * g + 4, :], start=True, stop=True),
                       [("KT", kt_), ("QT", t)], [pk_])
                    return p_, pk_
                nxt_s = issue_s(kts[0])
                for ki, kt in enumerate(kts):
                    p, pk = nxt_s
                    if ki + 1 < len(kts):
                        nxt_s = issue_s(kts[ki + 1])
                    pt, ptk = PT.next()
                    op("act", lambda e: e.activation(out=pt[:], in_=p[:], func=AF.Exp, scale=float(SC)), [pk], [ptk])
                    op("pe", lambda e: e.matmul(pOT[g][:], lhsT=V[:, kt, g, 0:128], rhs=pt[:], start=(ki == 0), stop=(ki == len(kts) - 1)),
                       [ptk, ("V", kt)], [f"pOT{g}"])
                    if ki == 0:
                        op("dve", lambda e: e.tensor_copy(out=a_[:], in_=pt[:]), [ptk], [ak_])
                    else:
                        op("dve", lambda e: e.tensor_tensor(out=a_[:], in0=a_[:], in1=pt[:], op=ALU.add), [ptk, ak_], [ak_])
                op("pe", lambda e: e.matmul(pden[:], lhsT=onesf[:], rhs=a_[:], start=True, stop=True), ["onesf", ak_], ["pden"])
                op("dve", lambda e: e.reciprocal(out=rdn[:], in_=pden[:]), ["pden"], ["rdn"])
                op("dve", lambda e: e.tensor_tensor(out=ot[:, 4 * g:4 * g + 4, :], in0=pOT[g][:].rearrange("p (h q) -> p h q", h=4),
                                                    in1=rdn[:].rearrange("p (h q) -> p h q", h=4), op=ALU.mult), [f"pOT{g}", "rdn"], [otk])
            xt, xk = load_x(W, src, t)
            out_proj(W, ot, otk, wout, "wout", xt, xk, mods["gt"][r], f"mod_gt{r}", mods, dst, t)
        S.barrier()
        es.close()
        eso.close()

    io["mixer_ab"] = make_mixer_ab(nc, S, io, mk, idf, idb, epsb, (Work, load_x, modulate, transpose8, load_mods, out_proj))
    phase_mod()
    all_tiles = list(range(NT))
    lat_tiles = list(range(2, NT))
    ffn_list = [(l, j) for l in layers for j in range(2)]
    cast_wgu(ffn_list[0][0], ffn_list[0][1], 0)
    fi = 0
    cur = io["xin"]
    for li, l in enumerate(layers):
        last = (li == len(layers) - 1)
        pf = None
        if fi + 1 < len(ffn_list):
            pf = (lambda a=ffn_list[fi + 1][0], b=ffn_list[fi + 1][1], c=(fi + 1) % 2: cast_wgu(a, b, c))
        phase_ffn(l, 0, cur, XS, all_tiles, fi % 2, prefetch=pf)
        fi += 1
        cur = XS
        qt = lat_tiles if (last and skip_ctx_last) else all_tiles
        if l % 2 == 1:
            phase_mixer_c(l, XS, XS, qt)
        else:
            io["mixer_ab"](l, XS, XS, qt)
        pf = None
        if fi + 1 < len(ffn_list):
            pf = (lambda a=ffn_list[fi + 1][0], b=ffn_list[fi + 1][1], c=(fi + 1) % 2: cast_wgu(a, b, c))
        phase_ffn(l, 1, XS, XS, qt, fi % 2, final=last, prefetch=pf)
        fi += 1
    S.barrier()
    es0.close()


import numpy as np


def _consts():
    c = {}
    c["ident"] = np.eye(128, dtype=np.float32)

    def rope(hd):
        nf = hd // 4
        inv = (10000.0 ** (-np.arange(nf, dtype=np.float32) / nf)).astype(np.float32)
        r, col = np.meshgrid(np.arange(64, dtype=np.float32), np.arange(64, dtype=np.float32), indexing="ij")
        r, col = r.reshape(-1), col.reshape(-1)
        ang = np.concatenate([r[:, None] * inv, col[:, None] * inv], axis=-1).astype(np.float32)
        return np.cos(ang).astype(np.float32), np.sin(ang).astype(np.float32)
    c["cosC"], c["sinC"] = rope(128)
    c["cosA"], c["sinA"] = rope(64)
    j = np.arange(128)[:, None]
    i = np.arange(128)[None, :]
    c["cA"] = np.stack([(j >= i), (j <= i)], axis=1).astype(np.float32)
    j = np.arange(64)[:, None]
    i = np.arange(64)[None, :]
    c["cB"] = np.stack([(j <= i), (j >= i), -1.0 * (i > j), -1.0 * (i < j)], axis=1).astype(np.float32)
    return c


W_NAMES = ["ada_w", "ada_b", "ln_g", "ln_b", "ffn_w_gu", "ffn_w_down", "ab_w_in", "ab_conv_w", "ab_a_log",
           "ab_dt_bias", "ab_gnorm", "ab_sink", "ab_w_out", "c_w_in", "c_q_norm", "c_k_norm", "c_w_out"]


def make_program(shapes, layers=(0, 1, 2, 3), dbg=False):
    nc = bass.Bass("TRN2", target_bir_lowering=False)
    try:
        nc.allow_low_precision("bf16 matmul operands, fp32 accumulation")
    except Exception:
        pass
    io = {}
    io["xin"] = nc.dram_tensor("xin", [NTOK, D], F32, kind="ExternalInput").ap()
    io["cvecT"] = nc.dram_tensor("cvecT", [128, 8, 2], F32, kind="ExternalInput").ap()
    for k in W_NAMES:
        io[k] = nc.dram_tensor(k, list(shapes[k]), F32, kind="ExternalInput").ap()
    for k, v in _consts().items():
        io[k] = nc.dram_tensor(k, list(v.shape), F32, kind="ExternalInput").ap()
    io["out"] = nc.dram_tensor("out", [SEQ, D], F32, kind="ExternalOutput").ap()
    io["XS"] = nc.dram_tensor("XS", [NTOK, D], F32, kind="ExternalOutput" if dbg else "Internal").ap()
    io["MOD"] = nc.dram_tensor("MOD", [4, 2, 9216], F32, kind="ExternalOutput" if dbg else "Internal").ap()
    io["AQT"] = nc.dram_tensor("AQT", [NT, 64, 8, 128], BF16, kind="Internal").ap()
    io["BZ"] = nc.dram_tensor("BZ", [NTOK, 512], F32, kind="Internal").ap()
    io["GB"] = nc.dram_tensor("GB", [NTOK, 16], F32, kind="Internal").ap()
    io["BQT"] = nc.dram_tensor("BQT", [4, 128, NTOK], F32, kind="Internal").ap()
    io["BKT"] = nc.dram_tensor("BKT", [4, 128, NTOK], F32, kind="Internal").ap()
    io["BKV"] = nc.dram_tensor("BKV", [NTOK, 2, 512], F32, kind="Internal").ap()
    io["OA"] = nc.dram_tensor("OA", [NTOK, 512], BF16, kind="ExternalOutput" if dbg else "Internal").ap()
    io["OB"] = nc.dram_tensor("OB", [2, NTOK, 512], F32, kind="ExternalOutput" if dbg else "Internal").ap()
    io["WGUB"] = [nc.dram_tensor(f"WGUB{i}", [11, 128, 8, 512], BF16, kind="Internal").ap() for i in range(2)]
    S = Sched(nc)
    build(nc, S, io, layers=layers)
    return nc, S


def make_in_maps(inputs):
    x = np.asarray(inputs["x"], dtype=np.float32)
    c = np.asarray(inputs["c"], dtype=np.float32)
    ctx = np.asarray(inputs["ctx"], dtype=np.float32)
    c_ctx = np.asarray(inputs["c_ctx"], dtype=np.float32)
    consts = _consts()
    shared = {k: np.ascontiguousarray(np.asarray(inputs[k], dtype=np.float32)) for k in W_NAMES}
    shared.update(consts)
    maps = []
    for b in range(8):
        m = dict(shared)
        m["xin"] = np.ascontiguousarray(np.concatenate([ctx[b], x[b]], axis=0))
        cv = np.stack([c[b], c_ctx], axis=0)
        m["cvecT"] = np.ascontiguousarray(cv.reshape(2, 8, 128).transpose(2, 1, 0))
        maps.append(m)
    return maps


def kernel(**inputs):
    shapes = {k: np.asarray(inputs[k]).shape for k in W_NAMES}
    nc, S = make_program(shapes)
    maps = make_in_maps(inputs)
    res = run_bass_kernel_spmd(nc, maps, core_ids=list(range(8)))
    return np.stack([np.asarray(r["out"], dtype=np.float32) for r in res.results], axis=0)
```
